# Optimizing a Trainium2 kernel written in Bass

```python
import math
import jax, jax.numpy as jnp
from jax import lax
import numpy as np


D_MODEL = 1024
BATCH = 1
SEQ = 16384
DEPTH = 1
DEC_BATCH = 16
DEC_SEQ = 32
PAST_LEN = 4096

CHUNK = 64
Q_BLOCK = 128
N_MEM = 256
EPS = 1e-6
NEG_INF = -1e30
MLA_HEADS = 8
MLA_Q_RANK = 384
MLA_KV_RANK = 256
MLA_NOPE = 64
MLA_ROPE = 32
MLA_V = 64
MLA_THETA = 10000.0
MLA_SCALE = (MLA_NOPE + MLA_ROPE) ** -0.5
DIFF_HEADS = 8
DIFF_DC = 32
DIFF_V = 2 * DIFF_DC
DIFF_ROT = DIFF_DC // 4
ROPE_THETA = 500000.0
DIFF_SCALE = DIFF_DC ** -0.5
MEM_HEADS = 4
MEM_DH = 128
MEM_SCALE = MEM_DH ** -0.5
D_FF = 4 * D_MODEL
N_BRANCH = 3
DIFF_QK_W = DIFF_HEADS * 2 * DIFF_DC
DIFF_V_W = DIFF_HEADS * DIFF_V
MEM_W = MEM_HEADS * MEM_DH
MLA_O_W = MLA_HEADS * MLA_V
IN_SIZES = (MLA_Q_RANK, MLA_KV_RANK, MLA_ROPE, DIFF_QK_W, DIFF_QK_W, DIFF_V_W, MEM_W)
IN_SPLIT_POINTS = tuple(int(v) for v in np.cumsum(IN_SIZES)[:-1])
IN_WIDTH = int(sum(IN_SIZES))

kernel_name = "hybrid_mla_diffattn_memory_stream_step"


def rmsnorm(x, g):
    xf = x.astype(jnp.float32)
    y = xf * lax.rsqrt(jnp.mean(xf * xf, axis=-1, keepdims=True) + EPS)
    return (y * g.astype(jnp.float32)).astype(x.dtype)


def rope(x, pos, rot_dim, theta):
    half = rot_dim // 2
    inv = jnp.power(jnp.float32(theta), -jnp.arange(half, dtype=jnp.float32) * (2.0 / rot_dim))
    ang = pos.astype(jnp.float32)[:, None] * inv[None, :]
    shape = (1, pos.shape[0]) + (1,) * (x.ndim - 3) + (half,)
    c = jnp.cos(ang).reshape(shape)
    s = jnp.sin(ang).reshape(shape)
    xf = x.astype(jnp.float32)
    x1 = xf[..., :half]
    x2 = xf[..., half:rot_dim]
    out = jnp.concatenate([x1 * c - x2 * s, x2 * c + x1 * s, xf[..., rot_dim:]], axis=-1)
    return out.astype(x.dtype)


def sweep_queries(fn, q_args, q_pos):
    t = q_pos.shape[0]
    if t <= Q_BLOCK or t % Q_BLOCK != 0:
        return fn(q_args, q_pos)
    nb = t // Q_BLOCK

    def to_blocks(a):
        return jnp.moveaxis(a.reshape((a.shape[0], nb, Q_BLOCK) + a.shape[2:]), 1, 0)

    def from_blocks(o):
        return jnp.moveaxis(o, 0, 1).reshape((o.shape[1], t) + o.shape[3:])

    blocks = jax.tree_util.tree_map(to_blocks, q_args)
    outs = lax.map(lambda bp: fn(bp[0], bp[1]), (blocks, q_pos.reshape(nb, Q_BLOCK)))
    return jax.tree_util.tree_map(from_blocks, outs)


def memory_kv(mem, norm_g, w_k, w_v):
    b, m, _ = mem.shape
    mn = rmsnorm(mem, norm_g)
    k = jnp.einsum('bmd,de->bme', mn, w_k).reshape(b, m, MEM_HEADS, MEM_DH)
    v = jnp.einsum('bmd,de->bme', mn, w_v).reshape(b, m, MEM_HEADS, MEM_DH)
    return k, v


def layer_forward(x, pos, past, mem_k, mem_v, p, lam_init):
    b, t, _ = x.shape
    f32 = jnp.float32
    xn = rmsnorm(x, p['pre_mix_g'])
    proj = jnp.einsum('btd,de->bte', xn, p['w_in'])
    cq, ckv, kr, dq, dk, dv, mq = jnp.split(proj, IN_SPLIT_POINTS, axis=-1)

    q = jnp.einsum('btr,re->bte', rmsnorm(cq, p['mla_q_norm_g']), p['mla_w_uq'])
    q = q.reshape(b, t, MLA_HEADS, MLA_NOPE + MLA_ROPE)
    q_nope = q[..., :MLA_NOPE]
    q_rope = rope(q[..., MLA_NOPE:], pos, MLA_ROPE, MLA_THETA)
    ckv_new = rmsnorm(ckv, p['mla_kv_norm_g'])
    kr_new = rope(kr[:, :, None, :], pos, MLA_ROPE, MLA_THETA)[:, :, 0, :]

    dq = rope(dq.reshape(b, t, DIFF_HEADS, 2, DIFF_DC), pos, DIFF_ROT, ROPE_THETA)
    dk_new = rope(dk.reshape(b, t, DIFF_HEADS, 2, DIFF_DC), pos, DIFF_ROT, ROPE_THETA)
    dk_new = dk_new.reshape(b, t, DIFF_HEADS, DIFF_V)
    dv_new = dv.reshape(b, t, DIFF_HEADS, DIFF_V)

    mq = mq.reshape(b, t, MEM_HEADS, MEM_DH)

    if past is None:
        ckv_all, kr_all, dk_all, dv_all = ckv_new, kr_new, dk_new, dv_new
    else:
        ckv_all = jnp.concatenate([past[0], ckv_new], axis=1)
        kr_all = jnp.concatenate([past[1], kr_new], axis=1)
        dk_all = jnp.concatenate([past[2], dk_new], axis=1)
        dv_all = jnp.concatenate([past[3], dv_new], axis=1)
    s_len = ckv_all.shape[1]
    k_chunk = jnp.arange(s_len, dtype=jnp.int32) // CHUNK
    k_nope = jnp.einsum('bsr,rhn->bshn', ckv_all, p['mla_w_uk'])
    v_mla = jnp.einsum('bsr,rhv->bshv', ckv_all, p['mla_w_uv'])
    dk_all = dk_all.reshape(b, s_len, DIFF_HEADS, 2, DIFF_DC)
    lam = (jnp.exp(jnp.sum(p['diff_lq1'].astype(f32) * p['diff_lk1'].astype(f32)))
           - jnp.exp(jnp.sum(p['diff_lq2'].astype(f32) * p['diff_lk2'].astype(f32)))
           + lam_init)

    def block(qa, q_pos):
        qn, qr, qd, qm = qa
        mask = k_chunk[None, :] <= (q_pos // CHUNK)[:, None]
        s = (jnp.einsum('bthn,bshn->bhts', qn, k_nope)
             + jnp.einsum('bthe,bse->bhts', qr, kr_all)).astype(f32) * MLA_SCALE
        pm = jax.nn.softmax(jnp.where(mask, s, NEG_INF), axis=-1).astype(v_mla.dtype)
        o_a = jnp.einsum('bhts,bshv->bthv', pm, v_mla)
        sd = jnp.einsum('bthcd,bshcd->bchts', qd, dk_all).astype(f32) * DIFF_SCALE
        pd = jax.nn.softmax(jnp.where(mask, sd, NEG_INF), axis=-1)
        a = (pd[:, 0] - lam * pd[:, 1]).astype(dv_all.dtype)
        o_b = jnp.einsum('bhts,bshv->bthv', a, dv_all)
        sm = jnp.einsum('bthd,bmhd->bhtm', qm, mem_k).astype(f32) * MEM_SCALE
        pmem = jax.nn.softmax(sm, axis=-1).astype(mem_v.dtype)
        o_c = jnp.einsum('bhtm,bmhd->bthd', pmem, mem_v)
        return (o_a, o_b, o_c)

    o_a, o_b, o_c = sweep_queries(block, (q_nope, q_rope, dq, mq), pos)
    o_mla = o_a.reshape(b, t, MLA_O_W)
    o_diff = (rmsnorm(o_b, p['diff_subln_g']) * (1.0 - lam_init)).reshape(b, t, DIFF_V_W)
    o_mem = o_c.reshape(b, t, MEM_W)

    gates = jax.nn.sigmoid(jnp.einsum('btd,de->bte', xn, p['w_gate']) + p['b_gate'])
    gates = gates.reshape(b, t, N_BRANCH, D_MODEL)
    merged = (gates[:, :, 0] * jnp.einsum('bte,ed->btd', o_mla, p['w_o_mla'])
              + gates[:, :, 1] * jnp.einsum('bte,ed->btd', o_diff, p['w_o_diff'])
              + gates[:, :, 2] * jnp.einsum('bte,ed->btd', o_mem, p['w_o_mem']))
    mix = jnp.einsum('btd,de->bte', merged, p['w_out'])
    x = x + rmsnorm(mix, p['post_mix_g'])

    h = rmsnorm(x, p['pre_mlp_g'])
    u = jax.nn.relu(jnp.einsum('btd,df->btf', h, p['w_mlp_up']))
    f = jnp.einsum('btf,fd->btd', u * u, p['w_mlp_down'])
    x = x + rmsnorm(f, p['post_mlp_g'])
    return x, (ckv_new, kr_new, dk_new, dv_new)


def setup_inputs(seed: int = 0) -> dict:
    key = jax.random.key(seed)
    keys = list(jax.random.split(key, 40))

    def nrm(shape, scale=1.0):
        return scale * jax.random.normal(keys.pop(), shape, dtype=jnp.float32)

    def gain(n):
        return 1.0 + 0.05 * nrm((DEPTH, n))

    d = D_MODEL
    return {
        'x_prompt': nrm((BATCH, SEQ, d)),
        'x_sample': nrm((DEC_BATCH, DEC_SEQ, d)),
        'cache_mla_ckv': nrm((DEPTH, DEC_BATCH, PAST_LEN, MLA_KV_RANK)),
        'cache_mla_krope': nrm((DEPTH, DEC_BATCH, PAST_LEN, MLA_ROPE)),
        'cache_diff_k': nrm((DEPTH, DEC_BATCH, PAST_LEN, DIFF_HEADS, DIFF_V)),
        'cache_diff_v': nrm((DEPTH, DEC_BATCH, PAST_LEN, DIFF_HEADS, DIFF_V)),
        'cache_mem_k': nrm((DEPTH, DEC_BATCH, N_MEM, MEM_HEADS, MEM_DH)),
        'cache_mem_v': nrm((DEPTH, DEC_BATCH, N_MEM, MEM_HEADS, MEM_DH)),
        'mem_prompt': nrm((BATCH, N_MEM, d)),
        'pre_mix_g': gain(d),
        'w_in': nrm((DEPTH, d, IN_WIDTH), d ** -0.5),
        'mla_q_norm_g': gain(MLA_Q_RANK),
        'mla_w_uq': nrm((DEPTH, MLA_Q_RANK, MLA_HEADS * (MLA_NOPE + MLA_ROPE)), MLA_Q_RANK ** -0.5),
        'mla_kv_norm_g': gain(MLA_KV_RANK),
        'mla_w_uk': nrm((DEPTH, MLA_KV_RANK, MLA_HEADS, MLA_NOPE), MLA_KV_RANK ** -0.5),
        'mla_w_uv': nrm((DEPTH, MLA_KV_RANK, MLA_HEADS, MLA_V), MLA_KV_RANK ** -0.5),
        'diff_lq1': nrm((DEPTH, DIFF_DC), 0.1),
        'diff_lk1': nrm((DEPTH, DIFF_DC), 0.1),
        'diff_lq2': nrm((DEPTH, DIFF_DC), 0.1),
        'diff_lk2': nrm((DEPTH, DIFF_DC), 0.1),
        'diff_subln_g': gain(DIFF_V),
        'mem_norm_g': gain(d),
        'w_mem_k': nrm((DEPTH, d, MEM_W), d ** -0.5),
        'w_mem_v': nrm((DEPTH, d, MEM_W), d ** -0.5),
        'w_o_mla': nrm((DEPTH, MLA_O_W, d), MLA_O_W ** -0.5),
        'w_o_diff': nrm((DEPTH, DIFF_V_W, d), DIFF_V_W ** -0.5),
        'w_o_mem': nrm((DEPTH, MEM_W, d), MEM_W ** -0.5),
        'w_gate': nrm((DEPTH, d, N_BRANCH * d), d ** -0.5),
        'b_gate': nrm((DEPTH, N_BRANCH * d), 0.01),
        'w_out': nrm((DEPTH, d, d), d ** -0.5),
        'post_mix_g': gain(d),
        'pre_mlp_g': gain(d),
        'w_mlp_up': nrm((DEPTH, d, D_FF), d ** -0.5),
        'w_mlp_down': nrm((DEPTH, D_FF, d), D_FF ** -0.5),
        'post_mlp_g': gain(d),
    }


def reference(x_prompt, x_sample, cache_mla_ckv, cache_mla_krope, cache_diff_k, cache_diff_v,
              cache_mem_k, cache_mem_v, mem_prompt, pre_mix_g, w_in, mla_q_norm_g, mla_w_uq,
              mla_kv_norm_g, mla_w_uk, mla_w_uv, diff_lq1, diff_lk1, diff_lq2, diff_lk2,
              diff_subln_g, mem_norm_g, w_mem_k, w_mem_v, w_o_mla, w_o_diff, w_o_mem, w_gate,
              b_gate, w_out, post_mix_g, pre_mlp_g, w_mlp_up, w_mlp_down, post_mlp_g):
    past_len = cache_mla_ckv.shape[2]
    pos_p = jnp.arange(x_prompt.shape[1], dtype=jnp.int32)
    pos_s = past_len + jnp.arange(x_sample.shape[1], dtype=jnp.int32)
    xp, xs = x_prompt, x_sample
    p_ckv, p_kr, p_dk, p_dv, p_mk, p_mv = [], [], [], [], [], []
    s_ckv, s_kr, s_dk, s_dv = [], [], [], []
    for l in range(DEPTH):
        lam_init = 0.8 - 0.6 * math.exp(-0.3 * l)
        p = {
            'pre_mix_g': pre_mix_g[l], 'w_in': w_in[l],
            'mla_q_norm_g': mla_q_norm_g[l], 'mla_w_uq': mla_w_uq[l],
            'mla_kv_norm_g': mla_kv_norm_g[l], 'mla_w_uk': mla_w_uk[l], 'mla_w_uv': mla_w_uv[l],
            'diff_lq1': diff_lq1[l], 'diff_lk1': diff_lk1[l],
            'diff_lq2': diff_lq2[l], 'diff_lk2': diff_lk2[l], 'diff_subln_g': diff_subln_g[l],
            'w_o_mla': w_o_mla[l], 'w_o_diff': w_o_diff[l], 'w_o_mem': w_o_mem[l],
            'w_gate': w_gate[l], 'b_gate': b_gate[l], 'w_out': w_out[l],
            'post_mix_g': post_mix_g[l], 'pre_mlp_g': pre_mlp_g[l],
            'w_mlp_up': w_mlp_up[l], 'w_mlp_down': w_mlp_down[l], 'post_mlp_g': post_mlp_g[l],
        }
        mk_p, mv_p = memory_kv(mem_prompt, mem_norm_g[l], w_mem_k[l], w_mem_v[l])
        xp, rows_p = layer_forward(xp, pos_p, None, mk_p, mv_p, p, lam_init)
        past = (cache_mla_ckv[l], cache_mla_krope[l], cache_diff_k[l], cache_diff_v[l])
        xs, rows_s = layer_forward(xs, pos_s, past, cache_mem_k[l], cache_mem_v[l], p, lam_init)
        p_ckv.append(rows_p[0]); p_kr.append(rows_p[1]); p_dk.append(rows_p[2]); p_dv.append(rows_p[3])
        p_mk.append(mk_p); p_mv.append(mv_p)
        s_ckv.append(rows_s[0]); s_kr.append(rows_s[1]); s_dk.append(rows_s[2]); s_dv.append(rows_s[3])
    new_p_ckv = jnp.stack(p_ckv, axis=0)
    new_p_krope = jnp.stack(p_kr, axis=0)
    new_p_dk = jnp.stack(p_dk, axis=0)
    new_p_dv = jnp.stack(p_dv, axis=0)
    new_p_mem_k = jnp.stack(p_mk, axis=0)
    new_p_mem_v = jnp.stack(p_mv, axis=0)
    new_s_ckv = jnp.stack(s_ckv, axis=0)
    new_s_krope = jnp.stack(s_kr, axis=0)
    new_s_dk = jnp.stack(s_dk, axis=0)
    new_s_dv = jnp.stack(s_dv, axis=0)
    return (xp, xs, new_p_ckv, new_p_krope, new_p_dk, new_p_dv, new_p_mem_k, new_p_mem_v,
            new_s_ckv, new_s_krope, new_s_dk, new_s_dv)
```

```python
import contextlib
import math
import numpy as np
import concourse.bass as bass
import concourse.mybir as mybir
from concourse.bass_utils import run_bass_kernel_spmd

F32 = mybir.dt.float32
BF16 = mybir.dt.bfloat16
ALU = mybir.AluOpType
AF = mybir.ActivationFunctionType
AX = mybir.AxisListType

ENGS = ('pe', 'act', 'dve', 'pool', 'sp')
SEM_CHUNK = 20000

D = 1024
T = 16384
NT = 128
NOWN = 18
NTOK = NOWN * 128
PAST = 4096
LS = 4224
EPS = 1e-6
MLA_SCALE = 96 ** -0.5
DIFF_SCALE = 32 ** -0.5
MEM_SCALE = 128 ** -0.5
LAM_INIT = 0.8 - 0.6 * math.exp(0.0)
VW = 80


class Op:
    __slots__ = ('eng', 'fn', 'deps', 'idx', 'signal', 'tok', 'dma_key', 'is_dma', 'is_bar')


class Sched:
    def __init__(self):
        self.ops = {e: [] for e in ENGS}
        self.lastw = {}
        self.readers = {}
        self.dma_counts = {}
        self.live_dma = {}

    def add(self, eng, fn, reads=(), writes=(), dma_key=None, extra_deps=()):
        op = Op()
        op.eng = eng
        op.fn = fn
        op.signal = False
        op.tok = None
        op.dma_key = dma_key
        op.is_dma = dma_key is not None
        op.is_bar = False
        deps = []
        seen = set()

        def push(d):
            if d is not None and id(d) not in seen:
                seen.add(id(d))
                deps.append(d)
        for d in extra_deps:
            push(d)
        for r in reads:
            push(self.lastw.get(r))
        for w_ in writes:
            push(self.lastw.get(w_))
            for rd in self.readers.get(w_, ()):
                push(rd)
        op.deps = deps
        for r in reads:
            self.readers.setdefault(r, []).append(op)
        for w_ in writes:
            self.lastw[w_] = op
            self.readers[w_] = []
        op.idx = len(self.ops[eng])
        self.ops[eng].append(op)
        if op.is_dma:
            c = self.dma_counts.get(dma_key, 0) + 1
            self.dma_counts[dma_key] = c
            op.tok = (('dma', dma_key), 16 * c)
            self.live_dma[dma_key] = op
        return op

    def barrier(self):
        last = []
        for e in ENGS:
            for op in reversed(self.ops[e]):
                if not op.is_dma and not op.is_bar:
                    last.append(op)
                    break
        dmas = list(self.live_dma.values())
        self.live_dma = {}
        self.lastw = {}
        self.readers = {}
        for e in ENGS:
            self.add(e, lambda eng: None, extra_deps=last + dmas).is_bar = True

    def emit(self, nc, es):
        for e in ENGS:
            for op in self.ops[e]:
                for d in op.deps:
                    if d.is_dma:
                        continue
                    if d.eng == op.eng and not op.is_dma and e == 'pe':
                        continue
                    d.signal = True
        nsem = {}
        for e in ENGS:
            c = 0
            for op in self.ops[e]:
                if op.is_dma:
                    continue
                if op.signal:
                    op.tok = (('eng', e, c // SEM_CHUNK), c % SEM_CHUNK + 1)
                    c += 1
            nsem[e] = (c + SEM_CHUNK - 1) // SEM_CHUNK
        sems = {}
        for e in ENGS:
            for k in range(nsem[e]):
                sems[('eng', e, k)] = es.enter_context(nc.semaphore(f"s_{e}_{k}"))
        for i, key in enumerate(self.dma_counts):
            sems[('dma', key)] = es.enter_context(nc.semaphore(f"d_{i}"))
        self.n_sems = len(sems)
        block = es.enter_context(nc.Block())

        def run(e, eng):
            waited = {}
            for op in self.ops[e]:
                need = {}
                for d in op.deps:
                    if not d.is_dma and d.eng == e and not op.is_dma and e == 'pe':
                        continue
                    sk, v = d.tok
                    if waited.get(sk, 0) >= v:
                        continue
                    if need.get(sk, 0) < v:
                        need[sk] = v
                for sk, v in need.items():
                    eng.wait_ge(sems[sk], v)
                    waited[sk] = v
                    if sk[0] == 'eng':
                        for k in range(sk[2]):
                            waited[('eng', sk[1], k)] = SEM_CHUNK
                ins = op.fn(eng)
                if op.is_dma:
                    ins.then_inc(sems[op.tok[0]], 16)
                elif op.signal:
                    ins.then_inc(sems[op.tok[0]], 1)
            if e == 'sp':
                for key, c in self.dma_counts.items():
                    sk = ('dma', key)
                    if waited.get(sk, 0) < 16 * c:
                        eng.wait_ge(sems[sk], 16 * c)

        @block.tensor
        def _(eng):
            run('pe', eng)

        @block.scalar
        def _(eng):
            run('act', eng)

        @block.vector
        def _(eng):
            run('dve', eng)

        @block.gpsimd
        def _(eng):
            run('pool', eng)

        @block.sync
        def _(eng):
            run('sp', eng)


def build_program():
    nc = bass.Bass("TRN2", target_bir_lowering=False)
    S = Sched()

    def din(name, shape):
        return nc.dram_tensor(name, list(shape), F32, kind="ExternalInput").ap()

    def dout(name, shape):
        return nc.dram_tensor(name, list(shape), F32, kind="ExternalOutput").ap()

    def dscr(name, shape, dt=BF16):
        return nc.dram_tensor(name, list(shape), dt).ap()

    x_all = din("x_all", [T, D])
    x_own = din("x_own", [NTOK, D])
    cs_all = din("cs_all", [T, 40])
    cs_own = din("cs_own", [NTOK, 40])
    maskd = din("maskd", [128, 8 * 128])
    c_ckv = din("c_ckv", [2, PAST, 256])
    c_kr = din("c_kr", [2, PAST, 32])
    c_dk = din("c_dk", [2, PAST, 512])
    c_dv = din("c_dv", [2, PAST, 512])
    c_mk = din("c_mk", [2, 256, 512])
    c_mv = din("c_mv", [2, 256, 512])
    mem_p = din("mem_p", [256, D])
    w_in = din("w_in", [D, 2720])
    w_uq = din("w_uq", [384, 768])
    w_uk = din("w_uk", [256, 512])
    w_uv = din("w_uv", [256, 512])
    w_mk = din("w_mk", [D, 512])
    w_mv = din("w_mv", [D, 512])
    w_oa = din("w_oa", [512, D])
    w_ob = din("w_ob", [512, D])
    w_oc = din("w_oc", [512, D])
    w_gate = din("w_gate", [D, 3072])
    b_gate = din("b_gate", [1, 3072])
    w_out = din("w_out", [D, D])
    w_up = din("w_up", [D, 4096])
    w_dn = din("w_dn", [4096, D])
    g_pre = din("g_pre", [1, D])
    g_q = din("g_q", [1, 384])
    g_kv = din("g_kv", [1, 256])
    g_sub = din("g_sub", [1, 64])
    g_mem = din("g_mem", [1, D])
    g_pm = din("g_pm", [1, D])
    g_mlp = din("g_mlp", [1, D])
    g_post = din("g_post", [1, D])
    lam_in = din("lam_in", [1, 128])

    y_own = dout("y_own", [NTOK, D])
    o_ckv = dout("o_ckv", [T, 256])
    o_kr = dout("o_kr", [T, 32])
    o_dk = dout("o_dk", [T, 512])
    o_dv = dout("o_dv", [T, 512])
    o_mk = dout("o_mk", [256, 512])
    o_mv = dout("o_mv", [256, 512])
    o_s_ckv = dout("o_s_ckv", [256, 256])
    o_s_kr = dout("o_s_kr", [256, 32])
    o_s_dk = dout("o_s_dk", [256, 512])
    o_s_dv = dout("o_s_dv", [256, 512])

    Ls = [T, LS, LS]
    KTm = [dscr(f"KTm{i}", [8, 64, L]) for i, L in enumerate(Ls)]
    KRT = [dscr(f"KRT{i}", [32, L]) for i, L in enumerate(Ls)]
    KTd = [dscr(f"KTd{i}", [8, 64, L]) for i, L in enumerate(Ls)]
    Vm = [dscr(f"Vm{i}", [8, 128, L // 128, VW]) for i, L in enumerate(Ls)]
    Vd = [dscr(f"Vd{i}", [8, 128, L // 128, VW]) for i, L in enumerate(Ls)]
    QTm = dscr("QTm", [8, 96, NTOK])
    QTd = dscr("QTd", [8, 64, NTOK])
    QTc = dscr("QTc", [4, 128, NTOK])
    X1s = dscr("X1s", [NTOK, D], F32)

    es = contextlib.ExitStack()
    with es:
        def sb(stack, name, shape, dt):
            return stack.enter_context(nc.sbuf_tensor(name, list(shape), dt))

        def A(eng, fn, r=(), w=(), key=None):
            return S.add(eng, fn, reads=r, writes=w, dma_key=key)

        def dma(q, out, in_, r=(), w=(), key=None, slow=False):
            if slow:
                return A(q, lambda e: e.dma_start(out=out, in_=in_, allow_slow_non_contiguous=True), r, w, key)
            return A(q, lambda e: e.dma_start(out=out, in_=in_), r, w, key)

        def act(out, in_, func, r, w, **kw):
            return A('act', lambda e: e.activation(out=out, in_=in_, func=func, **kw), r, w)

        def tt(eng, out, in0, in1, op, r, w):
            return A(eng, lambda e: e.tensor_tensor(out=out, in0=in0, in1=in1, op=op), r, w)

        def ts(eng, out, in0, s1, s2, op0, op1, r, w):
            if op1 is None:
                return A(eng, lambda e: e.tensor_scalar(out=out, in0=in0, scalar1=s1, scalar2=None, op0=op0), r, w)
            return A(eng, lambda e: e.tensor_scalar(out=out, in0=in0, scalar1=s1, scalar2=s2, op0=op0, op1=op1), r, w)

        def stt(eng, out, in0, scalar, in1, op0, op1, r, w):
            return A(eng, lambda e: e.scalar_tensor_tensor(out=out, in0=in0, scalar=scalar, in1=in1,
                                                           op0=op0, op1=op1), r, w)

        def cp(eng, out, in_, r, w):
            if eng == 'act':
                return A('act', lambda e: e.activation(out=out, in_=in_, func=AF.Copy), r, w)
            return A(eng, lambda e: e.tensor_copy(out=out, in_=in_), r, w)

        def mms(lst, r, w):
            def fn(e):
                ins = None
                for (o, l, rh, st, sp) in lst:
                    ins = e.matmul(o, lhsT=l, rhs=rh, start=st, stop=sp)
                return ins
            return A('pe', fn, r, w)

        def trs(lst, r, w):
            def fn(e):
                ins = None
                for (o, i, idn) in lst:
                    ins = e.transpose(out=o, in_=i, identity=idn)
                return ins
            return A('pe', fn, list(r) + ['ident'], w)

        def memset(eng, ap, val, w):
            return A(eng, lambda e: e.memset(ap, val), (), w)

        def treduce(out, in_, r, w):
            return A('dve', lambda e: e.tensor_reduce(out=out, in_=in_, axis=AX.X, op=ALU.add), r, w)

        def recip(out, in_, r, w):
            return A('dve', lambda e: e.reciprocal(out=out, in_=in_), r, w)

        PS = [es.enter_context(nc.psum_tensor(f"ps{i}", [128, 512], F32)) for i in range(8)]
        PSb = [p[:].bitcast(BF16).rearrange("p (a b) -> p a b", b=128) for p in PS]

        def pk(i):
            return ('ps', i)

        ident = sb(es, "ident", [128, 128], BF16)
        identf = sb(es, "identf", [128, 128], F32)
        eps_t = sb(es, "eps_t", [128, 1], F32)
        gc_pre = sb(es, "gc_pre", [128, 8], F32)
        gc_q = sb(es, "gc_q", [128, 3], F32)
        gc_mem = sb(es, "gc_mem", [128, 8], F32)
        gc_mlp = sb(es, "gc_mlp", [128, 8], F32)
        lam_t = sb(es, "lam_t", [128, 128], F32)
        lam_s = sb(es, "lam_s", [128, 8], F32)
        junk = sb(es, "junk", [128, 1024], F32)

        def mk_ident(e):
            e.memset(identf[:], 0.0)
            return e.affine_select(out=identf[:], in_=identf[:], pattern=[[-1, 128]], compare_op=ALU.not_equal,
                                   fill=1.0, base=0, channel_multiplier=1)
        A('pool', mk_ident, (), ['identf'])
        cp('dve', ident[:], identf[:], ['identf'], ['ident'])
        memset('dve', eps_t[:], EPS, ['eps'])
        for nm, gt, gd, n in (("pre", gc_pre, g_pre, 8), ("q", gc_q, g_q, 3), ("mem", gc_mem, g_mem, 8),
                              ("mlp", gc_mlp, g_mlp, 8)):
            dma('sp', gt[:], gd.rearrange("o (k p) -> p (o k)", p=128), (), [('gc', nm)], key=('gc', nm), slow=True)
        dma('sp', lam_t[:], lam_in[0:1, :].partition_broadcast(128), (), ['lam_t'], key='lam_t')
        tt('dve', lam_t[:, 0:32], lam_t[:, 0:32], lam_t[:, 32:64], ALU.mult, ['lam_t'], ['lam_a'])
        tt('dve', lam_t[:, 64:96], lam_t[:, 64:96], lam_t[:, 96:128], ALU.mult, ['lam_t'], ['lam_b'])
        A('dve', lambda e: e.tensor_reduce(out=lam_s[:, 0:1], in_=lam_t[:, 0:32], axis=AX.X, op=ALU.add),
          ['lam_a'], ['lam0'])
        A('dve', lambda e: e.tensor_reduce(out=lam_s[:, 1:2], in_=lam_t[:, 64:96], axis=AX.X, op=ALU.add),
          ['lam_b'], ['lam1'])
        act(lam_s[:, 2:4], lam_s[:, 0:2], AF.Exp, ['lam0', 'lam1'], ['lam2'])
        stt('dve', lam_s[:, 4:5], lam_s[:, 3:4], -LAM_INIT, lam_s[:, 2:3], ALU.add, ALU.subtract, ['lam2'], ['neglam'])
        neglam = lam_s[:, 4:5]

        wl_cnt = [0]

        def load_w(stack_stage, dst, src, nk, gcol=None, gres=None, res=None):
            cols = src.shape[1]
            for kc in range(nk):
                rows = src[kc * 128:(kc + 1) * 128, :]
                if gcol is None:
                    dma('pool', dst[:, kc, :], rows, (), [(res, kc)], key=('wl', res))
                else:
                    i = wl_cnt[0] % len(stack_stage)
                    wl_cnt[0] += 1
                    stg_ = stack_stage[i]
                    dma('sp', stg_[:, 0:cols], rows, (), [('stg', i)], key=('stg', i))
                    if wl_cnt[0] % 2:
                        act(dst[:, kc, :], stg_[:, 0:cols], AF.Copy, [('stg', i), gres], [(res, kc)],
                            scale=gcol[:, kc:kc + 1])
                    else:
                        ts('dve', dst[:, kc, :], stg_[:, 0:cols], gcol[:, kc:kc + 1], None, ALU.mult, None,
                           [('stg', i), gres], [(res, kc)])

        def wres(res, nk):
            return [(res, kc) for kc in range(nk)]

        def rstd_from(ssq_ap, tmp_ap, out_ap, scale, r, wtmp, wout):
            act(tmp_ap, ssq_ap, AF.Sqrt, list(r) + ['eps'], [wtmp], scale=scale, bias=eps_t[:])
            recip(out_ap, tmp_ap, [wtmp], [wout])

        def x_front(B, s, xsrc, psb=0):
            dma('sp', B['x32'][s][:], xsrc, (), [('x32', s)], key=('x32', s))
            dma('pool', B['xb'][s][:], xsrc, (), [('xb', s)], key=('xb', s))
            st_ = B['st'][s]
            act(junk[:], B['x32'][s][:], AF.Square, [('x32', s)], ['junk', ('st', s, 0)], accum_out=st_[:, 0:1])
            rstd_from(st_[:, 0:1], st_[:, 1:2], st_[:, 2:3], 1.0 / D, [('st', s, 0)], ('st', s, 1), ('st', s, 2))
            trs([(PSb[psb][:, kc, :], B['xb'][s][:, kc * 128:(kc + 1) * 128], ident[:]) for kc in range(8)],
                [('xb', s)], [pk(psb)])
            cp('dve', B['xT'][s][:], PSb[psb][:], [pk(psb)], [('xT', s)])

        def rope(view, x1s, x2s, cosb, sinb, tmp, rr, ww, shp):
            t1, t2, t3, t4 = (tmp[:, i, :].rearrange("p (g h) -> p g h", h=shp[1])[:, 0:shp[0], :] for i in range(4))
            x1 = view[:, :, x1s]
            x2 = view[:, :, x2s]
            tt('dve', t1, x1, cosb, ALU.mult, rr, ['rt1'])
            tt('dve', t2, x2, sinb, ALU.mult, rr, ['rt2'])
            tt('dve', t3, x2, cosb, ALU.mult, rr, ['rt3'])
            tt('dve', t4, x1, sinb, ALU.mult, rr, ['rt4'])
            tt('dve', x1, t1, t2, ALU.subtract, ['rt1', 'rt2', 'rt4'] + list(rr), ww)
            tt('dve', x2, t3, t4, ALU.add, ['rt3', 'rt4'] + list(ww), ww)

        with contextlib.ExitStack() as ph:
            wkv = sb(ph, "wkv", [128, 8, 1312], BF16)
            wuk = sb(ph, "wuk", [128, 2, 512], BF16)
            wuv = sb(ph, "wuv", [128, 2, 512], BF16)
            gkv_bc = sb(ph, "gkv_bc", [128, 256], F32)
            B = {
                'xT': [sb(ph, f"axT_{i}", [128, 8, 128], BF16) for i in range(3)],
                'st': [sb(ph, f"ast_{i}", [128, 8], F32) for i in range(3)],
            }
            X4 = [sb(ph, f"ax32_{i}", [128, D], F32) for i in range(4)]
            XB4 = [sb(ph, f"axb_{i}", [128, D], BF16) for i in range(4)]
            R = [sb(ph, f"R_{i}", [128, 1312], F32) for i in range(3)]
            CS = [sb(ph, f"CS_{i}", [128, 40], F32) for i in range(4)]
            Rb = [sb(ph, f"Rb_{i}", [128, 1312], BF16) for i in range(2)]
            TK = [sb(ph, f"TK_{i}", [128, 7, 128], BF16) for i in range(2)]
            KNs = [sb(ph, f"KNs_{i}", [128, 4, 512], BF16) for i in range(2)]
            DKs = [sb(ph, f"DKs_{i}", [128, 4, 512], BF16) for i in range(2)]
            KRs = [sb(ph, f"KRs_{i}", [32, 512], BF16) for i in range(2)]
            VMs = [sb(ph, f"VMs_{i}", [128, 8, 4, VW], BF16) for i in range(2)]
            VDs = [sb(ph, f"VDs_{i}", [128, 8, 4, VW], BF16) for i in range(2)]
            rtmp = sb(ph, "rtmp", [128, 4, 128], F32)
            stg = [sb(ph, f"astg{i}", [128, 1024], F32) for i in range(4)]

            load_w(stg, wkv[:, :, 0:288], w_in[:, 384:672], 8, gc_pre, ('gc', 'pre'), 'wkva')
            load_w(stg, wkv[:, :, 288:1312], w_in[:, 1184:2208], 8, gc_pre, ('gc', 'pre'), 'wkvb')
            load_w(stg, wuk, w_uk, 2, None, None, 'wuk')
            load_w(stg, wuv, w_uv, 2, None, None, 'wuv')
            WKV = wres('wkva', 8) + wres('wkvb', 8)
            dma('sp', gkv_bc[:], g_kv[0:1, :].partition_broadcast(128), (), ['gkv_bc'], key='gkv_bc')
            for i in range(2):
                memset('pool', VMs[i][:], 0.0, [('VMs', i)])
                memset('pool', VMs[i][:, :, :, 64:65], 1.0, [('VMs', i)])
                memset('pool', VDs[i][:], 0.0, [('VDs', i)])
                memset('pool', VDs[i][:, :, :, 64:65], 1.0, [('VDs', i)])

            NA = NT + 2

            def a_src(k):
                if k < NT:
                    return x_all[k * 128:(k + 1) * 128, :], cs_all[k * 128:(k + 1) * 128, :]
                tq = 16 + (k - NT)
                return x_own[tq * 128:(tq + 1) * 128, :], cs_own[tq * 128:(tq + 1) * 128, :]

            def A_load(k):
                s4 = k % 4
                xsrc, cssrc = a_src(k)
                dma('act', X4[s4][:], xsrc, (), [('x32', s4)], key=('x32', s4))
                dma('act', CS[s4][:], cssrc, (), [('CS', s4)], key=('CS', s4))

            def A_F1(k):
                s4 = k % 4
                s = k % 3
                st_ = B['st'][s]
                act(junk[:], X4[s4][:], AF.Square, [('x32', s4)], ['junk', ('st', s, 0)], accum_out=st_[:, 0:1])
                rstd_from(st_[:, 0:1], st_[:, 1:2], st_[:, 2:3], 1.0 / D, [('st', s, 0)], ('st', s, 1), ('st', s, 2))
                cp('dve', XB4[s4][:], X4[s4][:], [('x32', s4)], [('xb', s4)])

            def A_T(k):
                s4 = k % 4
                s = k % 3
                psb = 0 if k % 2 == 0 else 7
                trs([(PSb[psb][:, kc, :], XB4[s4][:, kc * 128:(kc + 1) * 128], ident[:]) for kc in range(8)],
                    [('xb', s4)], [pk(psb)])
                cp('dve', B['xT'][s][:], PSb[psb][:], [pk(psb)], [('xT', s)])

            def A_proj(k):
                s = k % 3
                c4 = k % 4
                xT = B['xT'][s]
                st_ = B['st'][s]
                for bi, (c0, c1) in enumerate(((0, 512), (512, 1024), (1024, 1312))):
                    mms([(PS[1 + bi][:, 0:c1 - c0], xT[:, kc, :], wkv[:, kc, c0:c1], kc == 0, kc == 7)
                         for kc in range(8)], [('xT', s)] + WKV, [pk(1 + bi)])
                    act(R[s][:, c0:c1], PS[1 + bi][:, 0:c1 - c0], AF.Copy, [pk(1 + bi), ('st', s, 2)],
                        [('R', s, bi)], scale=st_[:, 2:3])
                act(junk[:, 0:256], R[s][:, 0:256], AF.Square, [('R', s, 0)], ['junk', ('st', s, 3)],
                    accum_out=st_[:, 3:4])
                rstd_from(st_[:, 3:4], st_[:, 4:5], st_[:, 5:6], 1.0 / 256, [('st', s, 3)], ('st', s, 4), ('st', s, 5))
                stt('dve', R[s][:, 0:256], R[s][:, 0:256], st_[:, 5:6], gkv_bc[:], ALU.mult, ALU.mult,
                    [('R', s, 0), ('st', s, 5), 'gkv_bc'], [('R', s, 'ckv')])
                krv = R[s][:, 256:288].rearrange("p (g d) -> p g d", g=1)
                rope(krv, slice(0, 16), slice(16, 32), CS[c4][:, 0:16].unsqueeze(1), CS[c4][:, 16:32].unsqueeze(1),
                     rtmp, [('R', s, 0), ('CS', c4)], [('R', s, 'kr')], (1, 16))
                dkv = R[s][:, 288:800].rearrange("p (g d) -> p g d", d=32)
                rope(dkv, slice(0, 4), slice(4, 8), CS[c4][:, 32:36].unsqueeze(1).to_broadcast([128, 16, 4]),
                     CS[c4][:, 36:40].unsqueeze(1).to_broadcast([128, 16, 4]),
                     rtmp, [('R', s, 0), ('R', s, 1), ('CS', c4)], [('R', s, 'dk')], (16, 4))
                key = ('R_st', s)
                if k < NT:
                    rs = slice(k * 128, (k + 1) * 128)
                    dsts = (o_ckv[rs, :], o_kr[rs, :], o_dk[rs, :], o_dv[rs, :])
                else:
                    rs = slice((k - NT) * 128, (k - NT + 1) * 128)
                    dsts = (o_s_ckv[rs, :], o_s_kr[rs, :], o_s_dk[rs, :], o_s_dv[rs, :])
                for dst, (c0, c1) in zip(dsts, ((0, 256), (256, 288), (288, 800), (800, 1312))):
                    dma('sp', dst, R[s][:, c0:c1], r_all(s), (), key=key)

            def r_all(s):
                return [('R', s, 0), ('R', s, 1), ('R', s, 2), ('R', s, 'ckv'), ('R', s, 'kr'), ('R', s, 'dk')]

            def kv_S1(s, g, u, s2):
                cp('act', Rb[s2][:, 0:800], R[s][:, 0:800], r_all(s), [('Rb', s2)])
                cp('pool', VDs[g][:, :, u, 0:64], R[s][:, 800:1312].rearrange("p (h e) -> p h e", e=64),
                   r_all(s) + [('VDs', g)], [('VDs', g, u)])
                lst = [(PSb[4][:, i, :], Rb[s2][:, i * 128:(i + 1) * 128], ident[:]) for i in range(2)]
                lst += [(PSb[4][:, 2 + i, :], Rb[s2][:, 288 + i * 128:288 + (i + 1) * 128], ident[:]) for i in range(4)]
                lst += [(PSb[4][0:32, 6, :], Rb[s2][:, 256:288], ident[:])]
                trs(lst, [('Rb', s2)], [pk(4)])
                cp('dve', TK[s2][:, 0:2, :], PSb[4][:, 0:2, :], [pk(4)], [('TK', s2, 0)])
                cp('dve', DKs[g][:, :, u * 128:(u + 1) * 128], PSb[4][:, 2:6, :], [pk(4)], [('DKs', g, u)])
                cp('dve', KRs[g][:, u * 128:(u + 1) * 128], PSb[4][0:32, 6, :], [pk(4)], [('KRs', g, u)])

            def kv_S2(s, g, u, s2):
                lst = []
                for ch in range(4):
                    for kc in range(2):
                        lst.append((PS[5][:, ch * 128:(ch + 1) * 128], wuk[:, kc, ch * 128:(ch + 1) * 128],
                                    TK[s2][:, kc, :], kc == 0, kc == 1))
                mms(lst, [('TK', s2, 0)] + wres('wuk', 2), [pk(5)])
                act(KNs[g][:, :, u * 128:(u + 1) * 128], PS[5][:].rearrange("p (c t) -> p c t", t=128), AF.Copy,
                    [pk(5)], [('KNs', g, u)])
                mms([(PS[6][:], TK[s2][:, kc, :], wuv[:, kc, :], kc == 0, kc == 1) for kc in range(2)],
                    [('TK', s2, 0)] + wres('wuv', 2), [pk(6)])
                act(VMs[g][:, :, u, 0:64], PS[6][:].rearrange("p (h e) -> p h e", e=64), AF.Copy,
                    [pk(6), ('VMs', g)], [('VMs', g, u)])

            def stage_res(g, nu):
                r = []
                for u in range(nu):
                    r += [('DKs', g, u), ('KRs', g, u), ('KNs', g, u), ('VMs', g, u), ('VDs', g, u)]
                return r

            def flush_stage(g, seq, t0, kt0):
                key = ('stg_st', g)
                rr = stage_res(g, 4)
                for hh in range(2):
                    dst = KTm[seq].rearrange("(ch hh) n t -> hh n ch t", hh=2)[hh][:, :, t0:t0 + 512]
                    dma('sp', dst, KNs[g][hh * 64:(hh + 1) * 64, :, :], rr, (), key=key)
                    dst = KTd[seq].rearrange("(ch hh) n t -> hh n ch t", hh=2)[hh][:, :, t0:t0 + 512]
                    dma('sp', dst, DKs[g][hh * 64:(hh + 1) * 64, :, :], rr, (), key=key)
                dma('sp', KRT[seq][:, t0:t0 + 512], KRs[g][:, :], rr, (), key=key)
                dma('sp', Vm[seq][:, :, kt0:kt0 + 4, :].rearrange("h p k e -> p h k e"), VMs[g][:], rr, (), key=key)
                dma('sp', Vd[seq][:, :, kt0:kt0 + 4, :].rearrange("h p k e -> p h k e"), VDs[g][:], rr, (), key=key)

            def st_args(k):
                if k < NT:
                    return (k % 3, (k // 4) % 2, k % 4, k % 2)
                return (k % 3, 0, k - NT, k % 2)
            A_load(0)
            A_load(1)
            A_F1(0)
            for it in range(NA + 2):
                if it + 2 < NA:
                    A_load(it + 2)
                if it + 1 < NA:
                    A_F1(it + 1)
                if it < NA:
                    A_T(it)
                if it >= 2:
                    kv_S1(*st_args(it - 2))
                if 1 <= it < NA + 1:
                    A_proj(it - 1)
                if it >= 2:
                    k = it - 2
                    kv_S2(*st_args(k))
                    if k < NT and k % 4 == 3:
                        flush_stage((k // 4) % 2, 0, (k - 3) * 128, k - 3)
            rr = stage_res(0, 2)
            key = ('stg_st', 0)
            for b in range(2):
                seq = 1 + b
                for hh in range(2):
                    dst = KTm[seq].rearrange("(ch hh) n t -> hh n ch t", hh=2)[hh][:, :, PAST:PAST + 32]
                    dma('sp', dst, KNs[0][hh * 64:(hh + 1) * 64, :, b * 128:b * 128 + 32], rr, (), key=key, slow=True)
                    dst = KTd[seq].rearrange("(ch hh) n t -> hh n ch t", hh=2)[hh][:, :, PAST:PAST + 32]
                    dma('sp', dst, DKs[0][hh * 64:(hh + 1) * 64, :, b * 128:b * 128 + 32], rr, (), key=key, slow=True)
                dma('sp', KRT[seq][:, PAST:PAST + 32], KRs[0][:, b * 128:b * 128 + 32], rr, (), key=key, slow=True)
                dma('sp', Vm[seq][:, 0:32, 32, :].rearrange("h p e -> p h e"), VMs[0][0:32, :, b, :], rr, (), key=key)
                dma('sp', Vd[seq][:, 0:32, 32, :].rearrange("h p e -> p h e"), VDs[0][0:32, :, b, :], rr, (), key=key)
            def C_load(k):
                b, t = divmod(k, 32)
                s = k % 3
                rs = slice(t * 128, (t + 1) * 128)
                dma('act', R[s][:, 0:256], c_ckv[b, rs, :], (), [('R', s, 'ckv'), ('R', s, 0)], key=('Rl', s))
                dma('act', R[s][:, 256:288], c_kr[b, rs, :], (), [('R', s, 'kr')], key=('Rl', s))
                dma('act', R[s][:, 288:800], c_dk[b, rs, :], (), [('R', s, 'dk'), ('R', s, 1)], key=('Rl', s))
                dma('act', R[s][:, 800:1312], c_dv[b, rs, :], (), [('R', s, 2)], key=('Rl', s))
            def c_args(k):
                return (k % 3, 1 - ((k // 4) % 2), k % 4, k % 2)
            C_load(0)
            C_load(1)
            for it in range(64 + 1):
                if it + 2 < 64:
                    C_load(it + 2)
                if it < 64:
                    kv_S1(*c_args(it))
                if it >= 1:
                    k = it - 1
                    b, t = divmod(k, 32)
                    kv_S2(*c_args(k))
                    if k % 4 == 3:
                        flush_stage(1 - ((k // 4) % 2), 1 + b, (t - 3) * 128, t - 3)
            S.barrier()

        mid = contextlib.ExitStack()
        mid.__enter__()
        KTc = sb(mid, "KTc", [128, 3, 4, 256], BF16)
        Vc = sb(mid, "Vc", [128, 3, 2, 4, 132], BF16)
        Gs = dscr("Gs", [NTOK, 3072], F32)
        memset('pool', Vc[:], 0.0, ['Vc0'])
        memset('pool', Vc[:, :, :, :, 128:129], 1.0, ['Vc0'])
        with contextlib.ExitStack() as ph:
            wq = sb(ph, "wq", [128, 8, 1408], BF16)
            wuq = sb(ph, "wuq", [128, 3, 768], BF16)
            wmk = sb(ph, "wmk", [128, 8, 512], BF16)
            wmv = sb(ph, "wmv", [128, 8, 512], BF16)
            B = {
                'x32': [sb(ph, f"bx32_{i}", [128, D], F32) for i in range(2)],
                'xb': [sb(ph, f"bxb_{i}", [128, D], BF16) for i in range(2)],
                'xT': [sb(ph, f"bxT_{i}", [128, 8, 128], BF16) for i in range(2)],
                'st': [sb(ph, f"bst_{i}", [128, 12], F32) for i in range(2)],
            }
            CS = [sb(ph, f"bCS_{i}", [128, 40], F32) for i in range(2)]
            Q1 = [sb(ph, f"Q1_{i}", [128, 512], F32) for i in range(2)]
            MQb = [sb(ph, f"MQb_{i}", [128, 512], BF16) for i in range(2)]
            DQb = [sb(ph, f"DQb_{i}", [128, 512], BF16) for i in range(2)]
            CQT = [sb(ph, f"CQT_{i}", [128, 3, 128], BF16) for i in range(2)]
            Qf = [sb(ph, f"Qf_{i}", [128, 768], F32) for i in range(2)]
            Qb = [sb(ph, f"Qb_{i}", [128, 768], BF16) for i in range(2)]
            QTa = [sb(ph, f"QTa_{i}", [128, 8, 128], BF16) for i in range(2)]
            QTb = [sb(ph, f"QTb_{i}", [128, 8, 128], BF16) for i in range(2)]
            QTe = [sb(ph, f"QTe_{i}", [128, 4, 128], BF16) for i in range(2)]
            MKf = [sb(ph, f"MKf_{i}", [128, 512], F32) for i in range(2)]
            MKb = [sb(ph, f"MKb_{i}", [128, 512], BF16) for i in range(2)]
            rtmp = sb(ph, "brtmp", [128, 4, 128], F32)
            stg = [sb(ph, f"bstg{i}", [128, 1024], F32) for i in range(4)]
            wg = sb(ph, "wg", [128, 8, 3072], BF16)
            bg_bc = sb(ph, "bg_bc", [128, 3072], F32)
            Gb = sb(ph, "Gb", [128, 3072], F32)
            for i3 in range(3):
                load_w(stg, wg[:, :, i3 * 1024:(i3 + 1) * 1024], w_gate[:, i3 * 1024:(i3 + 1) * 1024], 8, gc_pre,
                       ('gc', 'pre'), f'wg{i3}')
            WG = wres('wg0', 8) + wres('wg1', 8) + wres('wg2', 8)
            dma('sp', bg_bc[:], b_gate[0:1, :].partition_broadcast(128), (), ['bg_bc'], key='bg_bc')

            load_w(stg, wq[:, :, 0:512], w_in[:, 672:1184], 8, gc_pre, ('gc', 'pre'), 'wqa')
            load_w(stg, wq[:, :, 512:1024], w_in[:, 2208:2720], 8, gc_pre, ('gc', 'pre'), 'wqb')
            load_w(stg, wq[:, :, 1024:1408], w_in[:, 0:384], 8, gc_pre, ('gc', 'pre'), 'wqc')
            load_w(stg, wuq, w_uq, 3, gc_q, ('gc', 'q'), 'wuq')
            load_w(stg, wmk, w_mk, 8, gc_mem, ('gc', 'mem'), 'wmk')
            load_w(stg, wmv, w_mv, 8, gc_mem, ('gc', 'mem'), 'wmv')
            WQ = wres('wqa', 8) + wres('wqb', 8) + wres('wqc', 8)

            for mt in range(2):
                s = mt
                x_front(B, s, mem_p[mt * 128:(mt + 1) * 128, :], psb=0)
                st_ = B['st'][s]
                for wi, (wt, wn, od) in enumerate(((wmk, 'wmk', o_mk), (wmv, 'wmv', o_mv))):
                    mms([(PS[1 + wi][:], B['xT'][s][:, kc, :], wt[:, kc, :], kc == 0, kc == 7) for kc in range(8)],
                        [('xT', s)] + wres(wn, 8), [pk(1 + wi)])
                    i2 = (2 * mt + wi) % 2
                    act(MKf[i2][:], PS[1 + wi][:], AF.Copy, [pk(1 + wi), ('st', s, 2)], [('MKf', i2)],
                        scale=st_[:, 2:3])
                    dma('sp', od[mt * 128:(mt + 1) * 128, :], MKf[i2][:], [('MKf', i2)], (), key=('MKf_st', i2))
                    if wi == 0:
                        cp('pool', MKb[i2][:], MKf[i2][:], [('MKf', i2)], [('MKb', i2)])
                        trs([(PSb[3][:, h, :], MKb[i2][:, h * 128:(h + 1) * 128], ident[:]) for h in range(4)],
                            [('MKb', i2)], [pk(3)])
                        cp('dve', KTc[:, 0, :, mt * 128:(mt + 1) * 128], PSb[3][:, 0:4, :], [pk(3)],
                           [('KTc', 0, mt)])
                    else:
                        cp('pool', Vc[:, 0, mt, :, 0:128], MKf[i2][:].rearrange("p (h e) -> p h e", e=128),
                           [('MKf', i2), 'Vc0'], [('Vc', 0, mt)])
            for b in range(2):
                for mt in range(2):
                    i2 = mt
                    dma('sp', MKf[i2][:], c_mk[b, mt * 128:(mt + 1) * 128, :], (), [('MKf', i2)], key=('MKf', i2))
                    cp('pool', MKb[i2][:], MKf[i2][:], [('MKf', i2)], [('MKb', i2)])
                    trs([(PSb[3][:, h, :], MKb[i2][:, h * 128:(h + 1) * 128], ident[:]) for h in range(4)],
                        [('MKb', i2)], [pk(3)])
                    cp('dve', KTc[:, 1 + b, :, mt * 128:(mt + 1) * 128], PSb[3][:, 0:4, :], [pk(3)],
                       [('KTc', 1 + b, mt)])
                    dma('sp', MKf[i2][:], c_mv[b, mt * 128:(mt + 1) * 128, :], (), [('MKf', i2)], key=('MKf', i2))
                    cp('pool', Vc[:, 1 + b, mt, :, 0:128], MKf[i2][:].rearrange("p (h e) -> p h e", e=128),
                       [('MKf', i2), 'Vc0'], [('Vc', 1 + b, mt)])

            def B_F(tq):
                s = tq % 2
                x_front(B, s, x_own[tq * 128:(tq + 1) * 128, :], psb=0)
                dma('sp', CS[s][:], cs_own[tq * 128:(tq + 1) * 128, :], (), [('CS', s)], key=('CS', s))

            def B_S1(tq):
                s = tq % 2
                xT = B['xT'][s]
                st_ = B['st'][s]
                for gi in range(3):
                    for hf in range(2):
                        c0 = gi * 1024 + hf * 512
                        bk = 1 + hf
                        mms([(PS[bk][:], xT[:, kc, :], wg[:, kc, c0:c0 + 512], kc == 0, kc == 7) for kc in range(8)],
                            [('xT', s)] + WG, [pk(bk)])
                        stt('dve', Gb[:, c0:c0 + 512], PS[bk][:], st_[:, 2:3], bg_bc[:, c0:c0 + 512], ALU.mult, ALU.add,
                            [pk(bk), ('st', s, 2), 'bg_bc'], [('Gb', gi, hf)])
                        act(Gb[:, c0:c0 + 512], Gb[:, c0:c0 + 512], AF.Sigmoid, [('Gb', gi, hf)], [('Gb', gi, hf)])
                dma('sp', Gs[tq * 128:(tq + 1) * 128, :], Gb[:], [('Gb', gi, hf) for gi in range(3) for hf in range(2)],
                    (), key='Gb_st')
                for bi, (c0, c1) in enumerate(((0, 512), (512, 1024), (1024, 1408))):
                    mms([(PS[3 + bi][:, 0:c1 - c0], xT[:, kc, :], wq[:, kc, c0:c1], kc == 0, kc == 7)
                         for kc in range(8)], [('xT', s)] + WQ, [pk(3 + bi)])
                act(Q1[s][:], PS[3][:], AF.Copy, [pk(3), ('st', s, 2)], [('Q1', s)], scale=st_[:, 2:3])
                act(MQb[s][:], PS[4][:], AF.Copy, [pk(4), ('st', s, 2)], [('MQb', s)], scale=st_[:, 2:3])
                act(junk[:, 0:384], PS[5][:, 0:384], AF.Square, [pk(5), ('st', s, 2)], ['junk', ('st', s, 3)],
                    scale=st_[:, 2:3], accum_out=st_[:, 3:4])
                rstd_from(st_[:, 3:4], st_[:, 4:5], st_[:, 5:6], 1.0 / 384, [('st', s, 3)], ('st', s, 4), ('st', s, 5))
                tt('dve', st_[:, 6:7], st_[:, 5:6], st_[:, 2:3], ALU.mult, [('st', s, 5), ('st', s, 2)], [('st', s, 6)])
                lst = []
                for ch in range(3):
                    for kc in range(8):
                        lst.append((PS[6][:, ch * 128:(ch + 1) * 128], wq[:, kc, 1024 + ch * 128:1024 + (ch + 1) * 128],
                                    xT[:, kc, :], kc == 0, kc == 7))
                mms(lst, [('xT', s)] + WQ, [pk(6)])
                cp('dve', CQT[s][:], PS[6][:, 0:384].rearrange("p (c t) -> p c t", t=128), [pk(6)], [('CQT', s)])
                for bi, (c0, c1) in enumerate(((0, 512), (512, 768))):
                    mms([(PS[1 + bi][:, 0:c1 - c0], CQT[s][:, rc, :], wuq[:, rc, c0:c1], rc == 0, rc == 2)
                         for rc in range(3)], [('CQT', s)] + wres('wuq', 3), [pk(1 + bi)])
                    act(Qf[s][:, c0:c1], PS[1 + bi][:, 0:c1 - c0], AF.Copy, [pk(1 + bi), ('st', s, 6)],
                        [('Qf', s, bi)], scale=st_[:, 6:7])
                qv = Qf[s][:].rearrange("p (h d) -> p h d", d=96)
                rope(qv, slice(64, 80), slice(80, 96), CS[s][:, 0:16].unsqueeze(1).to_broadcast([128, 8, 16]),
                     CS[s][:, 16:32].unsqueeze(1).to_broadcast([128, 8, 16]), rtmp,
                     [('Qf', s, 0), ('Qf', s, 1), ('CS', s)], [('Qf', s, 'r')], (8, 16))
                cp('dve', Qb[s][:], Qf[s][:], [('Qf', s, 0), ('Qf', s, 1), ('Qf', s, 'r')], [('Qb', s)])
                dqv = Q1[s][:].rearrange("p (g d) -> p g d", d=32)
                rope(dqv, slice(0, 4), slice(4, 8), CS[s][:, 32:36].unsqueeze(1).to_broadcast([128, 16, 4]),
                     CS[s][:, 36:40].unsqueeze(1).to_broadcast([128, 16, 4]), rtmp,
                     [('Q1', s), ('CS', s)], [('Q1', s, 'r')], (16, 4))
                cp('act', DQb[s][:], Q1[s][:], [('Q1', s), ('Q1', s, 'r')], [('DQb', s)])

            def B_S2(tq):
                s = tq % 2
                trs([(PSb[7][0:96, h, :], Qb[s][:, h * 96:(h + 1) * 96], ident[:]) for h in range(8)],
                    [('Qb', s)], [pk(7)])
                cp('dve', QTa[s][0:96, :, :], PSb[7][0:96, :, :], [pk(7)], [('QTa', s)])
                dma('sp', QTm[:, :, tq * 128:(tq + 1) * 128].rearrange("h r t -> r h t"), QTa[s][0:96, :, :],
                    [('QTa', s)], (), key=('QTa_st', s), slow=True)
                trs([(PSb[7][0:64, h, :], DQb[s][:, h * 64:(h + 1) * 64], ident[:]) for h in range(8)],
                    [('DQb', s)], [pk(7)])
                cp('dve', QTb[s][0:64, :, :], PSb[7][0:64, :, :], [pk(7)], [('QTb', s)])
                dma('sp', QTd[:, :, tq * 128:(tq + 1) * 128].rearrange("h r t -> r h t"), QTb[s][0:64, :, :],
                    [('QTb', s)], (), key=('QTb_st', s), slow=True)
                trs([(PSb[7][:, h, :], MQb[s][:, h * 128:(h + 1) * 128], ident[:]) for h in range(4)],
                    [('MQb', s)], [pk(7)])
                cp('dve', QTe[s][:], PSb[7][:, 0:4, :], [pk(7)], [('QTe', s)])
                dma('sp', QTc[:, :, tq * 128:(tq + 1) * 128].rearrange("h r t -> r h t"), QTe[s][:],
                    [('QTe', s)], (), key=('QTe_st', s), slow=True)

            B_F(0)
            for it in range(NOWN + 1):
                if it + 1 < NOWN:
                    B_F(it + 1)
                if it < NOWN:
                    B_S1(it)
                if it >= 1:
                    B_S2(it - 1)
            S.barrier()

        OA = sb(mid, "OA", [128, NOWN, 512], BF16)
        OB = sb(mid, "OB", [128, NOWN, 512], BF16)
        OC = sb(mid, "OC", [128, NOWN, 512], BF16)
        for nm, o in (('OA', OA), ('OB', OB), ('OC', OC)):
            memset('pool', o[:, 16:18, :], 0.0, [(nm, 'z')])
        with contextlib.ExitStack() as ph:
            KT = [sb(ph, f"KT_{i}", [128, T], BF16) for i in range(2)]
            VV = [sb(ph, f"VV_{i}", [128, 128, VW], BF16) for i in range(2)]
            QT = [sb(ph, f"QT_{i}", [128, NTOK], BF16) for i in range(2)]
            PT = [sb(ph, f"PT_{i}", [128, 4, 128], BF16) for i in range(6)]
            MK = sb(ph, "MK", [128, 8, 128], BF16)
            mstg = sb(ph, "mstg", [128, 1024], F32)
            ot = [sb(ph, f"ot_{i}", [128, 8], F32) for i in range(2)]
            ot1 = [sb(ph, f"ot1_{i}", [128, 64], F32) for i in range(2)]
            dma('sp', mstg[:], maskd[:, :], (), ['mstg'], key='mstg')
            cp('dve', MK[:].rearrange("p a b -> p (a b)"), mstg[:], ['mstg'], ['MK'])

            pend = []
            gcnt = [0]
            dcnt = [0]
            ucnt = [0]

            def push_task(sfn, pvfn, postfn, la=4):
                sfn()
                pend.append((pvfn, postfn))
                while len(pend) > la:
                    pv, post = pend.pop(0)
                    pv()
                    if post is not None:
                        post()

            def drain():
                while pend:
                    pv, post = pend.pop(0)
                    pv()
                    if post is not None:
                        post()

            def attn_unit(kt_ap, q_ap, v_ap, nkt, nk_last, nq, scale, masked, obank, res_in, postfn, pair=False):
                ngr = (nkt + 3) // 4
                for gi in range(ngr):
                    k0 = gi * 4
                    kn = min(4, nkt - k0)
                    if pair:
                        slot = 2 * (dcnt[0] % 3)
                        dcnt[0] += 1
                    else:
                        slot = gcnt[0] % 6
                        gcnt[0] += 1
                    nks = [nk_last if (k0 + i == nkt - 1) else 128 for i in range(kn)]

                    def sfn(k0=k0, kn=kn, slot=slot, nks=nks, gi=gi):
                        mms([(PS[slot][0:nks[i], i * 128:i * 128 + nq], kt_ap(k0 + i, nks[i]), q_ap, True, True)
                             for i in range(kn)], res_in, [pk(slot)])
                        if all(n == 128 for n in nks):
                            act(PT[slot][:, 0:kn, 0:nq],
                                PS[slot][:, 0:kn * 128].rearrange("p (a b) -> p a b", b=128)[:, :, 0:nq],
                                AF.Exp, [pk(slot)], [('PT', slot)], scale=scale)
                        else:
                            for i in range(kn):
                                act(PT[slot][0:nks[i], i, 0:nq], PS[slot][0:nks[i], i * 128:i * 128 + nq], AF.Exp,
                                    [pk(slot)], [('PT', slot)], scale=scale)
                        if masked and gi >= ngr - 2:
                            r0 = (gi - (ngr - 2)) * 4
                            tt('dve', PT[slot][:, :, :], PT[slot][:, :, :], MK[:, r0:r0 + 4, :], ALU.mult,
                               [('PT', slot), 'MK'], [('PT', slot)])

                    def pvfn(k0=k0, kn=kn, slot=slot, nks=nks):
                        lst = []
                        for i in range(kn):
                            kt = k0 + i
                            va = v_ap(kt, nks[i])
                            lst.append((PS[obank][0:nq, 0:va.shape[1]], PT[slot][0:nks[i], i, 0:nq], va, kt == 0,
                                        kt == nkt - 1))
                        mms(lst, [('PT', slot)] + list(res_in), [pk(obank)])
                    push_task(sfn, pvfn, postfn if gi == ngr - 1 else None, la=2 if pair else 4)

            SREG = {1: (0, 0), 2: (8192, 64)}
            heads = [('m', h) for h in range(8)] + [('d', h) for h in range(8)]

            def load_prompt(i):
                kind, h = heads[i]
                slot = i % 2
                wk1 = [('KT', slot, 'p'), ('KT', slot, 1), ('KT', slot, 2)]
                wk2 = [('KT', slot, 'p2'), ('KT', slot, 1, 2), ('KT', slot, 2, 2)]
                wv = [('VV', slot, 'p'), ('VV', slot, 1), ('VV', slot, 2)]
                if kind == 'm':
                    dma('sp', KT[slot][0:64, 0:T], KTm[0][h], (), wk1, key=('KT', slot))
                    dma('sp', KT[slot][64:96, 0:T], KRT[0], (), wk2, key=('KT', slot))
                    dma('sp', VV[slot][:, 0:128, :], Vm[0][h], (), wv, key=('VV', slot))
                else:
                    dma('sp', KT[slot][0:64, 0:T], KTd[0][h], (), wk1 + wk2, key=('KT', slot))
                    dma('sp', VV[slot][:, 0:128, :], Vd[0][h], (), wv, key=('VV', slot))

            def load_sample(i):
                kind, h = heads[i]
                slot = (i + 1) % 2
                for seq in (1, 2):
                    c0, t0 = SREG[seq]
                    if kind == 'm':
                        dma('sp', KT[slot][0:64, c0:c0 + LS], KTm[seq][h], (), [('KT', slot, 'p'), ('KT', slot, seq)],
                            key=('KTs', slot, seq))
                        dma('sp', KT[slot][64:96, c0:c0 + LS], KRT[seq], (), [('KT', slot, 'p2'), ('KT', slot, seq, 2)],
                            key=('KTs', slot, seq))
                        dma('sp', VV[slot][:, t0:t0 + 33, :], Vm[seq][h], (), [('VV', slot, 'p'), ('VV', slot, seq)],
                            key=('VVs', slot, seq))
                    else:
                        dma('sp', KT[slot][0:64, c0:c0 + LS], KTd[seq][h], (),
                            [('KT', slot, 'p'), ('KT', slot, 'p2'), ('KT', slot, seq), ('KT', slot, seq, 2)],
                            key=('KTs', slot, seq))
                        dma('sp', VV[slot][:, t0:t0 + 33, :], Vd[seq][h], (), [('VV', slot, 'p'), ('VV', slot, seq)],
                            key=('VVs', slot, seq))

            def mla_post(ob, nq, tq, h):
                def post():
                    os_ = ucnt[0] % 2
                    ucnt[0] += 1
                    recip(ot[os_][0:nq, 0:1], PS[ob][0:nq, 64:65], [pk(ob)], [('ot', os_)])
                    ts('dve', OA[0:nq, tq, h * 64:(h + 1) * 64], PS[ob][0:nq, 0:64], ot[os_][0:nq, 0:1], None,
                       ALU.mult, None, [pk(ob), ('ot', os_), ('OA', 'z')], [('OA', tq, h)])
                return post

            def diff_post(ob, nq, tq, h):
                def post():
                    os_ = ucnt[0] % 2
                    ucnt[0] += 1
                    recip(ot[os_][0:nq, 0:1], PS[6][0:nq, 64:65], [pk(6)], [('ot', os_, 0)])
                    recip(ot[os_][0:nq, 1:2], PS[7][0:nq, 64:65], [pk(7)], [('ot', os_, 1)])
                    tt('dve', ot[os_][0:nq, 2:3], ot[os_][0:nq, 1:2], neglam[0:nq, :], ALU.mult,
                       [('ot', os_, 1), 'neglam'], [('ot', os_, 2)])
                    ts('dve', ot1[os_][0:nq, :], PS[6][0:nq, 0:64], ot[os_][0:nq, 0:1], None, ALU.mult, None,
                       [pk(6), ('ot', os_, 0)], [('ot1', os_)])
                    stt('dve', OB[0:nq, tq, h * 64:(h + 1) * 64], PS[7][0:nq, 0:64], ot[os_][0:nq, 2:3],
                        ot1[os_][0:nq, :], ALU.mult, ALU.add, [pk(7), ('ot', os_, 2), ('ot1', os_), ('OB', 'z')],
                        [('OB', tq, h)])
                return post

            def mem_post(ob, nq, tq, h):
                def post():
                    os_ = ucnt[0] % 2
                    ucnt[0] += 1
                    recip(ot[os_][0:nq, 0:1], PS[ob][0:nq, 128:129], [pk(ob)], [('ot', os_)])
                    ts('dve', OC[0:nq, tq, h * 128:(h + 1) * 128], PS[ob][0:nq, 0:128], ot[os_][0:nq, 0:1], None,
                       ALU.mult, None, [pk(ob), ('ot', os_), ('OC', 'z')], [('OC', tq, h)])
                return post

            ocnt = [0]

            def run_units(kind, h, hb, units, slot, sample):
                QTh = QT[hb]
                for (seq, tq, nq, nkt, nkl, masked) in units:
                    ob = 6 + (ocnt[0] % 2)
                    ocnt[0] += 1
                    if sample:
                        c0, t0 = SREG[seq]
                        res_in = [('KT', slot, seq), ('KT', slot, seq, 2), ('VV', slot, seq), ('QT', hb)]
                    else:
                        c0, t0 = 0, 0
                        res_in = [('KT', slot, 'p'), ('KT', slot, 'p2'), ('VV', slot, 'p'), ('QT', hb)]
                    if kind == 'm':
                        attn_unit(lambda kt, nk, slot=slot, c0=c0: KT[slot][0:96, c0 + kt * 128:c0 + kt * 128 + nk],
                                  QTh[0:96, tq * 128:tq * 128 + nq],
                                  lambda kt, nk, slot=slot, t0=t0: VV[slot][0:nk, t0 + kt, 0:65],
                                  nkt, nkl, nq, MLA_SCALE, masked, ob, res_in, mla_post(ob, nq, tq, h))
                    else:
                        attn_unit_d(slot, c0, t0, QTh, tq, nkt, nkl, nq, masked, ob, res_in,
                                    diff_post(ob, nq, tq, h))

            def attn_unit_d(slot_kv, c0, t0, QTh, tq, nkt, nk_last, nq, masked, obank, res_in, postfn):
                ngr = (nkt + 3) // 4
                for gi in range(ngr):
                    k0 = gi * 4
                    kn = min(4, nkt - k0)
                    pr = dcnt[0] % 3
                    dcnt[0] += 1
                    nks = [nk_last if (k0 + i == nkt - 1) else 128 for i in range(kn)]

                    def sfn(k0=k0, kn=kn, pr=pr, nks=nks, gi=gi):
                        lst = []
                        for i in range(kn):
                            for c in range(2):
                                kc0 = c0 + (k0 + i) * 128
                                lst.append((PS[2 * pr + c][0:nks[i], i * 128:i * 128 + nq],
                                            KT[slot_kv][32 * c:32 * c + 32, kc0:kc0 + nks[i]],
                                            QTh[32 * c:32 * c + 32, tq * 128:tq * 128 + nq], True, True))
                        mms(lst, res_in, [pk(2 * pr), pk(2 * pr + 1)])
                        for c in range(2):
                            sl = 2 * pr + c
                            if all(n == 128 for n in nks):
                                act(PT[sl][:, 0:kn, 0:nq],
                                    PS[sl][:, 0:kn * 128].rearrange("p (a b) -> p a b", b=128)[:, :, 0:nq],
                                    AF.Exp, [pk(sl)], [('PT', sl)], scale=DIFF_SCALE)
                            else:
                                for i in range(kn):
                                    act(PT[sl][0:nks[i], i, 0:nq], PS[sl][0:nks[i], i * 128:i * 128 + nq], AF.Exp,
                                        [pk(sl)], [('PT', sl)], scale=DIFF_SCALE)
                            if masked and gi >= ngr - 2:
                                r0 = (gi - (ngr - 2)) * 4
                                tt('dve', PT[sl][:, :, :], PT[sl][:, :, :], MK[:, r0:r0 + 4, :], ALU.mult,
                                   [('PT', sl), 'MK'], [('PT', sl)])

                    def pvfn(k0=k0, kn=kn, pr=pr, nks=nks):
                        lst = []
                        for c in range(2):
                            for i in range(kn):
                                kt = k0 + i
                                lst.append((PS[6 + c][0:nq, 0:65], PT[2 * pr + c][0:nks[i], i, 0:nq],
                                            VV[slot_kv][0:nks[i], t0 + kt, 0:65], kt == 0, kt == nkt - 1))
                        mms(lst, [('PT', 2 * pr), ('PT', 2 * pr + 1)] + list(res_in), [pk(6), pk(7)])
                    push_task(sfn, pvfn, postfn if gi == ngr - 1 else None, la=2)

            prompt_units = [(0, j, 128, 8 * j + 8, 128, True) for j in range(16)]
            sample_units = [(1 + b, 16 + b, 32, 33, 32, False) for b in range(2)]

            load_prompt(0)
            for i, (kind, h) in enumerate(heads):
                hb = i % 2
                src = QTm if kind == 'm' else QTd
                nr = 96 if kind == 'm' else 64
                dma('sp', QT[hb][0:nr, :], src[h], (), [('QT', hb)], key=('QT', hb))
                drain()
                load_sample(i)
                run_units(kind, h, hb, sample_units, (i + 1) % 2, True)
                if i + 1 < len(heads):
                    drain()
                    load_prompt(i + 1)
                run_units(kind, h, hb, prompt_units, i % 2, False)

            for h in range(4):
                hb = h % 2
                QTh = QT[hb]
                dma('sp', QTh[:, :], QTc[h], (), [('QT', hb)], key=('QT', hb))
                for (seq, tq, nq, nkt, nkl, masked) in prompt_units + sample_units:
                    ob = 6 + (ocnt[0] % 2)
                    ocnt[0] += 1
                    res_in = [('KTc', seq, 0), ('KTc', seq, 1), ('Vc', seq, 0), ('Vc', seq, 1), ('QT', hb)]
                    attn_unit(lambda kt, nk, seq=seq, h=h: KTc[:, seq, h, kt * 128:kt * 128 + nk],
                              QTh[:, tq * 128:tq * 128 + nq],
                              lambda kt, nk, seq=seq, h=h: Vc[0:nk, seq, kt, h, 0:129],
                              2, 128, nq, MEM_SCALE, False, ob, res_in, mem_post(ob, nq, tq, h), pair=True)
            drain()
            S.barrier()

        with contextlib.ExitStack() as ph:
            wo = [sb(ph, f"wo_{i}", [128, 4, D], BF16) for i in range(3)]
            wout = sb(ph, "wout", [128, 8, D], BF16)
            gsub_bc = sb(ph, "gsub_bc", [128, 512], F32)
            gpm_bc = sb(ph, "gpm_bc", [128, D], F32)
            B = {
                'x32': [sb(ph, f"dx32_{i}", [128, D], F32) for i in range(2)],
                'st': [sb(ph, f"dst_{i}", [128, 32], F32) for i in range(2)],
            }
            Gt = [sb(ph, f"Gt_{i}", [128, 3072], F32) for i in range(2)]
            OBn2 = [sb(ph, f"OBn_{i}", [128, 512], BF16) for i in range(2)]
            obf2 = [sb(ph, f"obf_{i}", [128, 512], F32) for i in range(2)]
            OT2 = [sb(ph, f"OT_{i}", [128, 12, 128], BF16) for i in range(2)]
            M = sb(ph, "M", [128, D], F32)
            Mt = sb(ph, "Mt", [128, D], F32)
            Mb = sb(ph, "Mb", [128, D], BF16)
            MT = sb(ph, "MT", [128, 8, 128], BF16)
            X1 = [sb(ph, f"X1_{i}", [128, D], F32) for i in range(2)]
            stg = [sb(ph, f"dstg{i}", [128, 1024], F32) for i in range(4)]

            for i, wsrc in enumerate((w_oa, w_ob, w_oc)):
                load_w(stg, wo[i], wsrc, 4, None, None, f'wo{i}')
            load_w(stg, wout, w_out, 8, None, None, 'wout')
            dma('sp', gpm_bc[:], g_pm[0:1, :].partition_broadcast(128), (), ['gpm_bc'], key='gpm_bc')
            for hh in range(8):
                dma('sp', gsub_bc[:, hh * 64:(hh + 1) * 64], g_sub[0:1, :].partition_broadcast(128), (),
                    [('gsub_bc', hh)], key='gsub_bc')
            ts('dve', gsub_bc[:], gsub_bc[:], 1.0 - LAM_INIT, None, ALU.mult, None,
               [('gsub_bc', hh) for hh in range(8)], ['gsub'])

            def D_X(tq):
                s = tq % 2
                dma('sp', B['x32'][s][:], x_own[tq * 128:(tq + 1) * 128, :], (), [('x32', s)], key=('x32', s))
                dma('sp', Gt[s][:], Gs[tq * 128:(tq + 1) * 128, :], (), [('G', s)], key=('G', s))
                st_ = B['st'][s]
                obf_ = obf2[s]
                OBn_ = OBn2[s]
                tt('dve', obf_[:], OB[:, tq, :], OB[:, tq, :], ALU.mult, [], [('obf', s)])
                treduce(st_[:, 8:16], obf_[:].rearrange("p (h e) -> p h e", e=64), [('obf', s)], [('st', s, 8)])
                act(st_[:, 16:24], st_[:, 8:16], AF.Sqrt, [('st', s, 8), 'eps'], [('st', s, 16)], scale=1.0 / 64,
                    bias=eps_t[:])
                recip(st_[:, 24:32], st_[:, 16:24], [('st', s, 16)], [('st', s, 24)])
                tt('dve', obf_[:].rearrange("p (h e) -> p h e", e=64), OB[:, tq, :].rearrange("p (h e) -> p h e", e=64),
                   st_[:, 24:32].unsqueeze(2).to_broadcast([128, 8, 64]), ALU.mult, [('st', s, 24), ('obf', s)],
                   [('obf', s)])
                tt('dve', OBn_[:], obf_[:], gsub_bc[:], ALU.mult, [('obf', s), 'gsub'], [('OBn', s)])
                lst = [(PSb[3][:, i, :], OA[:, tq, i * 128:(i + 1) * 128], ident[:]) for i in range(4)]
                lst += [(PSb[3][:, 4 + i, :], OBn_[:, i * 128:(i + 1) * 128], ident[:]) for i in range(4)]
                trs(lst, [('OBn', s)], [pk(3)])
                trs([(PSb[4][:, i, :], OC[:, tq, i * 128:(i + 1) * 128], ident[:]) for i in range(4)], [], [pk(4)])
                cp('dve', OT2[s][:, 0:8, :], PSb[3][:], [pk(3)], [('OT', s, 0)])
                cp('act', OT2[s][:, 8:12, :], PSb[4][:, 0:4, :], [pk(4)], [('OT', s, 1)])

            def D_Y(tq):
                s = tq % 2
                G = Gt[s]
                st_ = B['st'][s]
                OT = OT2[s]
                for br in range(3):
                    for hf in range(2):
                        bk = 5 + hf
                        mms([(PS[bk][:], OT[:, 4 * br + kc, :], wo[br][:, kc, hf * 512:(hf + 1) * 512], kc == 0, kc == 3)
                             for kc in range(4)], [('OT', s, 0), ('OT', s, 1)] + wres(f'wo{br}', 4), [pk(bk)])
                        gs = G[:, br * 1024 + hf * 512:br * 1024 + (hf + 1) * 512]
                        ms = M[:, hf * 512:(hf + 1) * 512]
                        if br == 0:
                            tt('dve', ms, PS[bk][:], gs, ALU.mult, [pk(bk), ('G', s)], [('M', hf)])
                        else:
                            mt_ = Mt[:, hf * 512:(hf + 1) * 512]
                            tt('dve', mt_, PS[bk][:], gs, ALU.mult, [pk(bk), ('G', s)], [('Mt', hf)])
                            tt('pool', ms, ms, mt_, ALU.add, [('M', hf), ('Mt', hf)], [('M', hf)])
                cp('act', Mb[:], M[:], [('M', 0), ('M', 1)], ['Mb'])
                trs([(PSb[0][:, kc, :], Mb[:, kc * 128:(kc + 1) * 128], ident[:]) for kc in range(8)], ['Mb'], [pk(0)])
                cp('dve', MT[:], PSb[0][:], [pk(0)], ['MT'])
                for hf in range(2):
                    bk = 1 + hf
                    mms([(PS[bk][:], MT[:, kc, :], wout[:, kc, hf * 512:(hf + 1) * 512], kc == 0, kc == 7)
                         for kc in range(8)], ['MT'] + wres('wout', 8), [pk(bk)])
                    act(junk[:, hf * 512:(hf + 1) * 512], PS[bk][:], AF.Square, [pk(bk)], ['junk', ('st', s, 3 + hf)],
                        accum_out=st_[:, 3 + hf:4 + hf])
                tt('dve', st_[:, 5:6], st_[:, 3:4], st_[:, 4:5], ALU.add, [('st', s, 3), ('st', s, 4)], [('st', s, 5)])
                rstd_from(st_[:, 5:6], st_[:, 6:7], st_[:, 7:8], 1.0 / D, [('st', s, 5)], ('st', s, 6), ('st', s, 7))
                for hf in range(2):
                    bk = 1 + hf
                    cs_ = slice(hf * 512, (hf + 1) * 512)
                    stt('dve', Mt[:, cs_], PS[bk][:], st_[:, 7:8], gpm_bc[:, cs_], ALU.mult, ALU.mult,
                        [pk(bk), ('st', s, 7), 'gpm_bc'], [('Mt', hf)])
                    tt('pool', X1[s][:, cs_], Mt[:, cs_], B['x32'][s][:, cs_], ALU.add, [('Mt', hf), ('x32', s)],
                       [('X1', s, hf)])
                dma('sp', X1s[tq * 128:(tq + 1) * 128, :], X1[s][:], [('X1', s, 0), ('X1', s, 1)], (), key=('X1_st', s))

            D_X(0)
            for tq in range(NOWN):
                if tq + 1 < NOWN:
                    D_X(tq + 1)
                D_Y(tq)
            S.barrier()

        mid.__exit__(None, None, None)
        with contextlib.ExitStack() as ph:
            wup = sb(ph, "wup", [128, 8, 4096], BF16)
            wdn = sb(ph, "wdn", [128, 32, D], BF16)
            gpost_bc = sb(ph, "gpost_bc", [128, D], F32)
            X1 = [sb(ph, f"eX1_{i}", [128, D], F32) for i in range(2)]
            X1b = [sb(ph, f"eX1b_{i}", [128, D], BF16) for i in range(2)]
            hT = sb(ph, "hT", [128, 8, 256], BF16)
            U2T = sb(ph, "U2T", [128, 32, 256], BF16)
            ur = [sb(ph, f"ur_{i}", [128, 256], F32) for i in range(2)]
            Y = [sb(ph, f"Y_{i}", [128, D], F32) for i in range(2)]
            Yt = sb(ph, "Yt", [128, D], F32)
            st = [sb(ph, f"est_{i}", [128, 16], F32) for i in range(2)]
            stg = [sb(ph, f"estg{i}", [128, 1024], F32) for i in range(4)]
            for i4 in range(4):
                load_w(stg, wup[:, :, i4 * 1024:(i4 + 1) * 1024], w_up[:, i4 * 1024:(i4 + 1) * 1024], 8, gc_mlp,
                       ('gc', 'mlp'), f'wup{i4}')
            load_w(stg, wdn, w_dn, 32, None, None, 'wdn')
            WUP = wres('wup0', 8) + wres('wup1', 8) + wres('wup2', 8) + wres('wup3', 8)
            dma('sp', gpost_bc[:], g_post[0:1, :].partition_broadcast(128), (), ['gpost_bc'], key='gpost_bc')
            urc = [0]
            for sp_ in range(NOWN // 2):
                for i in range(2):
                    tq = sp_ * 2 + i
                    dma('sp', X1[i][:], X1s[tq * 128:(tq + 1) * 128, :], (), [('X1', i)], key=('X1', i))
                    dma('pool', X1b[i][:], X1s[tq * 128:(tq + 1) * 128, :], (), [('X1b', i)], key=('X1b', i))
                    act(junk[:], X1[i][:], AF.Square, [('X1', i)], ['junk', ('st', i, 0)], accum_out=st[i][:, 0:1])
                    rstd_from(st[i][:, 0:1], st[i][:, 1:2], st[i][:, 2:3], 1.0 / D, [('st', i, 0)], ('st', i, 1),
                              ('st', i, 2))
                    trs([(PSb[0][:, kc, :], X1b[i][:, kc * 128:(kc + 1) * 128], ident[:]) for kc in range(8)],
                        [('X1b', i)], [pk(0)])
                    cp('dve', hT[:, :, i * 128:(i + 1) * 128], PSb[0][:], [pk(0)], [('hT', i)])
                for fc in range(32):
                    bk = 1 + fc % 3
                    mms([(PS[bk][:, 0:256], wup[:, kc, fc * 128:(fc + 1) * 128], hT[:, kc, :], kc == 0, kc == 7)
                         for kc in range(8)], [('hT', 0), ('hT', 1)] + WUP, [pk(bk)])
                    ui = urc[0] % 2
                    urc[0] += 1
                    act(ur[ui][:], PS[bk][:, 0:256], AF.Relu, [pk(bk)], [('ur', ui)])
                    tt('pool' if fc % 2 else 'dve', U2T[:, fc, :], ur[ui][:], ur[ui][:], ALU.mult, [('ur', ui)],
                       [('U2T', fc)])
                for i in range(2):
                    tq = sp_ * 2 + i
                    for hf in range(2):
                        bk = 4 + 2 * i + hf
                        mms([(PS[bk][:], U2T[:, fc, i * 128:(i + 1) * 128], wdn[:, fc, hf * 512:(hf + 1) * 512],
                              fc == 0, fc == 31) for fc in range(32)],
                            [('U2T', fc) for fc in range(32)] + wres('wdn', 32), [pk(bk)])
                        act(junk[:, hf * 512:(hf + 1) * 512], PS[bk][:], AF.Square, [pk(bk)],
                            ['junk', ('st', i, 3 + hf)], accum_out=st[i][:, 3 + hf:4 + hf])
                    s_ = st[i]
                    tt('dve', s_[:, 5:6], s_[:, 3:4], s_[:, 4:5], ALU.add, [('st', i, 3), ('st', i, 4)], [('st', i, 5)])
                    tt('dve', s_[:, 6:7], s_[:, 2:3], s_[:, 2:3], ALU.mult, [('st', i, 2)], [('st', i, 6)])
                    tt('dve', s_[:, 7:8], s_[:, 6:7], s_[:, 6:7], ALU.mult, [('st', i, 6)], [('st', i, 7)])
                    tt('dve', s_[:, 8:9], s_[:, 7:8], s_[:, 5:6], ALU.mult, [('st', i, 7), ('st', i, 5)], [('st', i, 8)])
                    rstd_from(s_[:, 8:9], s_[:, 9:10], s_[:, 10:11], 1.0 / D, [('st', i, 8)], ('st', i, 9), ('st', i, 10))
                    tt('dve', s_[:, 11:12], s_[:, 10:11], s_[:, 6:7], ALU.mult, [('st', i, 10), ('st', i, 6)],
                       [('st', i, 11)])
                    for hf in range(2):
                        bk = 4 + 2 * i + hf
                        cs_ = slice(hf * 512, (hf + 1) * 512)
                        stt('dve', Yt[:, cs_], PS[bk][:], s_[:, 11:12], gpost_bc[:, cs_], ALU.mult, ALU.mult,
                            [pk(bk), ('st', i, 11), 'gpost_bc'], [('Yt', hf)])
                        tt('pool', Y[i][:, cs_], Yt[:, cs_], X1[i][:, cs_], ALU.add, [('Yt', hf), ('X1', i)],
                           [('Y', i, hf)])
                    dma('sp', y_own[tq * 128:(tq + 1) * 128, :], Y[i][:], [('Y', i, 0), ('Y', i, 1)], (),
                        key=('Y_st', i))
        S.emit(nc, es)
    return nc


_NC_CACHE = {}


def _rope_tables(pos):
    pos = np.asarray(pos, dtype=np.float32)
    out = np.zeros((pos.shape[0], 40), dtype=np.float32)
    invm = np.power(np.float32(10000.0), -np.arange(16, dtype=np.float32) * np.float32(2.0 / 32)).astype(np.float32)
    invd = np.power(np.float32(500000.0), -np.arange(4, dtype=np.float32) * np.float32(2.0 / 8)).astype(np.float32)
    am = pos[:, None] * invm[None, :]
    ad = pos[:, None] * invd[None, :]
    out[:, 0:16] = np.cos(am)
    out[:, 16:32] = np.sin(am)
    out[:, 32:36] = np.cos(ad)
    out[:, 36:40] = np.sin(ad)
    return out


def kernel(x_prompt, x_sample, cache_mla_ckv, cache_mla_krope, cache_diff_k, cache_diff_v,
           cache_mem_k, cache_mem_v, mem_prompt, pre_mix_g, w_in, mla_q_norm_g, mla_w_uq,
           mla_kv_norm_g, mla_w_uk, mla_w_uv, diff_lq1, diff_lk1, diff_lq2, diff_lk2,
           diff_subln_g, mem_norm_g, w_mem_k, w_mem_v, w_o_mla, w_o_diff, w_o_mem, w_gate,
           b_gate, w_out, post_mix_g, pre_mlp_g, w_mlp_up, w_mlp_down, post_mlp_g):
    f = lambda a: np.ascontiguousarray(np.asarray(a, dtype=np.float32))
    if 'nc' not in _NC_CACHE:
        _NC_CACHE['nc'] = build_program()
    nc = _NC_CACHE['nc']
    xp = f(x_prompt)[0]
    xs = f(x_sample)
    cs_all = _rope_tables(np.arange(T))
    shared = {
        "x_all": xp, "cs_all": cs_all, "mem_p": f(mem_prompt)[0],
        "w_in": f(w_in)[0], "w_uq": f(mla_w_uq)[0], "w_uk": f(mla_w_uk)[0].reshape(256, 512),
        "w_uv": f(mla_w_uv)[0].reshape(256, 512), "w_mk": f(w_mem_k)[0], "w_mv": f(w_mem_v)[0],
        "w_oa": f(w_o_mla)[0], "w_ob": f(w_o_diff)[0], "w_oc": f(w_o_mem)[0], "w_gate": f(w_gate)[0],
        "b_gate": f(b_gate), "w_out": f(w_out)[0], "w_up": f(w_mlp_up)[0], "w_dn": f(w_mlp_down)[0],
        "g_pre": f(pre_mix_g), "g_q": f(mla_q_norm_g), "g_kv": f(mla_kv_norm_g), "g_sub": f(diff_subln_g),
        "g_mem": f(mem_norm_g), "g_pm": f(post_mix_g), "g_mlp": f(pre_mlp_g), "g_post": f(post_mlp_g),
        "lam_in": np.concatenate([f(diff_lq1), f(diff_lk1), f(diff_lq2), f(diff_lk2)], axis=1),
    }
    in_maps = []
    kk = np.arange(128)[:, None] // 64
    qq = np.arange(128)[None, :] // 64
    diag = (kk <= qq).astype(np.float32)
    for c in range(8):
        blocks = [8 * j + c for j in range(16)]
        x_own = np.zeros((NTOK, D), np.float32)
        pos_own = np.zeros((NTOK,), np.float32)
        for j, b in enumerate(blocks):
            x_own[j * 128:(j + 1) * 128] = xp[b * 128:(b + 1) * 128]
            pos_own[j * 128:(j + 1) * 128] = np.arange(b * 128, (b + 1) * 128)
        for b in range(2):
            x_own[(16 + b) * 128:(16 + b) * 128 + 32] = xs[2 * c + b]
            pos_own[(16 + b) * 128:(16 + b) * 128 + 32] = PAST + np.arange(32)
        mask = np.zeros((128, 8, 128), np.float32)
        for r in range(8):
            if r < c:
                mask[:, r, :] = 1.0
            elif r == c:
                mask[:, r, :] = diag
        m = dict(shared)
        m.update({
            "x_own": x_own, "cs_own": _rope_tables(pos_own), "maskd": mask.reshape(128, 1024),
            "c_ckv": f(cache_mla_ckv)[0, 2 * c:2 * c + 2], "c_kr": f(cache_mla_krope)[0, 2 * c:2 * c + 2],
            "c_dk": f(cache_diff_k)[0, 2 * c:2 * c + 2].reshape(2, PAST, 512),
            "c_dv": f(cache_diff_v)[0, 2 * c:2 * c + 2].reshape(2, PAST, 512),
            "c_mk": f(cache_mem_k)[0, 2 * c:2 * c + 2].reshape(2, 256, 512),
            "c_mv": f(cache_mem_v)[0, 2 * c:2 * c + 2].reshape(2, 256, 512),
        })
        in_maps.append({k: np.ascontiguousarray(v) for k, v in m.items()})
    res = run_bass_kernel_spmd(nc, in_maps, core_ids=list(range(8)))
    R = res.results
    y_p = np.zeros((1, T, D), np.float32)
    y_s = np.zeros((16, 32, D), np.float32)
    s_ckv = np.zeros((1, 16, 32, 256), np.float32)
    s_kr = np.zeros((1, 16, 32, 32), np.float32)
    s_dk = np.zeros((1, 16, 32, 8, 64), np.float32)
    s_dv = np.zeros((1, 16, 32, 8, 64), np.float32)
    for c in range(8):
        yo = R[c]["y_own"]
        for j in range(16):
            b = 8 * j + c
            y_p[0, b * 128:(b + 1) * 128] = yo[j * 128:(j + 1) * 128]
        for b in range(2):
            y_s[2 * c + b] = yo[(16 + b) * 128:(16 + b) * 128 + 32]
            s_ckv[0, 2 * c + b] = R[c]["o_s_ckv"][b * 128:b * 128 + 32]
            s_kr[0, 2 * c + b] = R[c]["o_s_kr"][b * 128:b * 128 + 32]
            s_dk[0, 2 * c + b] = R[c]["o_s_dk"][b * 128:b * 128 + 32].reshape(32, 8, 64)
            s_dv[0, 2 * c + b] = R[c]["o_s_dv"][b * 128:b * 128 + 32].reshape(32, 8, 64)
    p_ckv = R[0]["o_ckv"].reshape(1, 1, T, 256)
    p_kr = R[0]["o_kr"].reshape(1, 1, T, 32)
    p_dk = R[0]["o_dk"].reshape(1, 1, T, 8, 64)
    p_dv = R[0]["o_dv"].reshape(1, 1, T, 8, 64)
    p_mk = R[0]["o_mk"].reshape(1, 1, 256, 4, 128)
    p_mv = R[0]["o_mv"].reshape(1, 1, 256, 4, 128)
    return (y_p, y_s, p_ckv, p_kr, p_dk, p_dv, p_mk, p_mv, s_ckv, s_kr, s_dk, s_dv)
```

```python
import contextlib
import math
import numpy as np
import concourse.bass as bass
import concourse.mybir as mybir
from concourse.bass_utils import run_bass_kernel_spmd

F32 = mybir.dt.float32
BF16 = mybir.dt.bfloat16
ALU = mybir.AluOpType
AF = mybir.ActivationFunctionType
AX = mybir.AxisListType

ENGS = ('pe', 'act', 'dve', 'pool', 'sp')
SEM_CHUNK = 20000

D = 1024
T = 16384
NT = 128
NOWN = 18
NTOK = NOWN * 128
PAST = 4096
LS = 4224
EPS = 1e-6
MLA_SCALE = 96 ** -0.5
DIFF_SCALE = 32 ** -0.5
MEM_SCALE = 128 ** -0.5
LAM_INIT = 0.8 - 0.6 * math.exp(0.0)
VW = 80


class Op:
    __slots__ = ('eng', 'fn', 'deps', 'idx', 'signal', 'tok', 'dma_key', 'is_dma', 'is_bar')


class Sched:
    def __init__(self):
        self.ops = {e: [] for e in ENGS}
        self.lastw = {}
        self.readers = {}
        self.dma_counts = {}
        self.live_dma = {}

    def add(self, eng, fn, reads=(), writes=(), dma_key=None, extra_deps=()):
        op = Op()
        op.eng = eng
        op.fn = fn
        op.signal = False
        op.tok = None
        op.dma_key = dma_key
        op.is_dma = dma_key is not None
        op.is_bar = False
        deps = []
        seen = set()

        def push(d):
            if d is not None and id(d) not in seen:
                seen.add(id(d))
                deps.append(d)
        for d in extra_deps:
            push(d)
        for r in reads:
            push(self.lastw.get(r))
        for w_ in writes:
            push(self.lastw.get(w_))
            for rd in self.readers.get(w_, ()):
                push(rd)
        op.deps = deps
        for r in reads:
            self.readers.setdefault(r, []).append(op)
        for w_ in writes:
            self.lastw[w_] = op
            self.readers[w_] = []
        op.idx = len(self.ops[eng])
        self.ops[eng].append(op)
        if op.is_dma:
            c = self.dma_counts.get(dma_key, 0) + 1
            self.dma_counts[dma_key] = c
            op.tok = (('dma', dma_key), 16 * c)
            self.live_dma[dma_key] = op
        return op

    def barrier(self):
        last = []
        for e in ENGS:
            for op in reversed(self.ops[e]):
                if not op.is_dma and not op.is_bar:
                    last.append(op)
                    break
        dmas = list(self.live_dma.values())
        self.live_dma = {}
        self.lastw = {}
        self.readers = {}
        for e in ENGS:
            self.add(e, lambda eng: None, extra_deps=last + dmas).is_bar = True

    def emit(self, nc, es):
        for e in ENGS:
            for op in self.ops[e]:
                for d in op.deps:
                    if d.is_dma:
                        continue
                    if d.eng == op.eng and not op.is_dma and e == 'pe':
                        continue
                    d.signal = True
        nsem = {}
        for e in ENGS:
            c = 0
            for op in self.ops[e]:
                if op.is_dma:
                    continue
                if op.signal:
                    op.tok = (('eng', e, c // SEM_CHUNK), c % SEM_CHUNK + 1)
                    c += 1
            nsem[e] = (c + SEM_CHUNK - 1) // SEM_CHUNK
        sems = {}
        for e in ENGS:
            for k in range(nsem[e]):
                sems[('eng', e, k)] = es.enter_context(nc.semaphore(f"s_{e}_{k}"))
        for i, key in enumerate(self.dma_counts):
            sems[('dma', key)] = es.enter_context(nc.semaphore(f"d_{i}"))
        self.n_sems = len(sems)
        block = es.enter_context(nc.Block())

        def run(e, eng):
            waited = {}
            for op in self.ops[e]:
                need = {}
                for d in op.deps:
                    if not d.is_dma and d.eng == e and not op.is_dma and e == 'pe':
                        continue
                    sk, v = d.tok
                    if waited.get(sk, 0) >= v:
                        continue
                    if need.get(sk, 0) < v:
                        need[sk] = v
                for sk, v in need.items():
                    eng.wait_ge(sems[sk], v)
                    waited[sk] = v
                    if sk[0] == 'eng':
                        for k in range(sk[2]):
                            waited[('eng', sk[1], k)] = SEM_CHUNK
                ins = op.fn(eng)
                if op.is_dma:
                    ins.then_inc(sems[op.tok[0]], 16)
                elif op.signal:
                    ins.then_inc(sems[op.tok[0]], 1)
            if e == 'sp':
                for key, c in self.dma_counts.items():
                    sk = ('dma', key)
                    if waited.get(sk, 0) < 16 * c:
                        eng.wait_ge(sems[sk], 16 * c)

        @block.tensor
        def _(eng):
            run('pe', eng)

        @block.scalar
        def _(eng):
            run('act', eng)

        @block.vector
        def _(eng):
            run('dve', eng)

        @block.gpsimd
        def _(eng):
            run('pool', eng)

        @block.sync
        def _(eng):
            run('sp', eng)


def build_program():
    nc = bass.Bass("TRN2", target_bir_lowering=False)
    S = Sched()

    def din(name, shape):
        return nc.dram_tensor(name, list(shape), F32, kind="ExternalInput").ap()

    def dout(name, shape):
        return nc.dram_tensor(name, list(shape), F32, kind="ExternalOutput").ap()

    def dscr(name, shape, dt=BF16):
        return nc.dram_tensor(name, list(shape), dt).ap()

    x_all = din("x_all", [T, D])
    x_own = din("x_own", [NTOK, D])
    cs_all = din("cs_all", [T, 40])
    cs_own = din("cs_own", [NTOK, 40])
    maskd = din("maskd", [128, 8 * 128])
    c_ckv = din("c_ckv", [2, PAST, 256])
    c_kr = din("c_kr", [2, PAST, 32])
    c_dk = din("c_dk", [2, PAST, 512])
    c_dv = din("c_dv", [2, PAST, 512])
    c_mk = din("c_mk", [2, 256, 512])
    c_mv = din("c_mv", [2, 256, 512])
    mem_p = din("mem_p", [256, D])
    w_in = din("w_in", [D, 2720])
    w_uq = din("w_uq", [384, 768])
    w_uk = din("w_uk", [256, 512])
    w_uv = din("w_uv", [256, 512])
    w_mk = din("w_mk", [D, 512])
    w_mv = din("w_mv", [D, 512])
    w_oa = din("w_oa", [512, D])
    w_ob = din("w_ob", [512, D])
    w_oc = din("w_oc", [512, D])
    w_gate = din("w_gate", [D, 3072])
    b_gate = din("b_gate", [1, 3072])
    w_out = din("w_out", [D, D])
    w_up = din("w_up", [D, 4096])
    w_dn = din("w_dn", [4096, D])
    g_pre = din("g_pre", [1, D])
    g_q = din("g_q", [1, 384])
    g_kv = din("g_kv", [1, 256])
    g_sub = din("g_sub", [1, 64])
    g_mem = din("g_mem", [1, D])
    g_pm = din("g_pm", [1, D])
    g_mlp = din("g_mlp", [1, D])
    g_post = din("g_post", [1, D])
    lam_in = din("lam_in", [1, 128])

    y_own = dout("y_own", [NTOK, D])
    o_ckv = dout("o_ckv", [T, 256])
    o_kr = dout("o_kr", [T, 32])
    o_dk = dout("o_dk", [T, 512])
    o_dv = dout("o_dv", [T, 512])
    o_mk = dout("o_mk", [256, 512])
    o_mv = dout("o_mv", [256, 512])
    o_s_ckv = dout("o_s_ckv", [256, 256])
    o_s_kr = dout("o_s_kr", [256, 32])
    o_s_dk = dout("o_s_dk", [256, 512])
    o_s_dv = dout("o_s_dv", [256, 512])

    Ls = [T, LS, LS]
    KTm = [dscr(f"KTm{i}", [8, 64, L]) for i, L in enumerate(Ls)]
    KRT = [dscr(f"KRT{i}", [32, L]) for i, L in enumerate(Ls)]
    KTd = [dscr(f"KTd{i}", [8, 64, L]) for i, L in enumerate(Ls)]
    Vm = [dscr(f"Vm{i}", [8, 128, L // 128, VW]) for i, L in enumerate(Ls)]
    Vd = [dscr(f"Vd{i}", [8, 128, L // 128, VW]) for i, L in enumerate(Ls)]
    QTm = dscr("QTm", [8, 96, NTOK])
    QTd = dscr("QTd", [8, 64, NTOK])
    QTc = dscr("QTc", [4, 128, NTOK])
    X1s = dscr("X1s", [NTOK, D], F32)

    es = contextlib.ExitStack()
    with es:
        def sb(stack, name, shape, dt):
            return stack.enter_context(nc.sbuf_tensor(name, list(shape), dt))

        def A(eng, fn, r=(), w=(), key=None):
            return S.add(eng, fn, reads=r, writes=w, dma_key=key)

        def dma(q, out, in_, r=(), w=(), key=None, slow=False):
            if slow:
                return A(q, lambda e: e.dma_start(out=out, in_=in_, allow_slow_non_contiguous=True), r, w, key)
            return A(q, lambda e: e.dma_start(out=out, in_=in_), r, w, key)

        def act(out, in_, func, r, w, **kw):
            return A('act', lambda e: e.activation(out=out, in_=in_, func=func, **kw), r, w)

        def tt(eng, out, in0, in1, op, r, w):
            return A(eng, lambda e: e.tensor_tensor(out=out, in0=in0, in1=in1, op=op), r, w)

        def ts(eng, out, in0, s1, s2, op0, op1, r, w):
            if op1 is None:
                return A(eng, lambda e: e.tensor_scalar(out=out, in0=in0, scalar1=s1, scalar2=None, op0=op0), r, w)
            return A(eng, lambda e: e.tensor_scalar(out=out, in0=in0, scalar1=s1, scalar2=s2, op0=op0, op1=op1), r, w)

        def stt(eng, out, in0, scalar, in1, op0, op1, r, w):
            return A(eng, lambda e: e.scalar_tensor_tensor(out=out, in0=in0, scalar=scalar, in1=in1,
                                                           op0=op0, op1=op1), r, w)

        def cp(eng, out, in_, r, w):
            if eng == 'act':
                return A('act', lambda e: e.activation(out=out, in_=in_, func=AF.Copy), r, w)
            return A(eng, lambda e: e.tensor_copy(out=out, in_=in_), r, w)

        def mms(lst, r, w):
            def fn(e):
                ins = None
                for (o, l, rh, st, sp) in lst:
                    ins = e.matmul(o, lhsT=l, rhs=rh, start=st, stop=sp)
                return ins
            return A('pe', fn, r, w)

        def trs(lst, r, w):
            def fn(e):
                ins = None
                for (o, i, idn) in lst:
                    ins = e.transpose(out=o, in_=i, identity=idn)
                return ins
            return A('pe', fn, list(r) + ['ident'], w)

        def memset(eng, ap, val, w):
            return A(eng, lambda e: e.memset(ap, val), (), w)

        def treduce(out, in_, r, w):
            return A('dve', lambda e: e.tensor_reduce(out=out, in_=in_, axis=AX.X, op=ALU.add), r, w)

        def recip(out, in_, r, w):
            return A('dve', lambda e: e.reciprocal(out=out, in_=in_), r, w)

        PS = [es.enter_context(nc.psum_tensor(f"ps{i}", [128, 512], F32)) for i in range(8)]
        PSb = [p[:].bitcast(BF16).rearrange("p (a b) -> p a b", b=128) for p in PS]

        def pk(i):
            return ('ps', i)

        ident = sb(es, "ident", [128, 128], BF16)
        identf = sb(es, "identf", [128, 128], F32)
        eps_t = sb(es, "eps_t", [128, 1], F32)
        gc_pre = sb(es, "gc_pre", [128, 8], F32)
        gc_q = sb(es, "gc_q", [128, 3], F32)
        gc_mem = sb(es, "gc_mem", [128, 8], F32)
        gc_mlp = sb(es, "gc_mlp", [128, 8], F32)
        lam_t = sb(es, "lam_t", [128, 128], F32)
        lam_s = sb(es, "lam_s", [128, 8], F32)
        junk = sb(es, "junk", [128, 1024], F32)

        def mk_ident(e):
            e.memset(identf[:], 0.0)
            return e.affine_select(out=identf[:], in_=identf[:], pattern=[[-1, 128]], compare_op=ALU.not_equal,
                                   fill=1.0, base=0, channel_multiplier=1)
        A('pool', mk_ident, (), ['identf'])
        cp('dve', ident[:], identf[:], ['identf'], ['ident'])
        memset('dve', eps_t[:], EPS, ['eps'])
        for nm, gt, gd, n in (("pre", gc_pre, g_pre, 8), ("q", gc_q, g_q, 3), ("mem", gc_mem, g_mem, 8),
                              ("mlp", gc_mlp, g_mlp, 8)):
            dma('sp', gt[:], gd.rearrange("o (k p) -> p (o k)", p=128), (), [('gc', nm)], key=('gc', nm), slow=True)
        dma('sp', lam_t[:], lam_in[0:1, :].partition_broadcast(128), (), ['lam_t'], key='lam_t')
        tt('dve', lam_t[:, 0:32], lam_t[:, 0:32], lam_t[:, 32:64], ALU.mult, ['lam_t'], ['lam_a'])
        tt('dve', lam_t[:, 64:96], lam_t[:, 64:96], lam_t[:, 96:128], ALU.mult, ['lam_t'], ['lam_b'])
        A('dve', lambda e: e.tensor_reduce(out=lam_s[:, 0:1], in_=lam_t[:, 0:32], axis=AX.X, op=ALU.add),
          ['lam_a'], ['lam0'])
        A('dve', lambda e: e.tensor_reduce(out=lam_s[:, 1:2], in_=lam_t[:, 64:96], axis=AX.X, op=ALU.add),
          ['lam_b'], ['lam1'])
        act(lam_s[:, 2:4], lam_s[:, 0:2], AF.Exp, ['lam0', 'lam1'], ['lam2'])
        stt('dve', lam_s[:, 4:5], lam_s[:, 3:4], -LAM_INIT, lam_s[:, 2:3], ALU.add, ALU.subtract, ['lam2'], ['neglam'])
        neglam = lam_s[:, 4:5]

        wl_cnt = [0]

        def load_w(stack_stage, dst, src, nk, gcol=None, gres=None, res=None):
            cols = src.shape[1]
            for kc in range(nk):
                rows = src[kc * 128:(kc + 1) * 128, :]
                if gcol is None:
                    dma('pool', dst[:, kc, :], rows, (), [(res, kc)], key=('wl', res))
                else:
                    i = wl_cnt[0] % len(stack_stage)
                    wl_cnt[0] += 1
                    stg_ = stack_stage[i]
                    dma('sp', stg_[:, 0:cols], rows, (), [('stg', i)], key=('stg', i))
                    if wl_cnt[0] % 2:
                        act(dst[:, kc, :], stg_[:, 0:cols], AF.Copy, [('stg', i), gres], [(res, kc)],
                            scale=gcol[:, kc:kc + 1])
                    else:
                        ts('dve', dst[:, kc, :], stg_[:, 0:cols], gcol[:, kc:kc + 1], None, ALU.mult, None,
                           [('stg', i), gres], [(res, kc)])

        def wres(res, nk):
            return [(res, kc) for kc in range(nk)]

        def rstd_from(ssq_ap, tmp_ap, out_ap, scale, r, wtmp, wout):
            act(tmp_ap, ssq_ap, AF.Sqrt, list(r) + ['eps'], [wtmp], scale=scale, bias=eps_t[:])
            recip(out_ap, tmp_ap, [wtmp], [wout])

        def x_front(B, s, xsrc, psb=0):
            dma('sp', B['x32'][s][:], xsrc, (), [('x32', s)], key=('x32', s))
            dma('pool', B['xb'][s][:], xsrc, (), [('xb', s)], key=('xb', s))
            st_ = B['st'][s]
            act(junk[:], B['x32'][s][:], AF.Square, [('x32', s)], ['junk', ('st', s, 0)], accum_out=st_[:, 0:1])
            rstd_from(st_[:, 0:1], st_[:, 1:2], st_[:, 2:3], 1.0 / D, [('st', s, 0)], ('st', s, 1), ('st', s, 2))
            trs([(PSb[psb][:, kc, :], B['xb'][s][:, kc * 128:(kc + 1) * 128], ident[:]) for kc in range(8)],
                [('xb', s)], [pk(psb)])
            cp('dve', B['xT'][s][:], PSb[psb][:], [pk(psb)], [('xT', s)])

        def rope(view, x1s, x2s, cosb, sinb, tmp, rr, ww, shp):
            t1, t2, t3, t4 = (tmp[:, i, :].rearrange("p (g h) -> p g h", h=shp[1])[:, 0:shp[0], :] for i in range(4))
            x1 = view[:, :, x1s]
            x2 = view[:, :, x2s]
            tt('dve', t1, x1, cosb, ALU.mult, rr, ['rt1'])
            tt('dve', t2, x2, sinb, ALU.mult, rr, ['rt2'])
            tt('dve', t3, x2, cosb, ALU.mult, rr, ['rt3'])
            tt('dve', t4, x1, sinb, ALU.mult, rr, ['rt4'])
            tt('dve', x1, t1, t2, ALU.subtract, ['rt1', 'rt2', 'rt4'] + list(rr), ww)
            tt('dve', x2, t3, t4, ALU.add, ['rt3', 'rt4'] + list(ww), ww)

        with contextlib.ExitStack() as ph:
            wkv = sb(ph, "wkv", [128, 8, 1312], BF16)
            wuk = sb(ph, "wuk", [128, 2, 512], BF16)
            wuv = sb(ph, "wuv", [128, 2, 512], BF16)
            gkv_bc = sb(ph, "gkv_bc", [128, 256], F32)
            B = {
                'xT': [sb(ph, f"axT_{i}", [128, 8, 128], BF16) for i in range(3)],
                'st': [sb(ph, f"ast_{i}", [128, 8], F32) for i in range(3)],
            }
            X4 = [sb(ph, f"ax32_{i}", [128, D], F32) for i in range(4)]
            XB4 = [sb(ph, f"axb_{i}", [128, D], BF16) for i in range(4)]
            R = [sb(ph, f"R_{i}", [128, 1312], F32) for i in range(3)]
            CS = [sb(ph, f"CS_{i}", [128, 40], F32) for i in range(4)]
            Rb = [sb(ph, f"Rb_{i}", [128, 1312], BF16) for i in range(2)]
            TK = [sb(ph, f"TK_{i}", [128, 7, 128], BF16) for i in range(2)]
            KNs = [sb(ph, f"KNs_{i}", [128, 4, 512], BF16) for i in range(2)]
            DKs = [sb(ph, f"DKs_{i}", [128, 4, 512], BF16) for i in range(2)]
            KRs = [sb(ph, f"KRs_{i}", [32, 512], BF16) for i in range(2)]
            VMs = [sb(ph, f"VMs_{i}", [128, 8, 4, VW], BF16) for i in range(2)]
            VDs = [sb(ph, f"VDs_{i}", [128, 8, 4, VW], BF16) for i in range(2)]
            rtmp = sb(ph, "rtmp", [128, 4, 128], F32)
            stg = [sb(ph, f"astg{i}", [128, 1024], F32) for i in range(4)]

            load_w(stg, wkv[:, :, 0:288], w_in[:, 384:672], 8, gc_pre, ('gc', 'pre'), 'wkva')
            load_w(stg, wkv[:, :, 288:1312], w_in[:, 1184:2208], 8, gc_pre, ('gc', 'pre'), 'wkvb')
            load_w(stg, wuk, w_uk, 2, None, None, 'wuk')
            load_w(stg, wuv, w_uv, 2, None, None, 'wuv')
            WKV = wres('wkva', 8) + wres('wkvb', 8)
            dma('sp', gkv_bc[:], g_kv[0:1, :].partition_broadcast(128), (), ['gkv_bc'], key='gkv_bc')
            for i in range(2):
                memset('pool', VMs[i][:], 0.0, [('VMs', i)])
                memset('pool', VMs[i][:, :, :, 64:65], 1.0, [('VMs', i)])
                memset('pool', VDs[i][:], 0.0, [('VDs', i)])
                memset('pool', VDs[i][:, :, :, 64:65], 1.0, [('VDs', i)])

            NA = NT + 2

            def a_src(k):
                if k < NT:
                    return x_all[k * 128:(k + 1) * 128, :], cs_all[k * 128:(k + 1) * 128, :]
                tq = 16 + (k - NT)
                return x_own[tq * 128:(tq + 1) * 128, :], cs_own[tq * 128:(tq + 1) * 128, :]

            def A_load(k):
                s4 = k % 4
                xsrc, cssrc = a_src(k)
                dma('sp', X4[s4][:], xsrc, (), [('x32', s4)], key=('x32', s4))
                dma('sp', CS[s4][:], cssrc, (), [('CS', s4)], key=('CS', s4))

            def A_F1(k):
                s4 = k % 4
                s = k % 3
                st_ = B['st'][s]
                act(junk[:], X4[s4][:], AF.Square, [('x32', s4)], ['junk', ('st', s, 0)], accum_out=st_[:, 0:1])
                rstd_from(st_[:, 0:1], st_[:, 1:2], st_[:, 2:3], 1.0 / D, [('st', s, 0)], ('st', s, 1), ('st', s, 2))
                cp('dve', XB4[s4][:], X4[s4][:], [('x32', s4)], [('xb', s4)])

            def A_T(k):
                s4 = k % 4
                s = k % 3
                psb = 0 if k % 2 == 0 else 7
                trs([(PSb[psb][:, kc, :], XB4[s4][:, kc * 128:(kc + 1) * 128], ident[:]) for kc in range(8)],
                    [('xb', s4)], [pk(psb)])
                cp('dve', B['xT'][s][:], PSb[psb][:], [pk(psb)], [('xT', s)])

            def A_proj(k):
                s = k % 3
                c4 = k % 4
                xT = B['xT'][s]
                st_ = B['st'][s]
                for bi, (c0, c1) in enumerate(((0, 512), (512, 1024), (1024, 1312))):
                    mms([(PS[1 + bi][:, 0:c1 - c0], xT[:, kc, :], wkv[:, kc, c0:c1], kc == 0, kc == 7)
                         for kc in range(8)], [('xT', s)] + WKV, [pk(1 + bi)])
                    act(R[s][:, c0:c1], PS[1 + bi][:, 0:c1 - c0], AF.Copy, [pk(1 + bi), ('st', s, 2)],
                        [('R', s, bi)], scale=st_[:, 2:3])
                act(junk[:, 0:256], R[s][:, 0:256], AF.Square, [('R', s, 0)], ['junk', ('st', s, 3)],
                    accum_out=st_[:, 3:4])
                rstd_from(st_[:, 3:4], st_[:, 4:5], st_[:, 5:6], 1.0 / 256, [('st', s, 3)], ('st', s, 4), ('st', s, 5))
                stt('dve', R[s][:, 0:256], R[s][:, 0:256], st_[:, 5:6], gkv_bc[:], ALU.mult, ALU.mult,
                    [('R', s, 0), ('st', s, 5), 'gkv_bc'], [('R', s, 'ckv')])
                krv = R[s][:, 256:288].rearrange("p (g d) -> p g d", g=1)
                rope(krv, slice(0, 16), slice(16, 32), CS[c4][:, 0:16].unsqueeze(1), CS[c4][:, 16:32].unsqueeze(1),
                     rtmp, [('R', s, 0), ('CS', c4)], [('R', s, 'kr')], (1, 16))
                dkv = R[s][:, 288:800].rearrange("p (g d) -> p g d", d=32)
                rope(dkv, slice(0, 4), slice(4, 8), CS[c4][:, 32:36].unsqueeze(1).to_broadcast([128, 16, 4]),
                     CS[c4][:, 36:40].unsqueeze(1).to_broadcast([128, 16, 4]),
                     rtmp, [('R', s, 0), ('R', s, 1), ('CS', c4)], [('R', s, 'dk')], (16, 4))
                key = ('R_st', s)
                if k < NT:
                    rs = slice(k * 128, (k + 1) * 128)
                    dsts = (o_ckv[rs, :], o_kr[rs, :], o_dk[rs, :], o_dv[rs, :])
                else:
                    rs = slice((k - NT) * 128, (k - NT + 1) * 128)
                    dsts = (o_s_ckv[rs, :], o_s_kr[rs, :], o_s_dk[rs, :], o_s_dv[rs, :])
                for dst, (c0, c1) in zip(dsts, ((0, 256), (256, 288), (288, 800), (800, 1312))):
                    dma('sp', dst, R[s][:, c0:c1], r_all(s), (), key=key)

            def r_all(s):
                return [('R', s, 0), ('R', s, 1), ('R', s, 2), ('R', s, 'ckv'), ('R', s, 'kr'), ('R', s, 'dk')]

            def kv_S1(s, g, u, s2):
                cp('act', Rb[s2][:, 0:800], R[s][:, 0:800], r_all(s), [('Rb', s2)])
                cp('pool', VDs[g][:, :, u, 0:64], R[s][:, 800:1312].rearrange("p (h e) -> p h e", e=64),
                   r_all(s) + [('VDs', g)], [('VDs', g, u)])
                lst = [(PSb[4][:, i, :], Rb[s2][:, i * 128:(i + 1) * 128], ident[:]) for i in range(2)]
                lst += [(PSb[4][:, 2 + i, :], Rb[s2][:, 288 + i * 128:288 + (i + 1) * 128], ident[:]) for i in range(4)]
                lst += [(PSb[4][0:32, 6, :], Rb[s2][:, 256:288], ident[:])]
                trs(lst, [('Rb', s2)], [pk(4)])
                cp('dve', TK[s2][:, 0:2, :], PSb[4][:, 0:2, :], [pk(4)], [('TK', s2, 0)])
                cp('dve', DKs[g][:, :, u * 128:(u + 1) * 128], PSb[4][:, 2:6, :], [pk(4)], [('DKs', g, u)])
                cp('dve', KRs[g][:, u * 128:(u + 1) * 128], PSb[4][0:32, 6, :], [pk(4)], [('KRs', g, u)])

            def kv_S2(s, g, u, s2):
                lst = []
                for ch in range(4):
                    for kc in range(2):
                        lst.append((PS[5][:, ch * 128:(ch + 1) * 128], wuk[:, kc, ch * 128:(ch + 1) * 128],
                                    TK[s2][:, kc, :], kc == 0, kc == 1))
                mms(lst, [('TK', s2, 0)] + wres('wuk', 2), [pk(5)])
                act(KNs[g][:, :, u * 128:(u + 1) * 128], PS[5][:].rearrange("p (c t) -> p c t", t=128), AF.Copy,
                    [pk(5)], [('KNs', g, u)])
                mms([(PS[6][:], TK[s2][:, kc, :], wuv[:, kc, :], kc == 0, kc == 1) for kc in range(2)],
                    [('TK', s2, 0)] + wres('wuv', 2), [pk(6)])
                act(VMs[g][:, :, u, 0:64], PS[6][:].rearrange("p (h e) -> p h e", e=64), AF.Copy,
                    [pk(6), ('VMs', g)], [('VMs', g, u)])

            def stage_res(g, nu):
                r = []
                for u in range(nu):
                    r += [('DKs', g, u), ('KRs', g, u), ('KNs', g, u), ('VMs', g, u), ('VDs', g, u)]
                return r

            def flush_stage(g, seq, t0, kt0):
                key = ('stg_st', g)
                rr = stage_res(g, 4)
                for hh in range(2):
                    dst = KTm[seq].rearrange("(ch hh) n t -> hh n ch t", hh=2)[hh][:, :, t0:t0 + 512]
                    dma('sp', dst, KNs[g][hh * 64:(hh + 1) * 64, :, :], rr, (), key=key)
                    dst = KTd[seq].rearrange("(ch hh) n t -> hh n ch t", hh=2)[hh][:, :, t0:t0 + 512]
                    dma('sp', dst, DKs[g][hh * 64:(hh + 1) * 64, :, :], rr, (), key=key)
                dma('sp', KRT[seq][:, t0:t0 + 512], KRs[g][:, :], rr, (), key=key)
                dma('sp', Vm[seq][:, :, kt0:kt0 + 4, :].rearrange("h p k e -> p h k e"), VMs[g][:], rr, (), key=key)
                dma('sp', Vd[seq][:, :, kt0:kt0 + 4, :].rearrange("h p k e -> p h k e"), VDs[g][:], rr, (), key=key)

            def st_args(k):
                if k < NT:
                    return (k % 3, (k // 4) % 2, k % 4, k % 2)
                return (k % 3, 0, k - NT, k % 2)
            A_load(0)
            A_load(1)
            A_F1(0)
            for it in range(NA + 2):
                if it + 2 < NA:
                    A_load(it + 2)
                if it + 1 < NA:
                    A_F1(it + 1)
                if it < NA:
                    A_T(it)
                if it >= 2:
                    kv_S1(*st_args(it - 2))
                if 1 <= it < NA + 1:
                    A_proj(it - 1)
                if it >= 2:
                    k = it - 2
                    kv_S2(*st_args(k))
                    if k < NT and k % 4 == 3:
                        flush_stage((k // 4) % 2, 0, (k - 3) * 128, k - 3)
            rr = stage_res(0, 2)
            key = ('stg_st', 0)
            for b in range(2):
                seq = 1 + b
                for hh in range(2):
                    dst = KTm[seq].rearrange("(ch hh) n t -> hh n ch t", hh=2)[hh][:, :, PAST:PAST + 32]
                    dma('sp', dst, KNs[0][hh * 64:(hh + 1) * 64, :, b * 128:b * 128 + 32], rr, (), key=key, slow=True)
                    dst = KTd[seq].rearrange("(ch hh) n t -> hh n ch t", hh=2)[hh][:, :, PAST:PAST + 32]
                    dma('sp', dst, DKs[0][hh * 64:(hh + 1) * 64, :, b * 128:b * 128 + 32], rr, (), key=key, slow=True)
                dma('sp', KRT[seq][:, PAST:PAST + 32], KRs[0][:, b * 128:b * 128 + 32], rr, (), key=key, slow=True)
                dma('sp', Vm[seq][:, 0:32, 32, :].rearrange("h p e -> p h e"), VMs[0][0:32, :, b, :], rr, (), key=key)
                dma('sp', Vd[seq][:, 0:32, 32, :].rearrange("h p e -> p h e"), VDs[0][0:32, :, b, :], rr, (), key=key)
            def C_load(k):
                b, t = divmod(k, 32)
                s = k % 3
                rs = slice(t * 128, (t + 1) * 128)
                dma('sp', R[s][:, 0:256], c_ckv[b, rs, :], (), [('R', s, 'ckv'), ('R', s, 0)], key=('Rl', s))
                dma('sp', R[s][:, 256:288], c_kr[b, rs, :], (), [('R', s, 'kr')], key=('Rl', s))
                dma('sp', R[s][:, 288:800], c_dk[b, rs, :], (), [('R', s, 'dk'), ('R', s, 1)], key=('Rl', s))
                dma('sp', R[s][:, 800:1312], c_dv[b, rs, :], (), [('R', s, 2)], key=('Rl', s))
            def c_args(k):
                return (k % 3, 1 - ((k // 4) % 2), k % 4, k % 2)
            C_load(0)
            C_load(1)
            for it in range(64 + 1):
                if it + 2 < 64:
                    C_load(it + 2)
                if it < 64:
                    kv_S1(*c_args(it))
                if it >= 1:
                    k = it - 1
                    b, t = divmod(k, 32)
                    kv_S2(*c_args(k))
                    if k % 4 == 3:
                        flush_stage(1 - ((k // 4) % 2), 1 + b, (t - 3) * 128, t - 3)
            S.barrier()

        mid = contextlib.ExitStack()
        mid.__enter__()
        KTc = sb(mid, "KTc", [128, 3, 4, 256], BF16)
        Vc = sb(mid, "Vc", [128, 3, 2, 4, 132], BF16)
        Gs = dscr("Gs", [NTOK, 3072], F32)
        memset('pool', Vc[:], 0.0, ['Vc0'])
        memset('pool', Vc[:, :, :, :, 128:129], 1.0, ['Vc0'])
        with contextlib.ExitStack() as ph:
            wq = sb(ph, "wq", [128, 8, 1408], BF16)
            wuq = sb(ph, "wuq", [128, 3, 768], BF16)
            wmk = sb(ph, "wmk", [128, 8, 512], BF16)
            wmv = sb(ph, "wmv", [128, 8, 512], BF16)
            B = {
                'x32': [sb(ph, f"bx32_{i}", [128, D], F32) for i in range(2)],
                'xb': [sb(ph, f"bxb_{i}", [128, D], BF16) for i in range(2)],
                'xT': [sb(ph, f"bxT_{i}", [128, 8, 128], BF16) for i in range(2)],
                'st': [sb(ph, f"bst_{i}", [128, 12], F32) for i in range(2)],
            }
            CS = [sb(ph, f"bCS_{i}", [128, 40], F32) for i in range(2)]
            Q1 = [sb(ph, f"Q1_{i}", [128, 512], F32) for i in range(2)]
            MQb = [sb(ph, f"MQb_{i}", [128, 512], BF16) for i in range(2)]
            DQb = [sb(ph, f"DQb_{i}", [128, 512], BF16) for i in range(2)]
            CQT = [sb(ph, f"CQT_{i}", [128, 3, 128], BF16) for i in range(2)]
            Qf = [sb(ph, f"Qf_{i}", [128, 768], F32) for i in range(2)]
            Qb = [sb(ph, f"Qb_{i}", [128, 768], BF16) for i in range(2)]
            QTa = [sb(ph, f"QTa_{i}", [128, 8, 128], BF16) for i in range(2)]
            QTb = [sb(ph, f"QTb_{i}", [128, 8, 128], BF16) for i in range(2)]
            QTe = [sb(ph, f"QTe_{i}", [128, 4, 128], BF16) for i in range(2)]
            MKf = [sb(ph, f"MKf_{i}", [128, 512], F32) for i in range(2)]
            MKb = [sb(ph, f"MKb_{i}", [128, 512], BF16) for i in range(2)]
            rtmp = sb(ph, "brtmp", [128, 4, 128], F32)
            stg = [sb(ph, f"bstg{i}", [128, 1024], F32) for i in range(4)]
            wg = sb(ph, "wg", [128, 8, 3072], BF16)
            bg_bc = sb(ph, "bg_bc", [128, 3072], F32)
            Gb = sb(ph, "Gb", [128, 3072], F32)
            for i3 in range(3):
                load_w(stg, wg[:, :, i3 * 1024:(i3 + 1) * 1024], w_gate[:, i3 * 1024:(i3 + 1) * 1024], 8, gc_pre,
                       ('gc', 'pre'), f'wg{i3}')
            WG = wres('wg0', 8) + wres('wg1', 8) + wres('wg2', 8)
            dma('sp', bg_bc[:], b_gate[0:1, :].partition_broadcast(128), (), ['bg_bc'], key='bg_bc')

            load_w(stg, wq[:, :, 0:512], w_in[:, 672:1184], 8, gc_pre, ('gc', 'pre'), 'wqa')
            load_w(stg, wq[:, :, 512:1024], w_in[:, 2208:2720], 8, gc_pre, ('gc', 'pre'), 'wqb')
            load_w(stg, wq[:, :, 1024:1408], w_in[:, 0:384], 8, gc_pre, ('gc', 'pre'), 'wqc')
            load_w(stg, wuq, w_uq, 3, gc_q, ('gc', 'q'), 'wuq')
            load_w(stg, wmk, w_mk, 8, gc_mem, ('gc', 'mem'), 'wmk')
            load_w(stg, wmv, w_mv, 8, gc_mem, ('gc', 'mem'), 'wmv')
            WQ = wres('wqa', 8) + wres('wqb', 8) + wres('wqc', 8)

            for mt in range(2):
                s = mt
                x_front(B, s, mem_p[mt * 128:(mt + 1) * 128, :], psb=0)
                st_ = B['st'][s]
                for wi, (wt, wn, od) in enumerate(((wmk, 'wmk', o_mk), (wmv, 'wmv', o_mv))):
                    mms([(PS[1 + wi][:], B['xT'][s][:, kc, :], wt[:, kc, :], kc == 0, kc == 7) for kc in range(8)],
                        [('xT', s)] + wres(wn, 8), [pk(1 + wi)])
                    i2 = (2 * mt + wi) % 2
                    act(MKf[i2][:], PS[1 + wi][:], AF.Copy, [pk(1 + wi), ('st', s, 2)], [('MKf', i2)],
                        scale=st_[:, 2:3])
                    dma('sp', od[mt * 128:(mt + 1) * 128, :], MKf[i2][:], [('MKf', i2)], (), key=('MKf_st', i2))
                    if wi == 0:
                        cp('pool', MKb[i2][:], MKf[i2][:], [('MKf', i2)], [('MKb', i2)])
                        trs([(PSb[3][:, h, :], MKb[i2][:, h * 128:(h + 1) * 128], ident[:]) for h in range(4)],
                            [('MKb', i2)], [pk(3)])
                        cp('dve', KTc[:, 0, :, mt * 128:(mt + 1) * 128], PSb[3][:, 0:4, :], [pk(3)],
                           [('KTc', 0, mt)])
                    else:
                        cp('pool', Vc[:, 0, mt, :, 0:128], MKf[i2][:].rearrange("p (h e) -> p h e", e=128),
                           [('MKf', i2), 'Vc0'], [('Vc', 0, mt)])
            for b in range(2):
                for mt in range(2):
                    i2 = mt
                    dma('sp', MKf[i2][:], c_mk[b, mt * 128:(mt + 1) * 128, :], (), [('MKf', i2)], key=('MKf', i2))
                    cp('pool', MKb[i2][:], MKf[i2][:], [('MKf', i2)], [('MKb', i2)])
                    trs([(PSb[3][:, h, :], MKb[i2][:, h * 128:(h + 1) * 128], ident[:]) for h in range(4)],
                        [('MKb', i2)], [pk(3)])
                    cp('dve', KTc[:, 1 + b, :, mt * 128:(mt + 1) * 128], PSb[3][:, 0:4, :], [pk(3)],
                       [('KTc', 1 + b, mt)])
                    dma('sp', MKf[i2][:], c_mv[b, mt * 128:(mt + 1) * 128, :], (), [('MKf', i2)], key=('MKf', i2))
                    cp('pool', Vc[:, 1 + b, mt, :, 0:128], MKf[i2][:].rearrange("p (h e) -> p h e", e=128),
                       [('MKf', i2), 'Vc0'], [('Vc', 1 + b, mt)])

            def B_F(tq):
                s = tq % 2
                x_front(B, s, x_own[tq * 128:(tq + 1) * 128, :], psb=0)
                dma('sp', CS[s][:], cs_own[tq * 128:(tq + 1) * 128, :], (), [('CS', s)], key=('CS', s))

            def B_S1(tq):
                s = tq % 2
                xT = B['xT'][s]
                st_ = B['st'][s]
                for gi in range(3):
                    for hf in range(2):
                        c0 = gi * 1024 + hf * 512
                        bk = 1 + hf
                        mms([(PS[bk][:], xT[:, kc, :], wg[:, kc, c0:c0 + 512], kc == 0, kc == 7) for kc in range(8)],
                            [('xT', s)] + WG, [pk(bk)])
                        stt('dve', Gb[:, c0:c0 + 512], PS[bk][:], st_[:, 2:3], bg_bc[:, c0:c0 + 512], ALU.mult, ALU.add,
                            [pk(bk), ('st', s, 2), 'bg_bc'], [('Gb', gi, hf)])
                        act(Gb[:, c0:c0 + 512], Gb[:, c0:c0 + 512], AF.Sigmoid, [('Gb', gi, hf)], [('Gb', gi, hf)])
                dma('sp', Gs[tq * 128:(tq + 1) * 128, :], Gb[:], [('Gb', gi, hf) for gi in range(3) for hf in range(2)],
                    (), key='Gb_st')
                for bi, (c0, c1) in enumerate(((0, 512), (512, 1024), (1024, 1408))):
                    mms([(PS[3 + bi][:, 0:c1 - c0], xT[:, kc, :], wq[:, kc, c0:c1], kc == 0, kc == 7)
                         for kc in range(8)], [('xT', s)] + WQ, [pk(3 + bi)])
                act(Q1[s][:], PS[3][:], AF.Copy, [pk(3), ('st', s, 2)], [('Q1', s)], scale=st_[:, 2:3])
                act(MQb[s][:], PS[4][:], AF.Copy, [pk(4), ('st', s, 2)], [('MQb', s)], scale=st_[:, 2:3])
                act(junk[:, 0:384], PS[5][:, 0:384], AF.Square, [pk(5), ('st', s, 2)], ['junk', ('st', s, 3)],
                    scale=st_[:, 2:3], accum_out=st_[:, 3:4])
                rstd_from(st_[:, 3:4], st_[:, 4:5], st_[:, 5:6], 1.0 / 384, [('st', s, 3)], ('st', s, 4), ('st', s, 5))
                tt('dve', st_[:, 6:7], st_[:, 5:6], st_[:, 2:3], ALU.mult, [('st', s, 5), ('st', s, 2)], [('st', s, 6)])
                lst = []
                for ch in range(3):
                    for kc in range(8):
                        lst.append((PS[6][:, ch * 128:(ch + 1) * 128], wq[:, kc, 1024 + ch * 128:1024 + (ch + 1) * 128],
                                    xT[:, kc, :], kc == 0, kc == 7))
                mms(lst, [('xT', s)] + WQ, [pk(6)])
                cp('dve', CQT[s][:], PS[6][:, 0:384].rearrange("p (c t) -> p c t", t=128), [pk(6)], [('CQT', s)])
                for bi, (c0, c1) in enumerate(((0, 512), (512, 768))):
                    mms([(PS[1 + bi][:, 0:c1 - c0], CQT[s][:, rc, :], wuq[:, rc, c0:c1], rc == 0, rc == 2)
                         for rc in range(3)], [('CQT', s)] + wres('wuq', 3), [pk(1 + bi)])
                    act(Qf[s][:, c0:c1], PS[1 + bi][:, 0:c1 - c0], AF.Copy, [pk(1 + bi), ('st', s, 6)],
                        [('Qf', s, bi)], scale=st_[:, 6:7])
                qv = Qf[s][:].rearrange("p (h d) -> p h d", d=96)
                rope(qv, slice(64, 80), slice(80, 96), CS[s][:, 0:16].unsqueeze(1).to_broadcast([128, 8, 16]),
                     CS[s][:, 16:32].unsqueeze(1).to_broadcast([128, 8, 16]), rtmp,
                     [('Qf', s, 0), ('Qf', s, 1), ('CS', s)], [('Qf', s, 'r')], (8, 16))
                cp('dve', Qb[s][:], Qf[s][:], [('Qf', s, 0), ('Qf', s, 1), ('Qf', s, 'r')], [('Qb', s)])
                dqv = Q1[s][:].rearrange("p (g d) -> p g d", d=32)
                rope(dqv, slice(0, 4), slice(4, 8), CS[s][:, 32:36].unsqueeze(1).to_broadcast([128, 16, 4]),
                     CS[s][:, 36:40].unsqueeze(1).to_broadcast([128, 16, 4]), rtmp,
                     [('Q1', s), ('CS', s)], [('Q1', s, 'r')], (16, 4))
                cp('act', DQb[s][:], Q1[s][:], [('Q1', s), ('Q1', s, 'r')], [('DQb', s)])

            def B_S2(tq):
                s = tq % 2
                trs([(PSb[7][0:96, h, :], Qb[s][:, h * 96:(h + 1) * 96], ident[:]) for h in range(8)],
                    [('Qb', s)], [pk(7)])
                cp('dve', QTa[s][0:96, :, :], PSb[7][0:96, :, :], [pk(7)], [('QTa', s)])
                dma('sp', QTm[:, :, tq * 128:(tq + 1) * 128].rearrange("h r t -> r h t"), QTa[s][0:96, :, :],
                    [('QTa', s)], (), key=('QTa_st', s), slow=True)
                trs([(PSb[7][0:64, h, :], DQb[s][:, h * 64:(h + 1) * 64], ident[:]) for h in range(8)],
                    [('DQb', s)], [pk(7)])
                cp('dve', QTb[s][0:64, :, :], PSb[7][0:64, :, :], [pk(7)], [('QTb', s)])
                dma('sp', QTd[:, :, tq * 128:(tq + 1) * 128].rearrange("h r t -> r h t"), QTb[s][0:64, :, :],
                    [('QTb', s)], (), key=('QTb_st', s), slow=True)
                trs([(PSb[7][:, h, :], MQb[s][:, h * 128:(h + 1) * 128], ident[:]) for h in range(4)],
                    [('MQb', s)], [pk(7)])
                cp('dve', QTe[s][:], PSb[7][:, 0:4, :], [pk(7)], [('QTe', s)])
                dma('sp', QTc[:, :, tq * 128:(tq + 1) * 128].rearrange("h r t -> r h t"), QTe[s][:],
                    [('QTe', s)], (), key=('QTe_st', s), slow=True)

            B_F(0)
            for it in range(NOWN + 1):
                if it + 1 < NOWN:
                    B_F(it + 1)
                if it < NOWN:
                    B_S1(it)
                if it >= 1:
                    B_S2(it - 1)
            S.barrier()

        OA = sb(mid, "OA", [128, NOWN, 512], BF16)
        OB = sb(mid, "OB", [128, NOWN, 512], BF16)
        OC = sb(mid, "OC", [128, NOWN, 512], BF16)
        for nm, o in (('OA', OA), ('OB', OB), ('OC', OC)):
            memset('pool', o[:, 16:18, :], 0.0, [(nm, 'z')])
        with contextlib.ExitStack() as ph:
            KT = [sb(ph, f"KT_{i}", [128, T], BF16) for i in range(2)]
            VV = [sb(ph, f"VV_{i}", [128, 128, VW], BF16) for i in range(2)]
            QT = [sb(ph, f"QT_{i}", [128, NTOK], BF16) for i in range(2)]
            PT = [sb(ph, f"PT_{i}", [128, 4, 128], BF16) for i in range(6)]
            MK = sb(ph, "MK", [128, 8, 128], BF16)
            mstg = sb(ph, "mstg", [128, 1024], F32)
            ot = [sb(ph, f"ot_{i}", [128, 8], F32) for i in range(2)]
            ot1 = [sb(ph, f"ot1_{i}", [128, 64], F32) for i in range(2)]
            dma('sp', mstg[:], maskd[:, :], (), ['mstg'], key='mstg')
            cp('dve', MK[:].rearrange("p a b -> p (a b)"), mstg[:], ['mstg'], ['MK'])

            pend = []
            gcnt = [0]
            dcnt = [0]
            ucnt = [0]

            def push_task(sfn, pvfn, postfn, la=4):
                sfn()
                pend.append((pvfn, postfn))
                while len(pend) > la:
                    pv, post = pend.pop(0)
                    pv()
                    if post is not None:
                        post()

            def drain():
                while pend:
                    pv, post = pend.pop(0)
                    pv()
                    if post is not None:
                        post()

            def attn_unit(kt_ap, q_ap, v_ap, nkt, nk_last, nq, scale, masked, obank, res_in, postfn, pair=False):
                ngr = (nkt + 3) // 4
                for gi in range(ngr):
                    k0 = gi * 4
                    kn = min(4, nkt - k0)
                    if pair:
                        slot = 2 * (dcnt[0] % 3)
                        dcnt[0] += 1
                    else:
                        slot = gcnt[0] % 6
                        gcnt[0] += 1
                    nks = [nk_last if (k0 + i == nkt - 1) else 128 for i in range(kn)]

                    def sfn(k0=k0, kn=kn, slot=slot, nks=nks, gi=gi):
                        mms([(PS[slot][0:nks[i], i * 128:i * 128 + nq], kt_ap(k0 + i, nks[i]), q_ap, True, True)
                             for i in range(kn)], res_in, [pk(slot)])
                        if all(n == 128 for n in nks):
                            act(PT[slot][:, 0:kn, 0:nq],
                                PS[slot][:, 0:kn * 128].rearrange("p (a b) -> p a b", b=128)[:, :, 0:nq],
                                AF.Exp, [pk(slot)], [('PT', slot)], scale=scale)
                        else:
                            for i in range(kn):
                                act(PT[slot][0:nks[i], i, 0:nq], PS[slot][0:nks[i], i * 128:i * 128 + nq], AF.Exp,
                                    [pk(slot)], [('PT', slot)], scale=scale)
                        if masked and gi >= ngr - 2:
                            r0 = (gi - (ngr - 2)) * 4
                            tt('dve', PT[slot][:, :, :], PT[slot][:, :, :], MK[:, r0:r0 + 4, :], ALU.mult,
                               [('PT', slot), 'MK'], [('PT', slot)])

                    def pvfn(k0=k0, kn=kn, slot=slot, nks=nks):
                        lst = []
                        for i in range(kn):
                            kt = k0 + i
                            va = v_ap(kt, nks[i])
                            lst.append((PS[obank][0:nq, 0:va.shape[1]], PT[slot][0:nks[i], i, 0:nq], va, kt == 0,
                                        kt == nkt - 1))
                        mms(lst, [('PT', slot)] + list(res_in), [pk(obank)])
                    push_task(sfn, pvfn, postfn if gi == ngr - 1 else None, la=2 if pair else 4)

            SREG = {1: (0, 0), 2: (8192, 64)}
            heads = [('m', h) for h in range(8)] + [('d', h) for h in range(8)]

            def load_prompt(i):
                kind, h = heads[i]
                slot = i % 2
                wk1 = [('KT', slot, 'p'), ('KT', slot, 1), ('KT', slot, 2)]
                wk2 = [('KT', slot, 'p2'), ('KT', slot, 1, 2), ('KT', slot, 2, 2)]
                wv = [('VV', slot, 'p'), ('VV', slot, 1), ('VV', slot, 2)]
                if kind == 'm':
                    dma('sp', KT[slot][0:64, 0:T], KTm[0][h], (), wk1, key=('KT', slot))
                    dma('sp', KT[slot][64:96, 0:T], KRT[0], (), wk2, key=('KT', slot))
                    dma('sp', VV[slot][:, 0:128, :], Vm[0][h], (), wv, key=('VV', slot))
                else:
                    dma('sp', KT[slot][0:64, 0:T], KTd[0][h], (), wk1 + wk2, key=('KT', slot))
                    dma('sp', VV[slot][:, 0:128, :], Vd[0][h], (), wv, key=('VV', slot))

            def load_sample(i):
                kind, h = heads[i]
                slot = (i + 1) % 2
                for seq in (1, 2):
                    c0, t0 = SREG[seq]
                    if kind == 'm':
                        dma('sp', KT[slot][0:64, c0:c0 + LS], KTm[seq][h], (), [('KT', slot, 'p'), ('KT', slot, seq)],
                            key=('KTs', slot, seq))
                        dma('sp', KT[slot][64:96, c0:c0 + LS], KRT[seq], (), [('KT', slot, 'p2'), ('KT', slot, seq, 2)],
                            key=('KTs', slot, seq))
                        dma('sp', VV[slot][:, t0:t0 + 33, :], Vm[seq][h], (), [('VV', slot, 'p'), ('VV', slot, seq)],
                            key=('VVs', slot, seq))
                    else:
                        dma('sp', KT[slot][0:64, c0:c0 + LS], KTd[seq][h], (),
                            [('KT', slot, 'p'), ('KT', slot, 'p2'), ('KT', slot, seq), ('KT', slot, seq, 2)],
                            key=('KTs', slot, seq))
                        dma('sp', VV[slot][:, t0:t0 + 33, :], Vd[seq][h], (), [('VV', slot, 'p'), ('VV', slot, seq)],
                            key=('VVs', slot, seq))

            def mla_post(ob, nq, tq, h):
                def post():
                    os_ = ucnt[0] % 2
                    ucnt[0] += 1
                    recip(ot[os_][0:nq, 0:1], PS[ob][0:nq, 64:65], [pk(ob)], [('ot', os_)])
                    ts('dve', OA[0:nq, tq, h * 64:(h + 1) * 64], PS[ob][0:nq, 0:64], ot[os_][0:nq, 0:1], None,
                       ALU.mult, None, [pk(ob), ('ot', os_), ('OA', 'z')], [('OA', tq, h)])
                return post

            def diff_post(ob, nq, tq, h):
                def post():
                    os_ = ucnt[0] % 2
                    ucnt[0] += 1
                    recip(ot[os_][0:nq, 0:1], PS[6][0:nq, 64:65], [pk(6)], [('ot', os_, 0)])
                    recip(ot[os_][0:nq, 1:2], PS[7][0:nq, 64:65], [pk(7)], [('ot', os_, 1)])
                    tt('dve', ot[os_][0:nq, 2:3], ot[os_][0:nq, 1:2], neglam[0:nq, :], ALU.mult,
                       [('ot', os_, 1), 'neglam'], [('ot', os_, 2)])
                    ts('dve', ot1[os_][0:nq, :], PS[6][0:nq, 0:64], ot[os_][0:nq, 0:1], None, ALU.mult, None,
                       [pk(6), ('ot', os_, 0)], [('ot1', os_)])
                    stt('dve', OB[0:nq, tq, h * 64:(h + 1) * 64], PS[7][0:nq, 0:64], ot[os_][0:nq, 2:3],
                        ot1[os_][0:nq, :], ALU.mult, ALU.add, [pk(7), ('ot', os_, 2), ('ot1', os_), ('OB', 'z')],
                        [('OB', tq, h)])
                return post

            def mem_post(ob, nq, tq, h):
                def post():
                    os_ = ucnt[0] % 2
                    ucnt[0] += 1
                    recip(ot[os_][0:nq, 0:1], PS[ob][0:nq, 128:129], [pk(ob)], [('ot', os_)])
                    ts('dve', OC[0:nq, tq, h * 128:(h + 1) * 128], PS[ob][0:nq, 0:128], ot[os_][0:nq, 0:1], None,
                       ALU.mult, None, [pk(ob), ('ot', os_), ('OC', 'z')], [('OC', tq, h)])
                return post

            ocnt = [0]

            def run_units(kind, h, hb, units, slot, sample):
                QTh = QT[hb]
                for (seq, tq, nq, nkt, nkl, masked) in units:
                    ob = 6 + (ocnt[0] % 2)
                    ocnt[0] += 1
                    if sample:
                        c0, t0 = SREG[seq]
                        res_in = [('KT', slot, seq), ('KT', slot, seq, 2), ('VV', slot, seq), ('QT', hb)]
                    else:
                        c0, t0 = 0, 0
                        res_in = [('KT', slot, 'p'), ('KT', slot, 'p2'), ('VV', slot, 'p'), ('QT', hb)]
                    if kind == 'm':
                        attn_unit(lambda kt, nk, slot=slot, c0=c0: KT[slot][0:96, c0 + kt * 128:c0 + kt * 128 + nk],
                                  QTh[0:96, tq * 128:tq * 128 + nq],
                                  lambda kt, nk, slot=slot, t0=t0: VV[slot][0:nk, t0 + kt, 0:65],
                                  nkt, nkl, nq, MLA_SCALE, masked, ob, res_in, mla_post(ob, nq, tq, h))
                    else:
                        attn_unit_d(slot, c0, t0, QTh, tq, nkt, nkl, nq, masked, ob, res_in,
                                    diff_post(ob, nq, tq, h))

            def attn_unit_d(slot_kv, c0, t0, QTh, tq, nkt, nk_last, nq, masked, obank, res_in, postfn):
                ngr = (nkt + 3) // 4
                for gi in range(ngr):
                    k0 = gi * 4
                    kn = min(4, nkt - k0)
                    pr = dcnt[0] % 3
                    dcnt[0] += 1
                    nks = [nk_last if (k0 + i == nkt - 1) else 128 for i in range(kn)]

                    def sfn(k0=k0, kn=kn, pr=pr, nks=nks, gi=gi):
                        lst = []
                        for i in range(kn):
                            for c in range(2):
                                kc0 = c0 + (k0 + i) * 128
                                lst.append((PS[2 * pr + c][0:nks[i], i * 128:i * 128 + nq],
                                            KT[slot_kv][32 * c:32 * c + 32, kc0:kc0 + nks[i]],
                                            QTh[32 * c:32 * c + 32, tq * 128:tq * 128 + nq], True, True))
                        mms(lst, res_in, [pk(2 * pr), pk(2 * pr + 1)])
                        for c in range(2):
                            sl = 2 * pr + c
                            if all(n == 128 for n in nks):
                                act(PT[sl][:, 0:kn, 0:nq],
                                    PS[sl][:, 0:kn * 128].rearrange("p (a b) -> p a b", b=128)[:, :, 0:nq],
                                    AF.Exp, [pk(sl)], [('PT', sl)], scale=DIFF_SCALE)
                            else:
                                for i in range(kn):
                                    act(PT[sl][0:nks[i], i, 0:nq], PS[sl][0:nks[i], i * 128:i * 128 + nq], AF.Exp,
                                        [pk(sl)], [('PT', sl)], scale=DIFF_SCALE)
                            if masked and gi >= ngr - 2:
                                r0 = (gi - (ngr - 2)) * 4
                                tt('dve', PT[sl][:, :, :], PT[sl][:, :, :], MK[:, r0:r0 + 4, :], ALU.mult,
                                   [('PT', sl), 'MK'], [('PT', sl)])

                    def pvfn(k0=k0, kn=kn, pr=pr, nks=nks):
                        lst = []
                        for c in range(2):
                            for i in range(kn):
                                kt = k0 + i
                                lst.append((PS[6 + c][0:nq, 0:65], PT[2 * pr + c][0:nks[i], i, 0:nq],
                                            VV[slot_kv][0:nks[i], t0 + kt, 0:65], kt == 0, kt == nkt - 1))
                        mms(lst, [('PT', 2 * pr), ('PT', 2 * pr + 1)] + list(res_in), [pk(6), pk(7)])
                    push_task(sfn, pvfn, postfn if gi == ngr - 1 else None, la=2)

            prompt_units = [(0, j, 128, 8 * j + 8, 128, True) for j in range(16)]
            sample_units = [(1 + b, 16 + b, 32, 33, 32, False) for b in range(2)]

            for i, (kind, h) in enumerate(heads):
                hb = i % 2
                src = QTm if kind == 'm' else QTd
                nr = 96 if kind == 'm' else 64
                dma('sp', QT[hb][0:nr, :], src[h], (), [('QT', hb)], key=('QT', hb))
                load_sample(i)
                run_units(kind, h, hb, sample_units, (i + 1) % 2, True)
            drain()
            load_prompt(0)
            for i, (kind, h) in enumerate(heads):
                hb = i % 2
                src = QTm if kind == 'm' else QTd
                nr = 96 if kind == 'm' else 64
                dma('sp', QT[hb][0:nr, :], src[h], (), [('QT', hb)], key=('QT', hb))
                if i + 1 < len(heads):
                    drain()
                    load_prompt(i + 1)
                run_units(kind, h, hb, prompt_units, i % 2, False)

            for h in range(4):
                hb = h % 2
                QTh = QT[hb]
                dma('sp', QTh[:, :], QTc[h], (), [('QT', hb)], key=('QT', hb))
                for (seq, tq, nq, nkt, nkl, masked) in prompt_units + sample_units:
                    ob = 6 + (ocnt[0] % 2)
                    ocnt[0] += 1
                    res_in = [('KTc', seq, 0), ('KTc', seq, 1), ('Vc', seq, 0), ('Vc', seq, 1), ('QT', hb)]
                    attn_unit(lambda kt, nk, seq=seq, h=h: KTc[:, seq, h, kt * 128:kt * 128 + nk],
                              QTh[:, tq * 128:tq * 128 + nq],
                              lambda kt, nk, seq=seq, h=h: Vc[0:nk, seq, kt, h, 0:129],
                              2, 128, nq, MEM_SCALE, False, ob, res_in, mem_post(ob, nq, tq, h), pair=True)
            drain()
            S.barrier()

        with contextlib.ExitStack() as ph:
            wo = [sb(ph, f"wo_{i}", [128, 4, D], BF16) for i in range(3)]
            wout = sb(ph, "wout", [128, 8, D], BF16)
            gsub_bc = sb(ph, "gsub_bc", [128, 512], F32)
            gpm_bc = sb(ph, "gpm_bc", [128, D], F32)
            B = {
                'x32': [sb(ph, f"dx32_{i}", [128, D], F32) for i in range(2)],
                'st': [sb(ph, f"dst_{i}", [128, 32], F32) for i in range(2)],
            }
            Gt = [sb(ph, f"Gt_{i}", [128, 3072], F32) for i in range(2)]
            OBn2 = [sb(ph, f"OBn_{i}", [128, 512], BF16) for i in range(2)]
            obf2 = [sb(ph, f"obf_{i}", [128, 512], F32) for i in range(2)]
            OT2 = [sb(ph, f"OT_{i}", [128, 12, 128], BF16) for i in range(2)]
            M = sb(ph, "M", [128, D], F32)
            Mt = sb(ph, "Mt", [128, D], F32)
            Mb = sb(ph, "Mb", [128, D], BF16)
            MT = sb(ph, "MT", [128, 8, 128], BF16)
            X1 = [sb(ph, f"X1_{i}", [128, D], F32) for i in range(2)]
            stg = [sb(ph, f"dstg{i}", [128, 1024], F32) for i in range(4)]

            for i, wsrc in enumerate((w_oa, w_ob, w_oc)):
                load_w(stg, wo[i], wsrc, 4, None, None, f'wo{i}')
            load_w(stg, wout, w_out, 8, None, None, 'wout')
            dma('sp', gpm_bc[:], g_pm[0:1, :].partition_broadcast(128), (), ['gpm_bc'], key='gpm_bc')
            for hh in range(8):
                dma('sp', gsub_bc[:, hh * 64:(hh + 1) * 64], g_sub[0:1, :].partition_broadcast(128), (),
                    [('gsub_bc', hh)], key='gsub_bc')
            ts('dve', gsub_bc[:], gsub_bc[:], 1.0 - LAM_INIT, None, ALU.mult, None,
               [('gsub_bc', hh) for hh in range(8)], ['gsub'])

            def D_X(tq):
                s = tq % 2
                dma('sp', B['x32'][s][:], x_own[tq * 128:(tq + 1) * 128, :], (), [('x32', s)], key=('x32', s))
                dma('sp', Gt[s][:], Gs[tq * 128:(tq + 1) * 128, :], (), [('G', s)], key=('G', s))
                st_ = B['st'][s]
                obf_ = obf2[s]
                OBn_ = OBn2[s]
                tt('dve', obf_[:], OB[:, tq, :], OB[:, tq, :], ALU.mult, [], [('obf', s)])
                treduce(st_[:, 8:16], obf_[:].rearrange("p (h e) -> p h e", e=64), [('obf', s)], [('st', s, 8)])
                act(st_[:, 16:24], st_[:, 8:16], AF.Sqrt, [('st', s, 8), 'eps'], [('st', s, 16)], scale=1.0 / 64,
                    bias=eps_t[:])
                recip(st_[:, 24:32], st_[:, 16:24], [('st', s, 16)], [('st', s, 24)])
                tt('dve', obf_[:].rearrange("p (h e) -> p h e", e=64), OB[:, tq, :].rearrange("p (h e) -> p h e", e=64),
                   st_[:, 24:32].unsqueeze(2).to_broadcast([128, 8, 64]), ALU.mult, [('st', s, 24), ('obf', s)],
                   [('obf', s)])
                tt('dve', OBn_[:], obf_[:], gsub_bc[:], ALU.mult, [('obf', s), 'gsub'], [('OBn', s)])
                lst = [(PSb[3][:, i, :], OA[:, tq, i * 128:(i + 1) * 128], ident[:]) for i in range(4)]
                lst += [(PSb[3][:, 4 + i, :], OBn_[:, i * 128:(i + 1) * 128], ident[:]) for i in range(4)]
                trs(lst, [('OBn', s)], [pk(3)])
                trs([(PSb[4][:, i, :], OC[:, tq, i * 128:(i + 1) * 128], ident[:]) for i in range(4)], [], [pk(4)])
                cp('dve', OT2[s][:, 0:8, :], PSb[3][:], [pk(3)], [('OT', s, 0)])
                cp('act', OT2[s][:, 8:12, :], PSb[4][:, 0:4, :], [pk(4)], [('OT', s, 1)])

            def D_Y(tq):
                s = tq % 2
                G = Gt[s]
                st_ = B['st'][s]
                OT = OT2[s]
                for br in range(3):
                    for hf in range(2):
                        bk = 5 + hf
                        mms([(PS[bk][:], OT[:, 4 * br + kc, :], wo[br][:, kc, hf * 512:(hf + 1) * 512], kc == 0, kc == 3)
                             for kc in range(4)], [('OT', s, 0), ('OT', s, 1)] + wres(f'wo{br}', 4), [pk(bk)])
                        gs = G[:, br * 1024 + hf * 512:br * 1024 + (hf + 1) * 512]
                        ms = M[:, hf * 512:(hf + 1) * 512]
                        if br == 0:
                            tt('dve', ms, PS[bk][:], gs, ALU.mult, [pk(bk), ('G', s)], [('M', hf)])
                        else:
                            mt_ = Mt[:, hf * 512:(hf + 1) * 512]
                            tt('dve', mt_, PS[bk][:], gs, ALU.mult, [pk(bk), ('G', s)], [('Mt', hf)])
                            tt('dve', ms, ms, mt_, ALU.add, [('M', hf), ('Mt', hf)], [('M', hf)])
                cp('act', Mb[:], M[:], [('M', 0), ('M', 1)], ['Mb'])
                trs([(PSb[0][:, kc, :], Mb[:, kc * 128:(kc + 1) * 128], ident[:]) for kc in range(8)], ['Mb'], [pk(0)])
                cp('dve', MT[:], PSb[0][:], [pk(0)], ['MT'])
                for hf in range(2):
                    bk = 1 + hf
                    mms([(PS[bk][:], MT[:, kc, :], wout[:, kc, hf * 512:(hf + 1) * 512], kc == 0, kc == 7)
                         for kc in range(8)], ['MT'] + wres('wout', 8), [pk(bk)])
                    act(junk[:, hf * 512:(hf + 1) * 512], PS[bk][:], AF.Square, [pk(bk)], ['junk', ('st', s, 3 + hf)],
                        accum_out=st_[:, 3 + hf:4 + hf])
                tt('dve', st_[:, 5:6], st_[:, 3:4], st_[:, 4:5], ALU.add, [('st', s, 3), ('st', s, 4)], [('st', s, 5)])
                rstd_from(st_[:, 5:6], st_[:, 6:7], st_[:, 7:8], 1.0 / D, [('st', s, 5)], ('st', s, 6), ('st', s, 7))
                for hf in range(2):
                    bk = 1 + hf
                    cs_ = slice(hf * 512, (hf + 1) * 512)
                    stt('dve', Mt[:, cs_], PS[bk][:], st_[:, 7:8], gpm_bc[:, cs_], ALU.mult, ALU.mult,
                        [pk(bk), ('st', s, 7), 'gpm_bc'], [('Mt', hf)])
                    tt('dve', X1[s][:, cs_], Mt[:, cs_], B['x32'][s][:, cs_], ALU.add, [('Mt', hf), ('x32', s)],
                       [('X1', s, hf)])
                dma('sp', X1s[tq * 128:(tq + 1) * 128, :], X1[s][:], [('X1', s, 0), ('X1', s, 1)], (), key=('X1_st', s))

            D_X(0)
            for tq in range(NOWN):
                if tq + 1 < NOWN:
                    D_X(tq + 1)
                D_Y(tq)
            S.barrier()

        mid.__exit__(None, None, None)
        with contextlib.ExitStack() as ph:
            wup = sb(ph, "wup", [128, 8, 4096], BF16)
            wdn = sb(ph, "wdn", [128, 32, D], BF16)
            gpost_bc = sb(ph, "gpost_bc", [128, D], F32)
            X1 = [sb(ph, f"eX1_{i}", [128, D], F32) for i in range(2)]
            X1b = [sb(ph, f"eX1b_{i}", [128, D], BF16) for i in range(2)]
            hT = sb(ph, "hT", [128, 8, 256], BF16)
            U2T = sb(ph, "U2T", [128, 32, 256], BF16)
            ur = [sb(ph, f"ur_{i}", [128, 256], F32) for i in range(2)]
            Y = [sb(ph, f"Y_{i}", [128, D], F32) for i in range(2)]
            Yt = sb(ph, "Yt", [128, D], F32)
            st = [sb(ph, f"est_{i}", [128, 16], F32) for i in range(2)]
            stg = [sb(ph, f"estg{i}", [128, 1024], F32) for i in range(4)]
            for i4 in range(4):
                load_w(stg, wup[:, :, i4 * 1024:(i4 + 1) * 1024], w_up[:, i4 * 1024:(i4 + 1) * 1024], 8, gc_mlp,
                       ('gc', 'mlp'), f'wup{i4}')
            load_w(stg, wdn, w_dn, 32, None, None, 'wdn')
            WUP = wres('wup0', 8) + wres('wup1', 8) + wres('wup2', 8) + wres('wup3', 8)
            dma('sp', gpost_bc[:], g_post[0:1, :].partition_broadcast(128), (), ['gpost_bc'], key='gpost_bc')
            urc = [0]
            for sp_ in range(NOWN // 2):
                for i in range(2):
                    tq = sp_ * 2 + i
                    dma('sp', X1[i][:], X1s[tq * 128:(tq + 1) * 128, :], (), [('X1', i)], key=('X1', i))
                    dma('pool', X1b[i][:], X1s[tq * 128:(tq + 1) * 128, :], (), [('X1b', i)], key=('X1b', i))
                    act(junk[:], X1[i][:], AF.Square, [('X1', i)], ['junk', ('st', i, 0)], accum_out=st[i][:, 0:1])
                    rstd_from(st[i][:, 0:1], st[i][:, 1:2], st[i][:, 2:3], 1.0 / D, [('st', i, 0)], ('st', i, 1),
                              ('st', i, 2))
                    trs([(PSb[0][:, kc, :], X1b[i][:, kc * 128:(kc + 1) * 128], ident[:]) for kc in range(8)],
                        [('X1b', i)], [pk(0)])
                    cp('dve', hT[:, :, i * 128:(i + 1) * 128], PSb[0][:], [pk(0)], [('hT', i)])
                for fc in range(32):
                    bk = 1 + fc % 3
                    mms([(PS[bk][:, 0:256], wup[:, kc, fc * 128:(fc + 1) * 128], hT[:, kc, :], kc == 0, kc == 7)
                         for kc in range(8)], [('hT', 0), ('hT', 1)] + WUP, [pk(bk)])
                    ui = urc[0] % 2
                    urc[0] += 1
                    act(ur[ui][:], PS[bk][:, 0:256], AF.Relu, [pk(bk)], [('ur', ui)])
                    tt('pool' if fc % 2 else 'dve', U2T[:, fc, :], ur[ui][:], ur[ui][:], ALU.mult, [('ur', ui)],
                       [('U2T', fc)])
                for i in range(2):
                    tq = sp_ * 2 + i
                    for hf in range(2):
                        bk = 4 + 2 * i + hf
                        mms([(PS[bk][:], U2T[:, fc, i * 128:(i + 1) * 128], wdn[:, fc, hf * 512:(hf + 1) * 512],
                              fc == 0, fc == 31) for fc in range(32)],
                            [('U2T', fc) for fc in range(32)] + wres('wdn', 32), [pk(bk)])
                        act(junk[:, hf * 512:(hf + 1) * 512], PS[bk][:], AF.Square, [pk(bk)],
                            ['junk', ('st', i, 3 + hf)], accum_out=st[i][:, 3 + hf:4 + hf])
                    s_ = st[i]
                    tt('dve', s_[:, 5:6], s_[:, 3:4], s_[:, 4:5], ALU.add, [('st', i, 3), ('st', i, 4)], [('st', i, 5)])
                    tt('dve', s_[:, 6:7], s_[:, 2:3], s_[:, 2:3], ALU.mult, [('st', i, 2)], [('st', i, 6)])
                    tt('dve', s_[:, 7:8], s_[:, 6:7], s_[:, 6:7], ALU.mult, [('st', i, 6)], [('st', i, 7)])
                    tt('dve', s_[:, 8:9], s_[:, 7:8], s_[:, 5:6], ALU.mult, [('st', i, 7), ('st', i, 5)], [('st', i, 8)])
                    rstd_from(s_[:, 8:9], s_[:, 9:10], s_[:, 10:11], 1.0 / D, [('st', i, 8)], ('st', i, 9), ('st', i, 10))
                    tt('dve', s_[:, 11:12], s_[:, 10:11], s_[:, 6:7], ALU.mult, [('st', i, 10), ('st', i, 6)],
                       [('st', i, 11)])
                    for hf in range(2):
                        bk = 4 + 2 * i + hf
                        cs_ = slice(hf * 512, (hf + 1) * 512)
                        stt('dve', Yt[:, cs_], PS[bk][:], s_[:, 11:12], gpost_bc[:, cs_], ALU.mult, ALU.mult,
                            [pk(bk), ('st', i, 11), 'gpost_bc'], [('Yt', hf)])
                        tt('pool', Y[i][:, cs_], Yt[:, cs_], X1[i][:, cs_], ALU.add, [('Yt', hf), ('X1', i)],
                           [('Y', i, hf)])
                    dma('sp', y_own[tq * 128:(tq + 1) * 128, :], Y[i][:], [('Y', i, 0), ('Y', i, 1)], (),
                        key=('Y_st', i))
        S.emit(nc, es)
    return nc


_NC_CACHE = {}


def _rope_tables(pos):
    pos = np.asarray(pos, dtype=np.float32)
    out = np.zeros((pos.shape[0], 40), dtype=np.float32)
    invm = np.power(np.float32(10000.0), -np.arange(16, dtype=np.float32) * np.float32(2.0 / 32)).astype(np.float32)
    invd = np.power(np.float32(500000.0), -np.arange(4, dtype=np.float32) * np.float32(2.0 / 8)).astype(np.float32)
    am = pos[:, None] * invm[None, :]
    ad = pos[:, None] * invd[None, :]
    out[:, 0:16] = np.cos(am)
    out[:, 16:32] = np.sin(am)
    out[:, 32:36] = np.cos(ad)
    out[:, 36:40] = np.sin(ad)
    return out


def kernel(x_prompt, x_sample, cache_mla_ckv, cache_mla_krope, cache_diff_k, cache_diff_v,
           cache_mem_k, cache_mem_v, mem_prompt, pre_mix_g, w_in, mla_q_norm_g, mla_w_uq,
           mla_kv_norm_g, mla_w_uk, mla_w_uv, diff_lq1, diff_lk1, diff_lq2, diff_lk2,
           diff_subln_g, mem_norm_g, w_mem_k, w_mem_v, w_o_mla, w_o_diff, w_o_mem, w_gate,
           b_gate, w_out, post_mix_g, pre_mlp_g, w_mlp_up, w_mlp_down, post_mlp_g):
    f = lambda a: np.ascontiguousarray(np.asarray(a, dtype=np.float32))
    if 'nc' not in _NC_CACHE:
        _NC_CACHE['nc'] = build_program()
    nc = _NC_CACHE['nc']
    xp = f(x_prompt)[0]
    xs = f(x_sample)
    cs_all = _rope_tables(np.arange(T))
    shared = {
        "x_all": xp, "cs_all": cs_all, "mem_p": f(mem_prompt)[0],
        "w_in": f(w_in)[0], "w_uq": f(mla_w_uq)[0], "w_uk": f(mla_w_uk)[0].reshape(256, 512),
        "w_uv": f(mla_w_uv)[0].reshape(256, 512), "w_mk": f(w_mem_k)[0], "w_mv": f(w_mem_v)[0],
        "w_oa": f(w_o_mla)[0], "w_ob": f(w_o_diff)[0], "w_oc": f(w_o_mem)[0], "w_gate": f(w_gate)[0],
        "b_gate": f(b_gate), "w_out": f(w_out)[0], "w_up": f(w_mlp_up)[0], "w_dn": f(w_mlp_down)[0],
        "g_pre": f(pre_mix_g), "g_q": f(mla_q_norm_g), "g_kv": f(mla_kv_norm_g), "g_sub": f(diff_subln_g),
        "g_mem": f(mem_norm_g), "g_pm": f(post_mix_g), "g_mlp": f(pre_mlp_g), "g_post": f(post_mlp_g),
        "lam_in": np.concatenate([f(diff_lq1), f(diff_lk1), f(diff_lq2), f(diff_lk2)], axis=1),
    }
    in_maps = []
    kk = np.arange(128)[:, None] // 64
    qq = np.arange(128)[None, :] // 64
    diag = (kk <= qq).astype(np.float32)
    for c in range(8):
        blocks = [8 * j + c for j in range(16)]
        x_own = np.zeros((NTOK, D), np.float32)
        pos_own = np.zeros((NTOK,), np.float32)
        for j, b in enumerate(blocks):
            x_own[j * 128:(j + 1) * 128] = xp[b * 128:(b + 1) * 128]
            pos_own[j * 128:(j + 1) * 128] = np.arange(b * 128, (b + 1) * 128)
        for b in range(2):
            x_own[(16 + b) * 128:(16 + b) * 128 + 32] = xs[2 * c + b]
            pos_own[(16 + b) * 128:(16 + b) * 128 + 32] = PAST + np.arange(32)
        mask = np.zeros((128, 8, 128), np.float32)
        for r in range(8):
            if r < c:
                mask[:, r, :] = 1.0
            elif r == c:
                mask[:, r, :] = diag
        m = dict(shared)
        m.update({
            "x_own": x_own, "cs_own": _rope_tables(pos_own), "maskd": mask.reshape(128, 1024),
            "c_ckv": f(cache_mla_ckv)[0, 2 * c:2 * c + 2], "c_kr": f(cache_mla_krope)[0, 2 * c:2 * c + 2],
            "c_dk": f(cache_diff_k)[0, 2 * c:2 * c + 2].reshape(2, PAST, 512),
            "c_dv": f(cache_diff_v)[0, 2 * c:2 * c + 2].reshape(2, PAST, 512),
            "c_mk": f(cache_mem_k)[0, 2 * c:2 * c + 2].reshape(2, 256, 512),
            "c_mv": f(cache_mem_v)[0, 2 * c:2 * c + 2].reshape(2, 256, 512),
        })
        in_maps.append({k: np.ascontiguousarray(v) for k, v in m.items()})
    res = run_bass_kernel_spmd(nc, in_maps, core_ids=list(range(8)))
    R = res.results
    y_p = np.zeros((1, T, D), np.float32)
    y_s = np.zeros((16, 32, D), np.float32)
    s_ckv = np.zeros((1, 16, 32, 256), np.float32)
    s_kr = np.zeros((1, 16, 32, 32), np.float32)
    s_dk = np.zeros((1, 16, 32, 8, 64), np.float32)
    s_dv = np.zeros((1, 16, 32, 8, 64), np.float32)
    for c in range(8):
        yo = R[c]["y_own"]
        for j in range(16):
            b = 8 * j + c
            y_p[0, b * 128:(b + 1) * 128] = yo[j * 128:(j + 1) * 128]
        for b in range(2):
            y_s[2 * c + b] = yo[(16 + b) * 128:(16 + b) * 128 + 32]
            s_ckv[0, 2 * c + b] = R[c]["o_s_ckv"][b * 128:b * 128 + 32]
            s_kr[0, 2 * c + b] = R[c]["o_s_kr"][b * 128:b * 128 + 32]
            s_dk[0, 2 * c + b] = R[c]["o_s_dk"][b * 128:b * 128 + 32].reshape(32, 8, 64)
            s_dv[0, 2 * c + b] = R[c]["o_s_dv"][b * 128:b * 128 + 32].reshape(32, 8, 64)
    p_ckv = R[0]["o_ckv"].reshape(1, 1, T, 256)
    p_kr = R[0]["o_kr"].reshape(1, 1, T, 32)
    p_dk = R[0]["o_dk"].reshape(1, 1, T, 8, 64)
    p_dv = R[0]["o_dv"].reshape(1, 1, T, 8, 64)
    p_mk = R[0]["o_mk"].reshape(1, 1, 256, 4, 128)
    p_mv = R[0]["o_mv"].reshape(1, 1, 256, 4, 128)
    return (y_p, y_s, p_ckv, p_kr, p_dk, p_dv, p_mk, p_mv, s_ckv, s_kr, s_dk, s_dv)
```

```python
import contextlib
import math
import numpy as np
import concourse.bass as bass
import concourse.mybir as mybir
from concourse.bass_utils import run_bass_kernel_spmd

F32 = mybir.dt.float32
BF16 = mybir.dt.bfloat16
ALU = mybir.AluOpType
AF = mybir.ActivationFunctionType
AX = mybir.AxisListType

ENGS = ('pe', 'act', 'dve', 'pool', 'sp')
SEM_CHUNK = 20000

D = 1024
T = 16384
NT = 128
NOWN = 18
NTOK = NOWN * 128
PAST = 4096
LS = 4224
EPS = 1e-6
MLA_SCALE = 96 ** -0.5
DIFF_SCALE = 32 ** -0.5
MEM_SCALE = 128 ** -0.5
LAM_INIT = 0.8 - 0.6 * math.exp(0.0)
VW = 80


class Op:
    __slots__ = ('eng', 'fn', 'deps', 'idx', 'signal', 'tok', 'dma_key', 'is_dma', 'is_bar')


class Sched:
    def __init__(self):
        self.ops = {e: [] for e in ENGS}
        self.lastw = {}
        self.readers = {}
        self.dma_counts = {}
        self.live_dma = {}

    def add(self, eng, fn, reads=(), writes=(), dma_key=None, extra_deps=()):
        op = Op()
        op.eng = eng
        op.fn = fn
        op.signal = False
        op.tok = None
        op.dma_key = dma_key
        op.is_dma = dma_key is not None
        op.is_bar = False
        deps = []
        seen = set()

        def push(d):
            if d is not None and id(d) not in seen:
                seen.add(id(d))
                deps.append(d)
        for d in extra_deps:
            push(d)
        for r in reads:
            push(self.lastw.get(r))
        for w_ in writes:
            push(self.lastw.get(w_))
            for rd in self.readers.get(w_, ()):
                push(rd)
        op.deps = deps
        for r in reads:
            self.readers.setdefault(r, []).append(op)
        for w_ in writes:
            self.lastw[w_] = op
            self.readers[w_] = []
        op.idx = len(self.ops[eng])
        self.ops[eng].append(op)
        if op.is_dma:
            c = self.dma_counts.get(dma_key, 0) + 1
            self.dma_counts[dma_key] = c
            op.tok = (('dma', dma_key), 16 * c)
            self.live_dma[dma_key] = op
        return op

    def barrier(self):
        last = []
        for e in ENGS:
            for op in reversed(self.ops[e]):
                if not op.is_dma and not op.is_bar:
                    last.append(op)
                    break
        dmas = list(self.live_dma.values())
        self.live_dma = {}
        self.lastw = {}
        self.readers = {}
        for e in ENGS:
            self.add(e, lambda eng: None, extra_deps=last + dmas).is_bar = True

    def emit(self, nc, es):
        for e in ENGS:
            for op in self.ops[e]:
                for d in op.deps:
                    if d.is_dma:
                        continue
                    if d.eng == op.eng and not op.is_dma and e == 'pe':
                        continue
                    d.signal = True
        nsem = {}
        for e in ENGS:
            c = 0
            for op in self.ops[e]:
                if op.is_dma:
                    continue
                if op.signal:
                    op.tok = (('eng', e, c // SEM_CHUNK), c % SEM_CHUNK + 1)
                    c += 1
            nsem[e] = (c + SEM_CHUNK - 1) // SEM_CHUNK
        sems = {}
        for e in ENGS:
            for k in range(nsem[e]):
                sems[('eng', e, k)] = es.enter_context(nc.semaphore(f"s_{e}_{k}"))
        for i, key in enumerate(self.dma_counts):
            sems[('dma', key)] = es.enter_context(nc.semaphore(f"d_{i}"))
        self.n_sems = len(sems)
        block = es.enter_context(nc.Block())

        def run(e, eng):
            waited = {}
            for op in self.ops[e]:
                need = {}
                for d in op.deps:
                    if not d.is_dma and d.eng == e and not op.is_dma and e == 'pe':
                        continue
                    sk, v = d.tok
                    if waited.get(sk, 0) >= v:
                        continue
                    if need.get(sk, 0) < v:
                        need[sk] = v
                for sk, v in need.items():
                    eng.wait_ge(sems[sk], v)
                    waited[sk] = v
                    if sk[0] == 'eng':
                        for k in range(sk[2]):
                            waited[('eng', sk[1], k)] = SEM_CHUNK
                ins = op.fn(eng)
                if op.is_dma:
                    ins.then_inc(sems[op.tok[0]], 16)
                elif op.signal:
                    ins.then_inc(sems[op.tok[0]], 1)
            if e == 'sp':
                for key, c in self.dma_counts.items():
                    sk = ('dma', key)
                    if waited.get(sk, 0) < 16 * c:
                        eng.wait_ge(sems[sk], 16 * c)

        @block.tensor
        def _(eng):
            run('pe', eng)

        @block.scalar
        def _(eng):
            run('act', eng)

        @block.vector
        def _(eng):
            run('dve', eng)

        @block.gpsimd
        def _(eng):
            run('pool', eng)

        @block.sync
        def _(eng):
            run('sp', eng)


def build_program():
    nc = bass.Bass("TRN2", target_bir_lowering=False)
    S = Sched()

    def din(name, shape):
        return nc.dram_tensor(name, list(shape), F32, kind="ExternalInput").ap()

    def dout(name, shape):
        return nc.dram_tensor(name, list(shape), F32, kind="ExternalOutput").ap()

    def dscr(name, shape, dt=BF16):
        return nc.dram_tensor(name, list(shape), dt).ap()

    x_all = din("x_all", [T, D])
    x_own = din("x_own", [NTOK, D])
    cs_all = din("cs_all", [T, 40])
    cs_own = din("cs_own", [NTOK, 40])
    maskd = din("maskd", [128, 8 * 128])
    c_ckv = din("c_ckv", [2, PAST, 256])
    c_kr = din("c_kr", [2, PAST, 32])
    c_dk = din("c_dk", [2, PAST, 512])
    c_dv = din("c_dv", [2, PAST, 512])
    c_mk = din("c_mk", [2, 256, 512])
    c_mv = din("c_mv", [2, 256, 512])
    mem_p = din("mem_p", [256, D])
    w_in = din("w_in", [D, 2720])
    w_uq = din("w_uq", [384, 768])
    w_uk = din("w_uk", [256, 512])
    w_uv = din("w_uv", [256, 512])
    w_mk = din("w_mk", [D, 512])
    w_mv = din("w_mv", [D, 512])
    w_oa = din("w_oa", [512, D])
    w_ob = din("w_ob", [512, D])
    w_oc = din("w_oc", [512, D])
    w_gate = din("w_gate", [D, 3072])
    b_gate = din("b_gate", [1, 3072])
    w_out = din("w_out", [D, D])
    w_up = din("w_up", [D, 4096])
    w_dn = din("w_dn", [4096, D])
    g_pre = din("g_pre", [1, D])
    g_q = din("g_q", [1, 384])
    g_kv = din("g_kv", [1, 256])
    g_sub = din("g_sub", [1, 64])
    g_mem = din("g_mem", [1, D])
    g_pm = din("g_pm", [1, D])
    g_mlp = din("g_mlp", [1, D])
    g_post = din("g_post", [1, D])
    lam_in = din("lam_in", [1, 128])

    y_own = dout("y_own", [NTOK, D])
    o_ckv = dout("o_ckv", [T, 256])
    o_kr = dout("o_kr", [T, 32])
    o_dk = dout("o_dk", [T, 512])
    o_dv = dout("o_dv", [T, 512])
    o_mk = dout("o_mk", [256, 512])
    o_mv = dout("o_mv", [256, 512])
    o_s_ckv = dout("o_s_ckv", [256, 256])
    o_s_kr = dout("o_s_kr", [256, 32])
    o_s_dk = dout("o_s_dk", [256, 512])
    o_s_dv = dout("o_s_dv", [256, 512])

    Ls = [T, LS, LS]
    KTm = [dscr(f"KTm{i}", [8, 64, L]) for i, L in enumerate(Ls)]
    KRT = [dscr(f"KRT{i}", [32, L]) for i, L in enumerate(Ls)]
    KTd = [dscr(f"KTd{i}", [8, 64, L]) for i, L in enumerate(Ls)]
    Vm = [dscr(f"Vm{i}", [8, 128, L // 128, VW]) for i, L in enumerate(Ls)]
    Vd = [dscr(f"Vd{i}", [8, 128, L // 128, VW]) for i, L in enumerate(Ls)]
    QTm = dscr("QTm", [8, 96, NTOK])
    QTd = dscr("QTd", [8, 64, NTOK])
    QTc = dscr("QTc", [4, 128, NTOK])
    X1s = dscr("X1s", [NTOK, D], F32)

    es = contextlib.ExitStack()
    with es:
        def sb(stack, name, shape, dt):
            return stack.enter_context(nc.sbuf_tensor(name, list(shape), dt))

        def A(eng, fn, r=(), w=(), key=None):
            return S.add(eng, fn, reads=r, writes=w, dma_key=key)

        def dma(q, out, in_, r=(), w=(), key=None, slow=False):
            if slow:
                return A(q, lambda e: e.dma_start(out=out, in_=in_, allow_slow_non_contiguous=True), r, w, key)
            return A(q, lambda e: e.dma_start(out=out, in_=in_), r, w, key)

        def act(out, in_, func, r, w, **kw):
            return A('act', lambda e: e.activation(out=out, in_=in_, func=func, **kw), r, w)

        def tt(eng, out, in0, in1, op, r, w):
            return A(eng, lambda e: e.tensor_tensor(out=out, in0=in0, in1=in1, op=op), r, w)

        def ts(eng, out, in0, s1, s2, op0, op1, r, w):
            if op1 is None:
                return A(eng, lambda e: e.tensor_scalar(out=out, in0=in0, scalar1=s1, scalar2=None, op0=op0), r, w)
            return A(eng, lambda e: e.tensor_scalar(out=out, in0=in0, scalar1=s1, scalar2=s2, op0=op0, op1=op1), r, w)

        def stt(eng, out, in0, scalar, in1, op0, op1, r, w):
            return A(eng, lambda e: e.scalar_tensor_tensor(out=out, in0=in0, scalar=scalar, in1=in1,
                                                           op0=op0, op1=op1), r, w)

        def cp(eng, out, in_, r, w):
            if eng == 'act':
                return A('act', lambda e: e.activation(out=out, in_=in_, func=AF.Copy), r, w)
            return A(eng, lambda e: e.tensor_copy(out=out, in_=in_), r, w)

        def mms(lst, r, w):
            def fn(e):
                ins = None
                for (o, l, rh, st, sp) in lst:
                    ins = e.matmul(o, lhsT=l, rhs=rh, start=st, stop=sp)
                return ins
            return A('pe', fn, r, w)

        def trs(lst, r, w):
            def fn(e):
                ins = None
                for (o, i, idn) in lst:
                    ins = e.transpose(out=o, in_=i, identity=idn)
                return ins
            return A('pe', fn, list(r) + ['ident'], w)

        def memset(eng, ap, val, w):
            return A(eng, lambda e: e.memset(ap, val), (), w)

        def treduce(out, in_, r, w):
            return A('dve', lambda e: e.tensor_reduce(out=out, in_=in_, axis=AX.X, op=ALU.add), r, w)

        def recip(out, in_, r, w):
            return A('dve', lambda e: e.reciprocal(out=out, in_=in_), r, w)

        PS = [es.enter_context(nc.psum_tensor(f"ps{i}", [128, 512], F32)) for i in range(8)]
        PSb = [p[:].bitcast(BF16).rearrange("p (a b) -> p a b", b=128) for p in PS]

        def pk(i):
            return ('ps', i)

        ident = sb(es, "ident", [128, 128], BF16)
        identf = sb(es, "identf", [128, 128], F32)
        eps_t = sb(es, "eps_t", [128, 1], F32)
        gc_pre = sb(es, "gc_pre", [128, 8], F32)
        gc_q = sb(es, "gc_q", [128, 3], F32)
        gc_mem = sb(es, "gc_mem", [128, 8], F32)
        gc_mlp = sb(es, "gc_mlp", [128, 8], F32)
        lam_t = sb(es, "lam_t", [128, 128], F32)
        lam_s = sb(es, "lam_s", [128, 8], F32)
        junk = sb(es, "junk", [128, 1024], F32)

        memset('pool', identf[:], 0.0, ['identf'])
        A('pool', lambda e: e.affine_select(out=identf[:], in_=identf[:], pattern=[[-1, 128]], compare_op=ALU.not_equal,
                                            fill=1.0, base=0, channel_multiplier=1), ['identf'], ['identf'])
        cp('dve', ident[:], identf[:], ['identf'], ['ident'])
        memset('dve', eps_t[:], EPS, ['eps'])
        for nm, gt, gd, n in (("pre", gc_pre, g_pre, 8), ("q", gc_q, g_q, 3), ("mem", gc_mem, g_mem, 8),
                              ("mlp", gc_mlp, g_mlp, 8)):
            dma('sp', gt[:], gd.rearrange("o (k p) -> p (o k)", p=128), (), [('gc', nm)], key=('gc', nm), slow=True)
        dma('sp', lam_t[:], lam_in[0:1, :].partition_broadcast(128), (), ['lam_t'], key='lam_t')
        tt('dve', lam_t[:, 0:32], lam_t[:, 0:32], lam_t[:, 32:64], ALU.mult, ['lam_t'], ['lam_a'])
        tt('dve', lam_t[:, 64:96], lam_t[:, 64:96], lam_t[:, 96:128], ALU.mult, ['lam_t'], ['lam_b'])
        A('dve', lambda e: e.tensor_reduce(out=lam_s[:, 0:1], in_=lam_t[:, 0:32], axis=AX.X, op=ALU.add),
          ['lam_a'], ['lam0'])
        A('dve', lambda e: e.tensor_reduce(out=lam_s[:, 1:2], in_=lam_t[:, 64:96], axis=AX.X, op=ALU.add),
          ['lam_b'], ['lam1'])
        act(lam_s[:, 2:4], lam_s[:, 0:2], AF.Exp, ['lam0', 'lam1'], ['lam2'])
        stt('dve', lam_s[:, 4:5], lam_s[:, 3:4], -LAM_INIT, lam_s[:, 2:3], ALU.add, ALU.subtract, ['lam2'], ['neglam'])
        neglam = lam_s[:, 4:5]

        wl_cnt = [0]

        def load_w(stack_stage, dst, src, nk, gcol=None, gres=None, res=None):
            cols = src.shape[1]
            for kc in range(nk):
                rows = src[kc * 128:(kc + 1) * 128, :]
                if gcol is None:
                    dma('pool', dst[:, kc, :], rows, (), [(res, kc)], key=('wl', res))
                else:
                    i = wl_cnt[0] % len(stack_stage)
                    wl_cnt[0] += 1
                    stg_ = stack_stage[i]
                    dma('sp', stg_[:, 0:cols], rows, (), [('stg', i)], key=('stg', i))
                    if wl_cnt[0] % 2:
                        act(dst[:, kc, :], stg_[:, 0:cols], AF.Copy, [('stg', i), gres], [(res, kc)],
                            scale=gcol[:, kc:kc + 1])
                    else:
                        ts('dve', dst[:, kc, :], stg_[:, 0:cols], gcol[:, kc:kc + 1], None, ALU.mult, None,
                           [('stg', i), gres], [(res, kc)])

        def wres(res, nk):
            return [(res, kc) for kc in range(nk)]

        def rstd_from(ssq_ap, tmp_ap, out_ap, scale, r, wtmp, wout):
            act(tmp_ap, ssq_ap, AF.Sqrt, list(r) + ['eps'], [wtmp], scale=scale, bias=eps_t[:])
            recip(out_ap, tmp_ap, [wtmp], [wout])

        def x_front(B, s, xsrc, psb=0):
            dma('sp', B['x32'][s][:], xsrc, (), [('x32', s)], key=('x32', s))
            dma('pool', B['xb'][s][:], xsrc, (), [('xb', s)], key=('xb', s))
            st_ = B['st'][s]
            act(junk[:], B['x32'][s][:], AF.Square, [('x32', s)], ['junk', ('st', s, 0)], accum_out=st_[:, 0:1])
            rstd_from(st_[:, 0:1], st_[:, 1:2], st_[:, 2:3], 1.0 / D, [('st', s, 0)], ('st', s, 1), ('st', s, 2))
            trs([(PSb[psb][:, kc, :], B['xb'][s][:, kc * 128:(kc + 1) * 128], ident[:]) for kc in range(8)],
                [('xb', s)], [pk(psb)])
            cp('dve', B['xT'][s][:], PSb[psb][:], [pk(psb)], [('xT', s)])

        def rope(view, x1s, x2s, cosb, sinb, tmp, rr, ww, shp):
            t1, t2, t3, t4 = (tmp[:, i, :].rearrange("p (g h) -> p g h", h=shp[1])[:, 0:shp[0], :] for i in range(4))
            x1 = view[:, :, x1s]
            x2 = view[:, :, x2s]
            tt('dve', t1, x1, cosb, ALU.mult, rr, ['rt1'])
            tt('dve', t2, x2, sinb, ALU.mult, rr, ['rt2'])
            tt('dve', t3, x2, cosb, ALU.mult, rr, ['rt3'])
            tt('dve', t4, x1, sinb, ALU.mult, rr, ['rt4'])
            tt('dve', x1, t1, t2, ALU.subtract, ['rt1', 'rt2', 'rt4'] + list(rr), ww)
            tt('dve', x2, t3, t4, ALU.add, ['rt3', 'rt4'] + list(ww), ww)

        with contextlib.ExitStack() as ph:
            wkv = sb(ph, "wkv", [128, 8, 1312], BF16)
            wuk = sb(ph, "wuk", [128, 2, 512], BF16)
            wuv = sb(ph, "wuv", [128, 2, 512], BF16)
            gkv_bc = sb(ph, "gkv_bc", [128, 256], F32)
            B = {
                'xT': [sb(ph, f"axT_{i}", [128, 8, 128], BF16) for i in range(3)],
                'st': [sb(ph, f"ast_{i}", [128, 8], F32) for i in range(3)],
            }
            X4 = [sb(ph, f"ax32_{i}", [128, D], F32) for i in range(4)]
            XB4 = [sb(ph, f"axb_{i}", [128, D], BF16) for i in range(4)]
            R = [sb(ph, f"R_{i}", [128, 1312], F32) for i in range(3)]
            CS = [sb(ph, f"CS_{i}", [128, 40], F32) for i in range(4)]
            Rb = [sb(ph, f"Rb_{i}", [128, 1312], BF16) for i in range(2)]
            TK = [sb(ph, f"TK_{i}", [128, 7, 128], BF16) for i in range(2)]
            KNs = [sb(ph, f"KNs_{i}", [128, 4, 512], BF16) for i in range(2)]
            DKs = [sb(ph, f"DKs_{i}", [128, 4, 512], BF16) for i in range(2)]
            KRs = [sb(ph, f"KRs_{i}", [32, 512], BF16) for i in range(2)]
            VMs = [sb(ph, f"VMs_{i}", [128, 8, 4, VW], BF16) for i in range(2)]
            VDs = [sb(ph, f"VDs_{i}", [128, 8, 4, VW], BF16) for i in range(2)]
            rtmp = sb(ph, "rtmp", [128, 4, 128], F32)
            stg = [sb(ph, f"astg{i}", [128, 1024], F32) for i in range(4)]

            load_w(stg, wkv[:, :, 0:288], w_in[:, 384:672], 8, gc_pre, ('gc', 'pre'), 'wkva')
            load_w(stg, wkv[:, :, 288:1312], w_in[:, 1184:2208], 8, gc_pre, ('gc', 'pre'), 'wkvb')
            load_w(stg, wuk, w_uk, 2, None, None, 'wuk')
            load_w(stg, wuv, w_uv, 2, None, None, 'wuv')
            WKV = wres('wkva', 8) + wres('wkvb', 8)
            dma('sp', gkv_bc[:], g_kv[0:1, :].partition_broadcast(128), (), ['gkv_bc'], key='gkv_bc')
            for i in range(2):
                memset('pool', VMs[i][:], 0.0, [('VMs', i)])
                memset('pool', VMs[i][:, :, :, 64:65], 1.0, [('VMs', i)])
                memset('pool', VDs[i][:], 0.0, [('VDs', i)])
                memset('pool', VDs[i][:, :, :, 64:65], 1.0, [('VDs', i)])

            NA = NT + 2

            def a_src(k):
                if k < NT:
                    return x_all[k * 128:(k + 1) * 128, :], cs_all[k * 128:(k + 1) * 128, :]
                tq = 16 + (k - NT)
                return x_own[tq * 128:(tq + 1) * 128, :], cs_own[tq * 128:(tq + 1) * 128, :]

            def A_load(k):
                s4 = k % 4
                xsrc, cssrc = a_src(k)
                dma('sp', X4[s4][:], xsrc, (), [('x32', s4)], key=('x32', s4))
                dma('sp', CS[s4][:], cssrc, (), [('CS', s4)], key=('CS', s4))

            def A_F1(k):
                s4 = k % 4
                s = k % 3
                st_ = B['st'][s]
                act(junk[:], X4[s4][:], AF.Square, [('x32', s4)], ['junk', ('st', s, 0)], accum_out=st_[:, 0:1])
                rstd_from(st_[:, 0:1], st_[:, 1:2], st_[:, 2:3], 1.0 / D, [('st', s, 0)], ('st', s, 1), ('st', s, 2))
                cp('dve', XB4[s4][:], X4[s4][:], [('x32', s4)], [('xb', s4)])

            def A_T(k):
                s4 = k % 4
                s = k % 3
                psb = 0 if k % 2 == 0 else 7
                trs([(PSb[psb][:, kc, :], XB4[s4][:, kc * 128:(kc + 1) * 128], ident[:]) for kc in range(8)],
                    [('xb', s4)], [pk(psb)])
                cp('dve', B['xT'][s][:], PSb[psb][:], [pk(psb)], [('xT', s)])

            def A_proj(k):
                s = k % 3
                c4 = k % 4
                xT = B['xT'][s]
                st_ = B['st'][s]
                for bi, (c0, c1) in enumerate(((0, 512), (512, 1024), (1024, 1312))):
                    mms([(PS[1 + bi][:, 0:c1 - c0], xT[:, kc, :], wkv[:, kc, c0:c1], kc == 0, kc == 7)
                         for kc in range(8)], [('xT', s)] + WKV, [pk(1 + bi)])
                    act(R[s][:, c0:c1], PS[1 + bi][:, 0:c1 - c0], AF.Copy, [pk(1 + bi), ('st', s, 2)],
                        [('R', s, bi)], scale=st_[:, 2:3])
                act(junk[:, 0:256], R[s][:, 0:256], AF.Square, [('R', s, 0)], ['junk', ('st', s, 3)],
                    accum_out=st_[:, 3:4])
                rstd_from(st_[:, 3:4], st_[:, 4:5], st_[:, 5:6], 1.0 / 256, [('st', s, 3)], ('st', s, 4), ('st', s, 5))
                stt('dve', R[s][:, 0:256], R[s][:, 0:256], st_[:, 5:6], gkv_bc[:], ALU.mult, ALU.mult,
                    [('R', s, 0), ('st', s, 5), 'gkv_bc'], [('R', s, 'ckv')])
                krv = R[s][:, 256:288].rearrange("p (g d) -> p g d", g=1)
                rope(krv, slice(0, 16), slice(16, 32), CS[c4][:, 0:16].unsqueeze(1), CS[c4][:, 16:32].unsqueeze(1),
                     rtmp, [('R', s, 0), ('CS', c4)], [('R', s, 'kr')], (1, 16))
                dkv = R[s][:, 288:800].rearrange("p (g d) -> p g d", d=32)
                rope(dkv, slice(0, 4), slice(4, 8), CS[c4][:, 32:36].unsqueeze(1).to_broadcast([128, 16, 4]),
                     CS[c4][:, 36:40].unsqueeze(1).to_broadcast([128, 16, 4]),
                     rtmp, [('R', s, 0), ('R', s, 1), ('CS', c4)], [('R', s, 'dk')], (16, 4))
                key = ('R_st', s)
                if k < NT:
                    rs = slice(k * 128, (k + 1) * 128)
                    dsts = (o_ckv[rs, :], o_kr[rs, :], o_dk[rs, :], o_dv[rs, :])
                else:
                    rs = slice((k - NT) * 128, (k - NT + 1) * 128)
                    dsts = (o_s_ckv[rs, :], o_s_kr[rs, :], o_s_dk[rs, :], o_s_dv[rs, :])
                for dst, (c0, c1) in zip(dsts, ((0, 256), (256, 288), (288, 800), (800, 1312))):
                    dma('sp', dst, R[s][:, c0:c1], r_all(s), (), key=key)

            def r_all(s):
                return [('R', s, 0), ('R', s, 1), ('R', s, 2), ('R', s, 'ckv'), ('R', s, 'kr'), ('R', s, 'dk')]

            def kv_S1(s, g, u, s2):
                cp('act', Rb[s2][:, 0:800], R[s][:, 0:800], r_all(s), [('Rb', s2)])
                cp('pool', VDs[g][:, :, u, 0:64], R[s][:, 800:1312].rearrange("p (h e) -> p h e", e=64),
                   r_all(s) + [('VDs', g)], [('VDs', g, u)])
                lst = [(PSb[4][:, i, :], Rb[s2][:, i * 128:(i + 1) * 128], ident[:]) for i in range(2)]
                lst += [(PSb[4][:, 2 + i, :], Rb[s2][:, 288 + i * 128:288 + (i + 1) * 128], ident[:]) for i in range(4)]
                lst += [(PSb[4][0:32, 6, :], Rb[s2][:, 256:288], ident[:])]
                trs(lst, [('Rb', s2)], [pk(4)])
                cp('dve', TK[s2][:, 0:2, :], PSb[4][:, 0:2, :], [pk(4)], [('TK', s2, 0)])
                cp('dve', DKs[g][:, :, u * 128:(u + 1) * 128], PSb[4][:, 2:6, :], [pk(4)], [('DKs', g, u)])
                cp('dve', KRs[g][:, u * 128:(u + 1) * 128], PSb[4][0:32, 6, :], [pk(4)], [('KRs', g, u)])

            def kv_S2(s, g, u, s2):
                lst = []
                for ch in range(4):
                    for kc in range(2):
                        lst.append((PS[5][:, ch * 128:(ch + 1) * 128], wuk[:, kc, ch * 128:(ch + 1) * 128],
                                    TK[s2][:, kc, :], kc == 0, kc == 1))
                mms(lst, [('TK', s2, 0)] + wres('wuk', 2), [pk(5)])
                act(KNs[g][:, :, u * 128:(u + 1) * 128], PS[5][:].rearrange("p (c t) -> p c t", t=128), AF.Copy,
                    [pk(5)], [('KNs', g, u)])
                mms([(PS[6][:], TK[s2][:, kc, :], wuv[:, kc, :], kc == 0, kc == 1) for kc in range(2)],
                    [('TK', s2, 0)] + wres('wuv', 2), [pk(6)])
                act(VMs[g][:, :, u, 0:64], PS[6][:].rearrange("p (h e) -> p h e", e=64), AF.Copy,
                    [pk(6), ('VMs', g)], [('VMs', g, u)])

            def stage_res(g, nu):
                r = []
                for u in range(nu):
                    r += [('DKs', g, u), ('KRs', g, u), ('KNs', g, u), ('VMs', g, u), ('VDs', g, u)]
                return r

            def flush_stage(g, seq, t0, kt0):
                key = ('stg_st', g)
                rr = stage_res(g, 4)
                for hh in range(2):
                    dst = KTm[seq].rearrange("(ch hh) n t -> hh n ch t", hh=2)[hh][:, :, t0:t0 + 512]
                    dma('sp', dst, KNs[g][hh * 64:(hh + 1) * 64, :, :], rr, (), key=key)
                    dst = KTd[seq].rearrange("(ch hh) n t -> hh n ch t", hh=2)[hh][:, :, t0:t0 + 512]
                    dma('sp', dst, DKs[g][hh * 64:(hh + 1) * 64, :, :], rr, (), key=key)
                dma('sp', KRT[seq][:, t0:t0 + 512], KRs[g][:, :], rr, (), key=key)
                dma('sp', Vm[seq][:, :, kt0:kt0 + 4, :].rearrange("h p k e -> p h k e"), VMs[g][:], rr, (), key=key)
                dma('sp', Vd[seq][:, :, kt0:kt0 + 4, :].rearrange("h p k e -> p h k e"), VDs[g][:], rr, (), key=key)

            def st_args(k):
                if k < NT:
                    return (k % 3, (k // 4) % 2, k % 4, k % 2)
                return (k % 3, 0, k - NT, k % 2)
            A_load(0)
            A_load(1)
            A_F1(0)
            for it in range(NA + 2):
                if it + 2 < NA:
                    A_load(it + 2)
                if it + 1 < NA:
                    A_F1(it + 1)
                if it < NA:
                    A_T(it)
                if it >= 2:
                    kv_S1(*st_args(it - 2))
                if 1 <= it < NA + 1:
                    A_proj(it - 1)
                if it >= 2:
                    k = it - 2
                    kv_S2(*st_args(k))
                    if k < NT and k % 4 == 3:
                        flush_stage((k // 4) % 2, 0, (k - 3) * 128, k - 3)
            rr = stage_res(0, 2)
            key = ('stg_st', 0)
            for b in range(2):
                seq = 1 + b
                for hh in range(2):
                    dst = KTm[seq].rearrange("(ch hh) n t -> hh n ch t", hh=2)[hh][:, :, PAST:PAST + 32]
                    dma('sp', dst, KNs[0][hh * 64:(hh + 1) * 64, :, b * 128:b * 128 + 32], rr, (), key=key, slow=True)
                    dst = KTd[seq].rearrange("(ch hh) n t -> hh n ch t", hh=2)[hh][:, :, PAST:PAST + 32]
                    dma('sp', dst, DKs[0][hh * 64:(hh + 1) * 64, :, b * 128:b * 128 + 32], rr, (), key=key, slow=True)
                dma('sp', KRT[seq][:, PAST:PAST + 32], KRs[0][:, b * 128:b * 128 + 32], rr, (), key=key, slow=True)
                dma('sp', Vm[seq][:, 0:32, 32, :].rearrange("h p e -> p h e"), VMs[0][0:32, :, b, :], rr, (), key=key)
                dma('sp', Vd[seq][:, 0:32, 32, :].rearrange("h p e -> p h e"), VDs[0][0:32, :, b, :], rr, (), key=key)
            def C_load(k):
                b, t = divmod(k, 32)
                s = k % 3
                rs = slice(t * 128, (t + 1) * 128)
                dma('sp', R[s][:, 0:256], c_ckv[b, rs, :], (), [('R', s, 'ckv'), ('R', s, 0)], key=('Rl', s, 0))
                dma('sp', R[s][:, 256:288], c_kr[b, rs, :], (), [('R', s, 'kr')], key=('Rl', s, 1))
                dma('sp', R[s][:, 288:800], c_dk[b, rs, :], (), [('R', s, 'dk'), ('R', s, 1)], key=('Rl', s, 2))
                dma('sp', R[s][:, 800:1312], c_dv[b, rs, :], (), [('R', s, 2)], key=('Rl', s, 3))
            def c_args(k):
                return (k % 3, 1 - ((k // 4) % 2), k % 4, k % 2)
            C_load(0)
            C_load(1)
            for it in range(64 + 1):
                if it + 2 < 64:
                    C_load(it + 2)
                if it < 64:
                    kv_S1(*c_args(it))
                if it >= 1:
                    k = it - 1
                    b, t = divmod(k, 32)
                    kv_S2(*c_args(k))
                    if k % 4 == 3:
                        flush_stage(1 - ((k // 4) % 2), 1 + b, (t - 3) * 128, t - 3)
            S.barrier()

        mid = contextlib.ExitStack()
        mid.__enter__()
        KTc = sb(mid, "KTc", [128, 3, 4, 256], BF16)
        Vc = sb(mid, "Vc", [128, 3, 2, 4, 132], BF16)
        Gs = dscr("Gs", [NTOK, 3072], F32)
        memset('pool', Vc[:], 0.0, ['Vc0'])
        memset('pool', Vc[:, :, :, :, 128:129], 1.0, ['Vc0'])
        with contextlib.ExitStack() as ph:
            wq = sb(ph, "wq", [128, 8, 1408], BF16)
            wuq = sb(ph, "wuq", [128, 3, 768], BF16)
            wmk = sb(ph, "wmk", [128, 8, 512], BF16)
            wmv = sb(ph, "wmv", [128, 8, 512], BF16)
            B = {
                'x32': [sb(ph, f"bx32_{i}", [128, D], F32) for i in range(2)],
                'xb': [sb(ph, f"bxb_{i}", [128, D], BF16) for i in range(2)],
                'xT': [sb(ph, f"bxT_{i}", [128, 8, 128], BF16) for i in range(2)],
                'st': [sb(ph, f"bst_{i}", [128, 12], F32) for i in range(2)],
            }
            CS = [sb(ph, f"bCS_{i}", [128, 40], F32) for i in range(2)]
            Q1 = [sb(ph, f"Q1_{i}", [128, 512], F32) for i in range(2)]
            MQb = [sb(ph, f"MQb_{i}", [128, 512], BF16) for i in range(2)]
            DQb = [sb(ph, f"DQb_{i}", [128, 512], BF16) for i in range(2)]
            CQT = [sb(ph, f"CQT_{i}", [128, 3, 128], BF16) for i in range(2)]
            Qf = [sb(ph, f"Qf_{i}", [128, 768], F32) for i in range(2)]
            Qb = [sb(ph, f"Qb_{i}", [128, 768], BF16) for i in range(2)]
            QTa = [sb(ph, f"QTa_{i}", [128, 8, 128], BF16) for i in range(2)]
            QTb = [sb(ph, f"QTb_{i}", [128, 8, 128], BF16) for i in range(2)]
            QTe = [sb(ph, f"QTe_{i}", [128, 4, 128], BF16) for i in range(2)]
            MKf = [sb(ph, f"MKf_{i}", [128, 512], F32) for i in range(2)]
            MKb = [sb(ph, f"MKb_{i}", [128, 512], BF16) for i in range(2)]
            rtmp = sb(ph, "brtmp", [128, 4, 128], F32)
            stg = [sb(ph, f"bstg{i}", [128, 1024], F32) for i in range(4)]
            wg = sb(ph, "wg", [128, 8, 3072], BF16)
            bg_bc = sb(ph, "bg_bc", [128, 3072], F32)
            Gb = sb(ph, "Gb", [128, 3072], F32)
            for i3 in range(3):
                load_w(stg, wg[:, :, i3 * 1024:(i3 + 1) * 1024], w_gate[:, i3 * 1024:(i3 + 1) * 1024], 8, gc_pre,
                       ('gc', 'pre'), f'wg{i3}')
            WG = wres('wg0', 8) + wres('wg1', 8) + wres('wg2', 8)
            dma('sp', bg_bc[:], b_gate[0:1, :].partition_broadcast(128), (), ['bg_bc'], key='bg_bc')

            load_w(stg, wq[:, :, 0:512], w_in[:, 672:1184], 8, gc_pre, ('gc', 'pre'), 'wqa')
            load_w(stg, wq[:, :, 512:1024], w_in[:, 2208:2720], 8, gc_pre, ('gc', 'pre'), 'wqb')
            load_w(stg, wq[:, :, 1024:1408], w_in[:, 0:384], 8, gc_pre, ('gc', 'pre'), 'wqc')
            load_w(stg, wuq, w_uq, 3, gc_q, ('gc', 'q'), 'wuq')
            load_w(stg, wmk, w_mk, 8, gc_mem, ('gc', 'mem'), 'wmk')
            load_w(stg, wmv, w_mv, 8, gc_mem, ('gc', 'mem'), 'wmv')
            WQ = wres('wqa', 8) + wres('wqb', 8) + wres('wqc', 8)

            for mt in range(2):
                s = mt
                x_front(B, s, mem_p[mt * 128:(mt + 1) * 128, :], psb=0)
                st_ = B['st'][s]
                for wi, (wt, wn, od) in enumerate(((wmk, 'wmk', o_mk), (wmv, 'wmv', o_mv))):
                    mms([(PS[1 + wi][:], B['xT'][s][:, kc, :], wt[:, kc, :], kc == 0, kc == 7) for kc in range(8)],
                        [('xT', s)] + wres(wn, 8), [pk(1 + wi)])
                    i2 = (2 * mt + wi) % 2
                    act(MKf[i2][:], PS[1 + wi][:], AF.Copy, [pk(1 + wi), ('st', s, 2)], [('MKf', i2)],
                        scale=st_[:, 2:3])
                    dma('sp', od[mt * 128:(mt + 1) * 128, :], MKf[i2][:], [('MKf', i2)], (), key=('MKf_st', i2))
                    if wi == 0:
                        cp('pool', MKb[i2][:], MKf[i2][:], [('MKf', i2)], [('MKb', i2)])
                        trs([(PSb[3][:, h, :], MKb[i2][:, h * 128:(h + 1) * 128], ident[:]) for h in range(4)],
                            [('MKb', i2)], [pk(3)])
                        cp('dve', KTc[:, 0, :, mt * 128:(mt + 1) * 128], PSb[3][:, 0:4, :], [pk(3)],
                           [('KTc', 0, mt)])
                    else:
                        cp('pool', Vc[:, 0, mt, :, 0:128], MKf[i2][:].rearrange("p (h e) -> p h e", e=128),
                           [('MKf', i2), 'Vc0'], [('Vc', 0, mt)])
            for b in range(2):
                for mt in range(2):
                    i2 = mt
                    dma('sp', MKf[i2][:], c_mk[b, mt * 128:(mt + 1) * 128, :], (), [('MKf', i2)], key=('MKf', i2))
                    cp('pool', MKb[i2][:], MKf[i2][:], [('MKf', i2)], [('MKb', i2)])
                    trs([(PSb[3][:, h, :], MKb[i2][:, h * 128:(h + 1) * 128], ident[:]) for h in range(4)],
                        [('MKb', i2)], [pk(3)])
                    cp('dve', KTc[:, 1 + b, :, mt * 128:(mt + 1) * 128], PSb[3][:, 0:4, :], [pk(3)],
                       [('KTc', 1 + b, mt)])
                    dma('sp', MKf[i2][:], c_mv[b, mt * 128:(mt + 1) * 128, :], (), [('MKf', i2)], key=('MKf', i2))
                    cp('pool', Vc[:, 1 + b, mt, :, 0:128], MKf[i2][:].rearrange("p (h e) -> p h e", e=128),
                       [('MKf', i2), 'Vc0'], [('Vc', 1 + b, mt)])

            def B_F(tq):
                s = tq % 2
                x_front(B, s, x_own[tq * 128:(tq + 1) * 128, :], psb=0)
                dma('sp', CS[s][:], cs_own[tq * 128:(tq + 1) * 128, :], (), [('CS', s)], key=('CS', s))

            def B_S1(tq):
                s = tq % 2
                xT = B['xT'][s]
                st_ = B['st'][s]
                for gi in range(3):
                    for hf in range(2):
                        c0 = gi * 1024 + hf * 512
                        bk = 1 + hf
                        mms([(PS[bk][:], xT[:, kc, :], wg[:, kc, c0:c0 + 512], kc == 0, kc == 7) for kc in range(8)],
                            [('xT', s)] + WG, [pk(bk)])
                        stt('dve', Gb[:, c0:c0 + 512], PS[bk][:], st_[:, 2:3], bg_bc[:, c0:c0 + 512], ALU.mult, ALU.add,
                            [pk(bk), ('st', s, 2), 'bg_bc'], [('Gb', gi, hf)])
                        act(Gb[:, c0:c0 + 512], Gb[:, c0:c0 + 512], AF.Sigmoid, [('Gb', gi, hf)], [('Gb', gi, hf)])
                dma('sp', Gs[tq * 128:(tq + 1) * 128, :], Gb[:], [('Gb', gi, hf) for gi in range(3) for hf in range(2)],
                    (), key='Gb_st')
                for bi, (c0, c1) in enumerate(((0, 512), (512, 1024), (1024, 1408))):
                    mms([(PS[3 + bi][:, 0:c1 - c0], xT[:, kc, :], wq[:, kc, c0:c1], kc == 0, kc == 7)
                         for kc in range(8)], [('xT', s)] + WQ, [pk(3 + bi)])
                act(Q1[s][:], PS[3][:], AF.Copy, [pk(3), ('st', s, 2)], [('Q1', s)], scale=st_[:, 2:3])
                act(MQb[s][:], PS[4][:], AF.Copy, [pk(4), ('st', s, 2)], [('MQb', s)], scale=st_[:, 2:3])
                act(junk[:, 0:384], PS[5][:, 0:384], AF.Square, [pk(5), ('st', s, 2)], ['junk', ('st', s, 3)],
                    scale=st_[:, 2:3], accum_out=st_[:, 3:4])
                rstd_from(st_[:, 3:4], st_[:, 4:5], st_[:, 5:6], 1.0 / 384, [('st', s, 3)], ('st', s, 4), ('st', s, 5))
                tt('dve', st_[:, 6:7], st_[:, 5:6], st_[:, 2:3], ALU.mult, [('st', s, 5), ('st', s, 2)], [('st', s, 6)])
                lst = []
                for ch in range(3):
                    for kc in range(8):
                        lst.append((PS[6][:, ch * 128:(ch + 1) * 128], wq[:, kc, 1024 + ch * 128:1024 + (ch + 1) * 128],
                                    xT[:, kc, :], kc == 0, kc == 7))
                mms(lst, [('xT', s)] + WQ, [pk(6)])
                cp('dve', CQT[s][:], PS[6][:, 0:384].rearrange("p (c t) -> p c t", t=128), [pk(6)], [('CQT', s)])
                for bi, (c0, c1) in enumerate(((0, 512), (512, 768))):
                    mms([(PS[1 + bi][:, 0:c1 - c0], CQT[s][:, rc, :], wuq[:, rc, c0:c1], rc == 0, rc == 2)
                         for rc in range(3)], [('CQT', s)] + wres('wuq', 3), [pk(1 + bi)])
                    act(Qf[s][:, c0:c1], PS[1 + bi][:, 0:c1 - c0], AF.Copy, [pk(1 + bi), ('st', s, 6)],
                        [('Qf', s, bi)], scale=st_[:, 6:7])
                qv = Qf[s][:].rearrange("p (h d) -> p h d", d=96)
                rope(qv, slice(64, 80), slice(80, 96), CS[s][:, 0:16].unsqueeze(1).to_broadcast([128, 8, 16]),
                     CS[s][:, 16:32].unsqueeze(1).to_broadcast([128, 8, 16]), rtmp,
                     [('Qf', s, 0), ('Qf', s, 1), ('CS', s)], [('Qf', s, 'r')], (8, 16))
                cp('dve', Qb[s][:], Qf[s][:], [('Qf', s, 0), ('Qf', s, 1), ('Qf', s, 'r')], [('Qb', s)])
                dqv = Q1[s][:].rearrange("p (g d) -> p g d", d=32)
                rope(dqv, slice(0, 4), slice(4, 8), CS[s][:, 32:36].unsqueeze(1).to_broadcast([128, 16, 4]),
                     CS[s][:, 36:40].unsqueeze(1).to_broadcast([128, 16, 4]), rtmp,
                     [('Q1', s), ('CS', s)], [('Q1', s, 'r')], (16, 4))
                cp('act', DQb[s][:], Q1[s][:], [('Q1', s), ('Q1', s, 'r')], [('DQb', s)])

            def B_S2(tq):
                s = tq % 2
                trs([(PSb[7][0:96, h, :], Qb[s][:, h * 96:(h + 1) * 96], ident[:]) for h in range(8)],
                    [('Qb', s)], [pk(7)])
                cp('dve', QTa[s][0:96, :, :], PSb[7][0:96, :, :], [pk(7)], [('QTa', s)])
                dma('sp', QTm[:, :, tq * 128:(tq + 1) * 128].rearrange("h r t -> r h t"), QTa[s][0:96, :, :],
                    [('QTa', s)], (), key=('QTa_st', s), slow=True)
                trs([(PSb[7][0:64, h, :], DQb[s][:, h * 64:(h + 1) * 64], ident[:]) for h in range(8)],
                    [('DQb', s)], [pk(7)])
                cp('dve', QTb[s][0:64, :, :], PSb[7][0:64, :, :], [pk(7)], [('QTb', s)])
                dma('sp', QTd[:, :, tq * 128:(tq + 1) * 128].rearrange("h r t -> r h t"), QTb[s][0:64, :, :],
                    [('QTb', s)], (), key=('QTb_st', s), slow=True)
                trs([(PSb[7][:, h, :], MQb[s][:, h * 128:(h + 1) * 128], ident[:]) for h in range(4)],
                    [('MQb', s)], [pk(7)])
                cp('dve', QTe[s][:], PSb[7][:, 0:4, :], [pk(7)], [('QTe', s)])
                dma('sp', QTc[:, :, tq * 128:(tq + 1) * 128].rearrange("h r t -> r h t"), QTe[s][:],
                    [('QTe', s)], (), key=('QTe_st', s), slow=True)

            B_F(0)
            for it in range(NOWN + 1):
                if it + 1 < NOWN:
                    B_F(it + 1)
                if it < NOWN:
                    B_S1(it)
                if it >= 1:
                    B_S2(it - 1)
            S.barrier()

        OA = sb(mid, "OA", [128, NOWN, 512], BF16)
        OB = sb(mid, "OB", [128, NOWN, 512], BF16)
        OC = sb(mid, "OC", [128, NOWN, 512], BF16)
        for nm, o in (('OA', OA), ('OB', OB), ('OC', OC)):
            memset('pool', o[:, 16:18, :], 0.0, [(nm, 'z')])
        with contextlib.ExitStack() as ph:
            KT = [sb(ph, f"KT_{i}", [128, T], BF16) for i in range(2)]
            VV = [sb(ph, f"VV_{i}", [128, 128, VW], BF16) for i in range(2)]
            QT = [sb(ph, f"QT_{i}", [128, NTOK], BF16) for i in range(2)]
            PT = [sb(ph, f"PT_{i}", [128, 4, 128], BF16) for i in range(6)]
            MK = sb(ph, "MK", [128, 8, 128], BF16)
            mstg = sb(ph, "mstg", [128, 1024], F32)
            ot = [sb(ph, f"ot_{i}", [128, 8], F32) for i in range(2)]
            ot1 = [sb(ph, f"ot1_{i}", [128, 64], F32) for i in range(2)]
            dma('sp', mstg[:], maskd[:, :], (), ['mstg'], key='mstg')
            cp('dve', MK[:].rearrange("p a b -> p (a b)"), mstg[:], ['mstg'], ['MK'])

            pend = []
            gcnt = [0]
            dcnt = [0]
            ucnt = [0]

            def push_task(sfn, pvfn, postfn, la=4):
                sfn()
                pend.append((pvfn, postfn))
                while len(pend) > la:
                    pv, post = pend.pop(0)
                    pv()
                    if post is not None:
                        post()

            def drain():
                while pend:
                    pv, post = pend.pop(0)
                    pv()
                    if post is not None:
                        post()

            def attn_unit(kt_ap, q_ap, v_ap, nkt, nk_last, nq, scale, masked, obank, res_in, postfn, pair=False):
                ngr = (nkt + 3) // 4
                for gi in range(ngr):
                    k0 = gi * 4
                    kn = min(4, nkt - k0)
                    if pair:
                        slot = 2 * (dcnt[0] % 3)
                        dcnt[0] += 1
                    else:
                        slot = gcnt[0] % 6
                        gcnt[0] += 1
                    nks = [nk_last if (k0 + i == nkt - 1) else 128 for i in range(kn)]

                    def sfn(k0=k0, kn=kn, slot=slot, nks=nks, gi=gi):
                        mms([(PS[slot][0:nks[i], i * 128:i * 128 + nq], kt_ap(k0 + i, nks[i]), q_ap, True, True)
                             for i in range(kn)], res_in, [pk(slot)])
                        if all(n == 128 for n in nks):
                            act(PT[slot][:, 0:kn, 0:nq],
                                PS[slot][:, 0:kn * 128].rearrange("p (a b) -> p a b", b=128)[:, :, 0:nq],
                                AF.Exp, [pk(slot)], [('PT', slot)], scale=scale)
                        else:
                            for i in range(kn):
                                act(PT[slot][0:nks[i], i, 0:nq], PS[slot][0:nks[i], i * 128:i * 128 + nq], AF.Exp,
                                    [pk(slot)], [('PT', slot)], scale=scale)
                        if masked and gi >= ngr - 2:
                            r0 = (gi - (ngr - 2)) * 4
                            tt('dve', PT[slot][:, :, :], PT[slot][:, :, :], MK[:, r0:r0 + 4, :], ALU.mult,
                               [('PT', slot), 'MK'], [('PT', slot)])

                    def pvfn(k0=k0, kn=kn, slot=slot, nks=nks):
                        lst = []
                        for i in range(kn):
                            kt = k0 + i
                            va = v_ap(kt, nks[i])
                            lst.append((PS[obank][0:nq, 0:va.shape[1]], PT[slot][0:nks[i], i, 0:nq], va, kt == 0,
                                        kt == nkt - 1))
                        mms(lst, [('PT', slot)] + list(res_in), [pk(obank)])
                    push_task(sfn, pvfn, postfn if gi == ngr - 1 else None, la=2 if pair else 4)

            SREG = {1: (0, 0), 2: (8192, 64)}
            heads = [('m', h) for h in range(8)] + [('d', h) for h in range(8)]

            def load_prompt(i):
                kind, h = heads[i]
                slot = i % 2
                wk1 = [('KT', slot, 'p'), ('KT', slot, 1), ('KT', slot, 2)]
                wk2 = [('KT', slot, 'p2'), ('KT', slot, 1, 2), ('KT', slot, 2, 2)]
                wv = [('VV', slot, 'p'), ('VV', slot, 1), ('VV', slot, 2)]
                if kind == 'm':
                    dma('sp', KT[slot][0:64, 0:T], KTm[0][h], (), wk1, key=('KT', slot, 0))
                    dma('sp', KT[slot][64:96, 0:T], KRT[0], (), wk2, key=('KT', slot, 1))
                    dma('sp', VV[slot][:, 0:128, :], Vm[0][h], (), wv, key=('VV', slot))
                else:
                    dma('sp', KT[slot][0:64, 0:T], KTd[0][h], (), wk1 + wk2, key=('KT', slot, 0))
                    dma('sp', VV[slot][:, 0:128, :], Vd[0][h], (), wv, key=('VV', slot))

            def load_sample(i):
                kind, h = heads[i]
                slot = (i + 1) % 2
                for seq in (1, 2):
                    c0, t0 = SREG[seq]
                    if kind == 'm':
                        dma('sp', KT[slot][0:64, c0:c0 + LS], KTm[seq][h], (), [('KT', slot, 'p'), ('KT', slot, seq)],
                            key=('KTs', slot, seq, 0))
                        dma('sp', KT[slot][64:96, c0:c0 + LS], KRT[seq], (), [('KT', slot, 'p2'), ('KT', slot, seq, 2)],
                            key=('KTs', slot, seq, 1))
                        dma('sp', VV[slot][:, t0:t0 + 33, :], Vm[seq][h], (), [('VV', slot, 'p'), ('VV', slot, seq)],
                            key=('VVs', slot, seq))
                    else:
                        dma('sp', KT[slot][0:64, c0:c0 + LS], KTd[seq][h], (),
                            [('KT', slot, 'p'), ('KT', slot, 'p2'), ('KT', slot, seq), ('KT', slot, seq, 2)],
                            key=('KTs', slot, seq, 0))
                        dma('sp', VV[slot][:, t0:t0 + 33, :], Vd[seq][h], (), [('VV', slot, 'p'), ('VV', slot, seq)],
                            key=('VVs', slot, seq))

            def mla_post(ob, nq, tq, h):
                def post():
                    os_ = ucnt[0] % 2
                    ucnt[0] += 1
                    recip(ot[os_][0:nq, 0:1], PS[ob][0:nq, 64:65], [pk(ob)], [('ot', os_)])
                    ts('dve', OA[0:nq, tq, h * 64:(h + 1) * 64], PS[ob][0:nq, 0:64], ot[os_][0:nq, 0:1], None,
                       ALU.mult, None, [pk(ob), ('ot', os_), ('OA', 'z')], [('OA', tq, h)])
                return post

            def diff_post(ob, nq, tq, h):
                def post():
                    os_ = ucnt[0] % 2
                    ucnt[0] += 1
                    recip(ot[os_][0:nq, 0:1], PS[6][0:nq, 64:65], [pk(6)], [('ot', os_, 0)])
                    recip(ot[os_][0:nq, 1:2], PS[7][0:nq, 64:65], [pk(7)], [('ot', os_, 1)])
                    tt('dve', ot[os_][0:nq, 2:3], ot[os_][0:nq, 1:2], neglam[0:nq, :], ALU.mult,
                       [('ot', os_, 1), 'neglam'], [('ot', os_, 2)])
                    ts('dve', ot1[os_][0:nq, :], PS[6][0:nq, 0:64], ot[os_][0:nq, 0:1], None, ALU.mult, None,
                       [pk(6), ('ot', os_, 0)], [('ot1', os_)])
                    stt('dve', OB[0:nq, tq, h * 64:(h + 1) * 64], PS[7][0:nq, 0:64], ot[os_][0:nq, 2:3],
                        ot1[os_][0:nq, :], ALU.mult, ALU.add, [pk(7), ('ot', os_, 2), ('ot1', os_), ('OB', 'z')],
                        [('OB', tq, h)])
                return post

            def mem_post(ob, nq, tq, h):
                def post():
                    os_ = ucnt[0] % 2
                    ucnt[0] += 1
                    recip(ot[os_][0:nq, 0:1], PS[ob][0:nq, 128:129], [pk(ob)], [('ot', os_)])
                    ts('dve', OC[0:nq, tq, h * 128:(h + 1) * 128], PS[ob][0:nq, 0:128], ot[os_][0:nq, 0:1], None,
                       ALU.mult, None, [pk(ob), ('ot', os_), ('OC', 'z')], [('OC', tq, h)])
                return post

            ocnt = [0]

            def run_units(kind, h, hb, units, slot, sample):
                QTh = QT[hb]
                for (seq, tq, nq, nkt, nkl, masked) in units:
                    ob = 6 + (ocnt[0] % 2)
                    ocnt[0] += 1
                    if sample:
                        c0, t0 = SREG[seq]
                        res_in = [('KT', slot, seq), ('KT', slot, seq, 2), ('VV', slot, seq), ('QT', hb)]
                    else:
                        c0, t0 = 0, 0
                        res_in = [('KT', slot, 'p'), ('KT', slot, 'p2'), ('VV', slot, 'p'), ('QT', hb)]
                    if kind == 'm':
                        attn_unit(lambda kt, nk, slot=slot, c0=c0: KT[slot][0:96, c0 + kt * 128:c0 + kt * 128 + nk],
                                  QTh[0:96, tq * 128:tq * 128 + nq],
                                  lambda kt, nk, slot=slot, t0=t0: VV[slot][0:nk, t0 + kt, 0:65],
                                  nkt, nkl, nq, MLA_SCALE, masked, ob, res_in, mla_post(ob, nq, tq, h))
                    else:
                        attn_unit_d(slot, c0, t0, QTh, tq, nkt, nkl, nq, masked, ob, res_in,
                                    diff_post(ob, nq, tq, h))

            def attn_unit_d(slot_kv, c0, t0, QTh, tq, nkt, nk_last, nq, masked, obank, res_in, postfn):
                ngr = (nkt + 3) // 4
                for gi in range(ngr):
                    k0 = gi * 4
                    kn = min(4, nkt - k0)
                    pr = dcnt[0] % 3
                    dcnt[0] += 1
                    nks = [nk_last if (k0 + i == nkt - 1) else 128 for i in range(kn)]

                    def sfn(k0=k0, kn=kn, pr=pr, nks=nks, gi=gi):
                        lst = []
                        for i in range(kn):
                            for c in range(2):
                                kc0 = c0 + (k0 + i) * 128
                                lst.append((PS[2 * pr + c][0:nks[i], i * 128:i * 128 + nq],
                                            KT[slot_kv][32 * c:32 * c + 32, kc0:kc0 + nks[i]],
                                            QTh[32 * c:32 * c + 32, tq * 128:tq * 128 + nq], True, True))
                        mms(lst, res_in, [pk(2 * pr), pk(2 * pr + 1)])
                        for c in range(2):
                            sl = 2 * pr + c
                            if all(n == 128 for n in nks):
                                act(PT[sl][:, 0:kn, 0:nq],
                                    PS[sl][:, 0:kn * 128].rearrange("p (a b) -> p a b", b=128)[:, :, 0:nq],
                                    AF.Exp, [pk(sl)], [('PT', sl)], scale=DIFF_SCALE)
                            else:
                                for i in range(kn):
                                    act(PT[sl][0:nks[i], i, 0:nq], PS[sl][0:nks[i], i * 128:i * 128 + nq], AF.Exp,
                                        [pk(sl)], [('PT', sl)], scale=DIFF_SCALE)
                            if masked and gi >= ngr - 2:
                                r0 = (gi - (ngr - 2)) * 4
                                tt('dve', PT[sl][:, :, :], PT[sl][:, :, :], MK[:, r0:r0 + 4, :], ALU.mult,
                                   [('PT', sl), 'MK'], [('PT', sl)])

                    def pvfn(k0=k0, kn=kn, pr=pr, nks=nks):
                        lst = []
                        for c in range(2):
                            for i in range(kn):
                                kt = k0 + i
                                lst.append((PS[6 + c][0:nq, 0:65], PT[2 * pr + c][0:nks[i], i, 0:nq],
                                            VV[slot_kv][0:nks[i], t0 + kt, 0:65], kt == 0, kt == nkt - 1))
                        mms(lst, [('PT', 2 * pr), ('PT', 2 * pr + 1)] + list(res_in), [pk(6), pk(7)])
                    push_task(sfn, pvfn, postfn if gi == ngr - 1 else None, la=2)

            prompt_units = [(0, j, 128, 8 * j + 8, 128, True) for j in range(16)]
            sample_units = [(1 + b, 16 + b, 32, 33, 32, False) for b in range(2)]

            for i, (kind, h) in enumerate(heads):
                hb = i % 2
                src = QTm if kind == 'm' else QTd
                nr = 96 if kind == 'm' else 64
                dma('sp', QT[hb][0:nr, :], src[h], (), [('QT', hb)], key=('QT', hb))
                load_sample(i)
                run_units(kind, h, hb, sample_units, (i + 1) % 2, True)
            drain()
            load_prompt(0)
            for i, (kind, h) in enumerate(heads):
                hb = i % 2
                src = QTm if kind == 'm' else QTd
                nr = 96 if kind == 'm' else 64
                dma('sp', QT[hb][0:nr, :], src[h], (), [('QT', hb)], key=('QT', hb))
                if i + 1 < len(heads):
                    drain()
                    load_prompt(i + 1)
                run_units(kind, h, hb, prompt_units, i % 2, False)

            for h in range(4):
                hb = h % 2
                QTh = QT[hb]
                dma('sp', QTh[:, :], QTc[h], (), [('QT', hb)], key=('QT', hb))
                for (seq, tq, nq, nkt, nkl, masked) in prompt_units + sample_units:
                    ob = 6 + (ocnt[0] % 2)
                    ocnt[0] += 1
                    res_in = [('KTc', seq, 0), ('KTc', seq, 1), ('Vc', seq, 0), ('Vc', seq, 1), ('QT', hb)]
                    attn_unit(lambda kt, nk, seq=seq, h=h: KTc[:, seq, h, kt * 128:kt * 128 + nk],
                              QTh[:, tq * 128:tq * 128 + nq],
                              lambda kt, nk, seq=seq, h=h: Vc[0:nk, seq, kt, h, 0:129],
                              2, 128, nq, MEM_SCALE, False, ob, res_in, mem_post(ob, nq, tq, h), pair=True)
            drain()
            S.barrier()

        with contextlib.ExitStack() as ph:
            wo = [sb(ph, f"wo_{i}", [128, 4, D], BF16) for i in range(3)]
            wout = sb(ph, "wout", [128, 8, D], BF16)
            gsub_bc = sb(ph, "gsub_bc", [128, 512], F32)
            gpm_bc = sb(ph, "gpm_bc", [128, D], F32)
            B = {
                'x32': [sb(ph, f"dx32_{i}", [128, D], F32) for i in range(2)],
                'st': [sb(ph, f"dst_{i}", [128, 32], F32) for i in range(2)],
            }
            Gt = [sb(ph, f"Gt_{i}", [128, 3072], F32) for i in range(2)]
            OBn2 = [sb(ph, f"OBn_{i}", [128, 512], BF16) for i in range(2)]
            obf2 = [sb(ph, f"obf_{i}", [128, 512], F32) for i in range(2)]
            OT2 = [sb(ph, f"OT_{i}", [128, 12, 128], BF16) for i in range(2)]
            M = sb(ph, "M", [128, D], F32)
            Mt = sb(ph, "Mt", [128, D], F32)
            Mb = sb(ph, "Mb", [128, D], BF16)
            MT = sb(ph, "MT", [128, 8, 128], BF16)
            X1 = [sb(ph, f"X1_{i}", [128, D], F32) for i in range(2)]
            stg = [sb(ph, f"dstg{i}", [128, 1024], F32) for i in range(4)]

            for i, wsrc in enumerate((w_oa, w_ob, w_oc)):
                load_w(stg, wo[i], wsrc, 4, None, None, f'wo{i}')
            load_w(stg, wout, w_out, 8, None, None, 'wout')
            dma('sp', gpm_bc[:], g_pm[0:1, :].partition_broadcast(128), (), ['gpm_bc'], key='gpm_bc')
            for hh in range(8):
                dma('sp', gsub_bc[:, hh * 64:(hh + 1) * 64], g_sub[0:1, :].partition_broadcast(128), (),
                    [('gsub_bc', hh)], key='gsub_bc')
            ts('dve', gsub_bc[:], gsub_bc[:], 1.0 - LAM_INIT, None, ALU.mult, None,
               [('gsub_bc', hh) for hh in range(8)], ['gsub'])

            def D_X(tq):
                s = tq % 2
                dma('sp', B['x32'][s][:], x_own[tq * 128:(tq + 1) * 128, :], (), [('x32', s)], key=('x32', s))
                dma('sp', Gt[s][:], Gs[tq * 128:(tq + 1) * 128, :], (), [('G', s)], key=('G', s))
                st_ = B['st'][s]
                obf_ = obf2[s]
                OBn_ = OBn2[s]
                tt('dve', obf_[:], OB[:, tq, :], OB[:, tq, :], ALU.mult, [], [('obf', s)])
                treduce(st_[:, 8:16], obf_[:].rearrange("p (h e) -> p h e", e=64), [('obf', s)], [('st', s, 8)])
                act(st_[:, 16:24], st_[:, 8:16], AF.Sqrt, [('st', s, 8), 'eps'], [('st', s, 16)], scale=1.0 / 64,
                    bias=eps_t[:])
                recip(st_[:, 24:32], st_[:, 16:24], [('st', s, 16)], [('st', s, 24)])
                tt('dve', obf_[:].rearrange("p (h e) -> p h e", e=64), OB[:, tq, :].rearrange("p (h e) -> p h e", e=64),
                   st_[:, 24:32].unsqueeze(2).to_broadcast([128, 8, 64]), ALU.mult, [('st', s, 24), ('obf', s)],
                   [('obf', s)])
                tt('dve', OBn_[:], obf_[:], gsub_bc[:], ALU.mult, [('obf', s), 'gsub'], [('OBn', s)])
                lst = [(PSb[3][:, i, :], OA[:, tq, i * 128:(i + 1) * 128], ident[:]) for i in range(4)]
                lst += [(PSb[3][:, 4 + i, :], OBn_[:, i * 128:(i + 1) * 128], ident[:]) for i in range(4)]
                trs(lst, [('OBn', s)], [pk(3)])
                trs([(PSb[4][:, i, :], OC[:, tq, i * 128:(i + 1) * 128], ident[:]) for i in range(4)], [], [pk(4)])
                cp('dve', OT2[s][:, 0:8, :], PSb[3][:], [pk(3)], [('OT', s, 0)])
                cp('act', OT2[s][:, 8:12, :], PSb[4][:, 0:4, :], [pk(4)], [('OT', s, 1)])

            def D_Y(tq):
                s = tq % 2
                G = Gt[s]
                st_ = B['st'][s]
                OT = OT2[s]
                for br in range(3):
                    for hf in range(2):
                        bk = 5 + hf
                        mms([(PS[bk][:], OT[:, 4 * br + kc, :], wo[br][:, kc, hf * 512:(hf + 1) * 512], kc == 0, kc == 3)
                             for kc in range(4)], [('OT', s, 0), ('OT', s, 1)] + wres(f'wo{br}', 4), [pk(bk)])
                        gs = G[:, br * 1024 + hf * 512:br * 1024 + (hf + 1) * 512]
                        ms = M[:, hf * 512:(hf + 1) * 512]
                        if br == 0:
                            tt('dve', ms, PS[bk][:], gs, ALU.mult, [pk(bk), ('G', s)], [('M', hf)])
                        else:
                            mt_ = Mt[:, hf * 512:(hf + 1) * 512]
                            tt('dve', mt_, PS[bk][:], gs, ALU.mult, [pk(bk), ('G', s)], [('Mt', hf)])
                            tt('dve', ms, ms, mt_, ALU.add, [('M', hf), ('Mt', hf)], [('M', hf)])
                cp('act', Mb[:], M[:], [('M', 0), ('M', 1)], ['Mb'])
                trs([(PSb[0][:, kc, :], Mb[:, kc * 128:(kc + 1) * 128], ident[:]) for kc in range(8)], ['Mb'], [pk(0)])
                cp('dve', MT[:], PSb[0][:], [pk(0)], ['MT'])
                for hf in range(2):
                    bk = 1 + hf
                    mms([(PS[bk][:], MT[:, kc, :], wout[:, kc, hf * 512:(hf + 1) * 512], kc == 0, kc == 7)
                         for kc in range(8)], ['MT'] + wres('wout', 8), [pk(bk)])
                    act(junk[:, hf * 512:(hf + 1) * 512], PS[bk][:], AF.Square, [pk(bk)], ['junk', ('st', s, 3 + hf)],
                        accum_out=st_[:, 3 + hf:4 + hf])
                tt('dve', st_[:, 5:6], st_[:, 3:4], st_[:, 4:5], ALU.add, [('st', s, 3), ('st', s, 4)], [('st', s, 5)])
                rstd_from(st_[:, 5:6], st_[:, 6:7], st_[:, 7:8], 1.0 / D, [('st', s, 5)], ('st', s, 6), ('st', s, 7))
                for hf in range(2):
                    bk = 1 + hf
                    cs_ = slice(hf * 512, (hf + 1) * 512)
                    stt('dve', Mt[:, cs_], PS[bk][:], st_[:, 7:8], gpm_bc[:, cs_], ALU.mult, ALU.mult,
                        [pk(bk), ('st', s, 7), 'gpm_bc'], [('Mt', hf)])
                    tt('dve', X1[s][:, cs_], Mt[:, cs_], B['x32'][s][:, cs_], ALU.add, [('Mt', hf), ('x32', s)],
                       [('X1', s, hf)])
                dma('sp', X1s[tq * 128:(tq + 1) * 128, :], X1[s][:], [('X1', s, 0), ('X1', s, 1)], (), key=('X1_st', s))

            D_X(0)
            for tq in range(NOWN):
                if tq + 1 < NOWN:
                    D_X(tq + 1)
                D_Y(tq)
            S.barrier()

        mid.__exit__(None, None, None)
        with contextlib.ExitStack() as ph:
            wup = sb(ph, "wup", [128, 8, 4096], BF16)
            wdn = sb(ph, "wdn", [128, 32, D], BF16)
            gpost_bc = sb(ph, "gpost_bc", [128, D], F32)
            X1 = [sb(ph, f"eX1_{i}", [128, D], F32) for i in range(2)]
            X1b = [sb(ph, f"eX1b_{i}", [128, D], BF16) for i in range(2)]
            hT = sb(ph, "hT", [128, 8, 256], BF16)
            U2T = sb(ph, "U2T", [128, 32, 256], BF16)
            ur = [sb(ph, f"ur_{i}", [128, 256], F32) for i in range(2)]
            Y = [sb(ph, f"Y_{i}", [128, D], F32) for i in range(2)]
            Yt = sb(ph, "Yt", [128, D], F32)
            st = [sb(ph, f"est_{i}", [128, 16], F32) for i in range(2)]
            stg = [sb(ph, f"estg{i}", [128, 1024], F32) for i in range(4)]
            for i4 in range(4):
                load_w(stg, wup[:, :, i4 * 1024:(i4 + 1) * 1024], w_up[:, i4 * 1024:(i4 + 1) * 1024], 8, gc_mlp,
                       ('gc', 'mlp'), f'wup{i4}')
            load_w(stg, wdn, w_dn, 32, None, None, 'wdn')
            WUP = wres('wup0', 8) + wres('wup1', 8) + wres('wup2', 8) + wres('wup3', 8)
            dma('sp', gpost_bc[:], g_post[0:1, :].partition_broadcast(128), (), ['gpost_bc'], key='gpost_bc')
            urc = [0]
            for sp_ in range(NOWN // 2):
                for i in range(2):
                    tq = sp_ * 2 + i
                    dma('sp', X1[i][:], X1s[tq * 128:(tq + 1) * 128, :], (), [('X1', i)], key=('X1', i))
                    dma('pool', X1b[i][:], X1s[tq * 128:(tq + 1) * 128, :], (), [('X1b', i)], key=('X1b', i))
                    act(junk[:], X1[i][:], AF.Square, [('X1', i)], ['junk', ('st', i, 0)], accum_out=st[i][:, 0:1])
                    rstd_from(st[i][:, 0:1], st[i][:, 1:2], st[i][:, 2:3], 1.0 / D, [('st', i, 0)], ('st', i, 1),
                              ('st', i, 2))
                    trs([(PSb[0][:, kc, :], X1b[i][:, kc * 128:(kc + 1) * 128], ident[:]) for kc in range(8)],
                        [('X1b', i)], [pk(0)])
                    cp('dve', hT[:, :, i * 128:(i + 1) * 128], PSb[0][:], [pk(0)], [('hT', i)])
                for fc in range(32):
                    bk = 1 + fc % 3
                    mms([(PS[bk][:, 0:256], wup[:, kc, fc * 128:(fc + 1) * 128], hT[:, kc, :], kc == 0, kc == 7)
                         for kc in range(8)], [('hT', 0), ('hT', 1)] + WUP, [pk(bk)])
                    ui = urc[0] % 2
                    urc[0] += 1
                    act(ur[ui][:], PS[bk][:, 0:256], AF.Relu, [pk(bk)], [('ur', ui)])
                    tt('pool' if fc % 2 else 'dve', U2T[:, fc, :], ur[ui][:], ur[ui][:], ALU.mult, [('ur', ui)],
                       [('U2T', fc)])
                for i in range(2):
                    tq = sp_ * 2 + i
                    for hf in range(2):
                        bk = 4 + 2 * i + hf
                        mms([(PS[bk][:], U2T[:, fc, i * 128:(i + 1) * 128], wdn[:, fc, hf * 512:(hf + 1) * 512],
                              fc == 0, fc == 31) for fc in range(32)],
                            [('U2T', fc) for fc in range(32)] + wres('wdn', 32), [pk(bk)])
                        act(junk[:, hf * 512:(hf + 1) * 512], PS[bk][:], AF.Square, [pk(bk)],
                            ['junk', ('st', i, 3 + hf)], accum_out=st[i][:, 3 + hf:4 + hf])
                    s_ = st[i]
                    tt('dve', s_[:, 5:6], s_[:, 3:4], s_[:, 4:5], ALU.add, [('st', i, 3), ('st', i, 4)], [('st', i, 5)])
                    tt('dve', s_[:, 6:7], s_[:, 2:3], s_[:, 2:3], ALU.mult, [('st', i, 2)], [('st', i, 6)])
                    tt('dve', s_[:, 7:8], s_[:, 6:7], s_[:, 6:7], ALU.mult, [('st', i, 6)], [('st', i, 7)])
                    tt('dve', s_[:, 8:9], s_[:, 7:8], s_[:, 5:6], ALU.mult, [('st', i, 7), ('st', i, 5)], [('st', i, 8)])
                    rstd_from(s_[:, 8:9], s_[:, 9:10], s_[:, 10:11], 1.0 / D, [('st', i, 8)], ('st', i, 9), ('st', i, 10))
                    tt('dve', s_[:, 11:12], s_[:, 10:11], s_[:, 6:7], ALU.mult, [('st', i, 10), ('st', i, 6)],
                       [('st', i, 11)])
                    for hf in range(2):
                        bk = 4 + 2 * i + hf
                        cs_ = slice(hf * 512, (hf + 1) * 512)
                        stt('dve', Yt[:, cs_], PS[bk][:], s_[:, 11:12], gpost_bc[:, cs_], ALU.mult, ALU.mult,
                            [pk(bk), ('st', i, 11), 'gpost_bc'], [('Yt', hf)])
                        tt('pool', Y[i][:, cs_], Yt[:, cs_], X1[i][:, cs_], ALU.add, [('Yt', hf), ('X1', i)],
                           [('Y', i, hf)])
                    dma('sp', y_own[tq * 128:(tq + 1) * 128, :], Y[i][:], [('Y', i, 0), ('Y', i, 1)], (),
                        key=('Y_st', i))
        S.emit(nc, es)
        build_program.n_sems = S.n_sems
    return nc


_NC_CACHE = {}


def _rope_tables(pos):
    pos = np.asarray(pos, dtype=np.float32)
    out = np.zeros((pos.shape[0], 40), dtype=np.float32)
    invm = np.power(np.float32(10000.0), -np.arange(16, dtype=np.float32) * np.float32(2.0 / 32)).astype(np.float32)
    invd = np.power(np.float32(500000.0), -np.arange(4, dtype=np.float32) * np.float32(2.0 / 8)).astype(np.float32)
    am = pos[:, None] * invm[None, :]
    ad = pos[:, None] * invd[None, :]
    out[:, 0:16] = np.cos(am)
    out[:, 16:32] = np.sin(am)
    out[:, 32:36] = np.cos(ad)
    out[:, 36:40] = np.sin(ad)
    return out


def kernel(x_prompt, x_sample, cache_mla_ckv, cache_mla_krope, cache_diff_k, cache_diff_v,
           cache_mem_k, cache_mem_v, mem_prompt, pre_mix_g, w_in, mla_q_norm_g, mla_w_uq,
           mla_kv_norm_g, mla_w_uk, mla_w_uv, diff_lq1, diff_lk1, diff_lq2, diff_lk2,
           diff_subln_g, mem_norm_g, w_mem_k, w_mem_v, w_o_mla, w_o_diff, w_o_mem, w_gate,
           b_gate, w_out, post_mix_g, pre_mlp_g, w_mlp_up, w_mlp_down, post_mlp_g):
    f = lambda a: np.ascontiguousarray(np.asarray(a, dtype=np.float32))
    if 'nc' not in _NC_CACHE:
        _NC_CACHE['nc'] = build_program()
    nc = _NC_CACHE['nc']
    xp = f(x_prompt)[0]
    xs = f(x_sample)
    cs_all = _rope_tables(np.arange(T))
    shared = {
        "x_all": xp, "cs_all": cs_all, "mem_p": f(mem_prompt)[0],
        "w_in": f(w_in)[0], "w_uq": f(mla_w_uq)[0], "w_uk": f(mla_w_uk)[0].reshape(256, 512),
        "w_uv": f(mla_w_uv)[0].reshape(256, 512), "w_mk": f(w_mem_k)[0], "w_mv": f(w_mem_v)[0],
        "w_oa": f(w_o_mla)[0], "w_ob": f(w_o_diff)[0], "w_oc": f(w_o_mem)[0], "w_gate": f(w_gate)[0],
        "b_gate": f(b_gate), "w_out": f(w_out)[0], "w_up": f(w_mlp_up)[0], "w_dn": f(w_mlp_down)[0],
        "g_pre": f(pre_mix_g), "g_q": f(mla_q_norm_g), "g_kv": f(mla_kv_norm_g), "g_sub": f(diff_subln_g),
        "g_mem": f(mem_norm_g), "g_pm": f(post_mix_g), "g_mlp": f(pre_mlp_g), "g_post": f(post_mlp_g),
        "lam_in": np.concatenate([f(diff_lq1), f(diff_lk1), f(diff_lq2), f(diff_lk2)], axis=1),
    }
    in_maps = []
    kk = np.arange(128)[:, None] // 64
    qq = np.arange(128)[None, :] // 64
    diag = (kk <= qq).astype(np.float32)
    for c in range(8):
        blocks = [8 * j + c for j in range(16)]
        x_own = np.zeros((NTOK, D), np.float32)
        pos_own = np.zeros((NTOK,), np.float32)
        for j, b in enumerate(blocks):
            x_own[j * 128:(j + 1) * 128] = xp[b * 128:(b + 1) * 128]
            pos_own[j * 128:(j + 1) * 128] = np.arange(b * 128, (b + 1) * 128)
        for b in range(2):
            x_own[(16 + b) * 128:(16 + b) * 128 + 32] = xs[2 * c + b]
            pos_own[(16 + b) * 128:(16 + b) * 128 + 32] = PAST + np.arange(32)
        mask = np.zeros((128, 8, 128), np.float32)
        for r in range(8):
            if r < c:
                mask[:, r, :] = 1.0
            elif r == c:
                mask[:, r, :] = diag
        m = dict(shared)
        m.update({
            "x_own": x_own, "cs_own": _rope_tables(pos_own), "maskd": mask.reshape(128, 1024),
            "c_ckv": f(cache_mla_ckv)[0, 2 * c:2 * c + 2], "c_kr": f(cache_mla_krope)[0, 2 * c:2 * c + 2],
            "c_dk": f(cache_diff_k)[0, 2 * c:2 * c + 2].reshape(2, PAST, 512),
            "c_dv": f(cache_diff_v)[0, 2 * c:2 * c + 2].reshape(2, PAST, 512),
            "c_mk": f(cache_mem_k)[0, 2 * c:2 * c + 2].reshape(2, 256, 512),
            "c_mv": f(cache_mem_v)[0, 2 * c:2 * c + 2].reshape(2, 256, 512),
        })
        in_maps.append({k: np.ascontiguousarray(v) for k, v in m.items()})
    res = run_bass_kernel_spmd(nc, in_maps, core_ids=list(range(8)))
    R = res.results
    y_p = np.zeros((1, T, D), np.float32)
    y_s = np.zeros((16, 32, D), np.float32)
    s_ckv = np.zeros((1, 16, 32, 256), np.float32)
    s_kr = np.zeros((1, 16, 32, 32), np.float32)
    s_dk = np.zeros((1, 16, 32, 8, 64), np.float32)
    s_dv = np.zeros((1, 16, 32, 8, 64), np.float32)
    for c in range(8):
        yo = R[c]["y_own"]
        for j in range(16):
            b = 8 * j + c
            y_p[0, b * 128:(b + 1) * 128] = yo[j * 128:(j + 1) * 128]
        for b in range(2):
            y_s[2 * c + b] = yo[(16 + b) * 128:(16 + b) * 128 + 32]
            s_ckv[0, 2 * c + b] = R[c]["o_s_ckv"][b * 128:b * 128 + 32]
            s_kr[0, 2 * c + b] = R[c]["o_s_kr"][b * 128:b * 128 + 32]
            s_dk[0, 2 * c + b] = R[c]["o_s_dk"][b * 128:b * 128 + 32].reshape(32, 8, 64)
            s_dv[0, 2 * c + b] = R[c]["o_s_dv"][b * 128:b * 128 + 32].reshape(32, 8, 64)
    p_ckv = R[0]["o_ckv"].reshape(1, 1, T, 256)
    p_kr = R[0]["o_kr"].reshape(1, 1, T, 32)
    p_dk = R[0]["o_dk"].reshape(1, 1, T, 8, 64)
    p_dv = R[0]["o_dv"].reshape(1, 1, T, 8, 64)
    p_mk = R[0]["o_mk"].reshape(1, 1, 256, 4, 128)
    p_mv = R[0]["o_mv"].reshape(1, 1, 256, 4, 128)
    return (y_p, y_s, p_ckv, p_kr, p_dk, p_dv, p_mk, p_mv, s_ckv, s_kr, s_dk, s_dv)
```
